# Optimizing a Trainium2 kernel written in Bass

```python
import jax
import jax.numpy as jnp
from jax import lax
import numpy as np

D_MODEL = 2048
BATCH = 1
SEQ = 16384
DEPTH = 2

NH_M = 4
DQK_M = D_MODEL // 8
DV_M = D_MODEL // 4
NH_R = 8
DQK_R = D_MODEL // 16
DV_R = D_MODEL // 8
WM_QK = NH_M * DQK_M
WM_V = NH_M * DV_M
WR_QK = NH_R * DQK_R
WR_V = NH_R * DV_R
N_MGATE = 4 * NH_M
CONV_W = 5
CHUNK = 128
ROPE_BASE = 10000.0
EPS = 1e-6
IN_SPLITS = (WM_QK, WM_QK, WM_V, WM_V, WM_V, N_MGATE, WR_QK, WR_QK, WR_V, WR_V, D_MODEL, D_MODEL)
D_IN = sum(IN_SPLITS)

kernel_name = 'hybrid_mlstm_retention_encoder'


def rmsnorm(x, w):
    xf = x.astype(jnp.float32)
    y = xf * lax.rsqrt(jnp.mean(xf * xf, axis=-1, keepdims=True) + EPS)
    return (y * w.astype(jnp.float32)).astype(x.dtype)


def split_heads(t, n_heads):
    b, s, _ = t.shape
    return t.reshape(b, s, n_heads, -1).transpose(0, 2, 1, 3)


def head_rmsnorm(t, w):
    b, h, s, d = t.shape
    y = t * lax.rsqrt(jnp.mean(t * t, axis=-1, keepdims=True) + EPS)
    y = y * w.astype(jnp.float32).reshape(h, d)[None, :, None, :]
    return y.transpose(0, 2, 1, 3).reshape(b, s, h * d)


def to_chunks(t):
    b, h, s = t.shape[:3]
    t = t.reshape((b, h, s // CHUNK, CHUNK) + t.shape[3:])
    return jnp.moveaxis(t, 2, 0)


def from_chunks(t):
    nc, b, h, l = t.shape[:4]
    return jnp.moveaxis(t, 0, 2).reshape((b, h, nc * l) + t.shape[4:])


def flip_seq(t):
    return jnp.flip(t, axis=2)


def depthwise_conv(t, w, bias):
    c = t.shape[-1]
    out = lax.conv_general_dilated(t, w[:, None, :], window_strides=(1,), padding='SAME',
                                   dimension_numbers=('NWC', 'WIO', 'NWC'), feature_group_count=c)
    return out + bias


def rope(t, cos, sin):
    half = t.shape[-1] // 2
    t1, t2 = t[..., :half], t[..., half:]
    return jnp.concatenate([t1 * cos - t2 * sin, t1 * sin + t2 * cos], axis=-1)


def mlstm_scan(q, k, v, i_pre, f_pre):
    b, h, _, dk = q.shape
    dv = v.shape[-1]
    tril = jnp.tril(jnp.ones((CHUNK, CHUNK), dtype=bool))
    log_f = jax.nn.log_sigmoid(f_pre)

    def step(carry, inp):
        c_state, n_state, m_state = carry
        qc, kc, vc, ic, lfc = inp
        cum = jnp.cumsum(lfc, axis=-1)
        total = cum[..., -1]
        d_log = cum[..., :, None] - cum[..., None, :] + ic[..., None, :]
        d_log = jnp.where(tril, d_log, -jnp.inf)
        inter_log = cum + m_state[..., None]
        m_row = jnp.maximum(jnp.max(d_log, axis=-1), inter_log)
        w = jnp.exp(d_log - m_row[..., None]) * jnp.einsum('bhid,bhjd->bhij', qc, kc)
        a = jnp.exp(inter_log - m_row)
        num = jnp.einsum('bhij,bhjv->bhiv', w, vc) + a[..., None] * jnp.einsum('bhid,bhdv->bhiv', qc, c_state)
        den = jnp.sum(w, axis=-1) + a * jnp.einsum('bhid,bhd->bhi', qc, n_state)
        out = num / jnp.maximum(jnp.abs(den), jnp.exp(-m_row))[..., None]
        w_in = total[..., None] - cum + ic
        m_new = jnp.maximum(total + m_state, jnp.max(w_in, axis=-1))
        s_in = jnp.exp(w_in - m_new[..., None])
        decay = jnp.exp(total + m_state - m_new)
        c_state = decay[..., None, None] * c_state + jnp.einsum('bhj,bhjd,bhjv->bhdv', s_in, kc, vc)
        n_state = decay[..., None] * n_state + jnp.einsum('bhj,bhjd->bhd', s_in, kc)
        return (c_state, n_state, m_new), out

    init = (jnp.zeros((b, h, dk, dv), jnp.float32), jnp.zeros((b, h, dk), jnp.float32),
            jnp.zeros((b, h), jnp.float32))
    xs = (to_chunks(q), to_chunks(k), to_chunks(v), to_chunks(i_pre), to_chunks(log_f))
    _, out = lax.scan(step, init, xs)
    return from_chunks(out)


def retention_scan(q, k, v, log_gamma, include_diag):
    b, h, _, dk = q.shape
    dv = v.shape[-1]
    idx = jnp.arange(CHUNK, dtype=jnp.float32)
    diff = idx[:, None] - idx[None, :]
    mask = diff >= 0 if include_diag else diff > 0
    intra_decay = jnp.where(mask, jnp.exp(jnp.where(mask, diff, 0.0)[None] * log_gamma[:, None, None]), 0.0)
    q_decay = jnp.exp((idx + 1.0)[None, :] * log_gamma[:, None])
    k_decay = jnp.exp((CHUNK - 1.0 - idx)[None, :] * log_gamma[:, None])
    chunk_decay = jnp.exp(CHUNK * log_gamma)

    def step(s_state, inp):
        qc, kc, vc = inp
        scores = jnp.einsum('bhid,bhjd->bhij', qc, kc) * intra_decay[None]
        out = (jnp.einsum('bhij,bhjv->bhiv', scores, vc)
               + q_decay[None, :, :, None] * jnp.einsum('bhid,bhdv->bhiv', qc, s_state))
        s_state = (chunk_decay[None, :, None, None] * s_state
                   + jnp.einsum('bhjd,bhjv->bhdv', kc * k_decay[None, :, :, None], vc))
        return s_state, out

    init = jnp.zeros((b, h, dk, dv), jnp.float32)
    _, out = lax.scan(step, init, (to_chunks(q), to_chunks(k), to_chunks(v)))
    return from_chunks(out)


def hybrid_layer(x, cos, sin, log_gamma, norm_w, w_in, b_mgate, conv_w, conv_b,
                 m_norm_w, r_norm_w, w_proj_m, w_proj_r, b_mix, w_out):
    b, s, _ = x.shape
    f32 = jnp.float32
    h = rmsnorm(x, norm_w)
    proj = jnp.einsum('bsd,de->bse', h, w_in)
    (q_m, k_m, v_m, z_m, o_m, g_m, q_r, k_r, v_r, z_r, gate_m, gate_r) = jnp.split(
        proj, np.cumsum(IN_SPLITS)[:-1].tolist(), axis=-1)

    qk_m = jax.nn.silu(depthwise_conv(jnp.concatenate([q_m, k_m], axis=-1), conv_w, conv_b))
    q_m = split_heads(qk_m[..., :WM_QK], NH_M).astype(f32) * DQK_M ** -0.5
    k_m = split_heads(qk_m[..., WM_QK:], NH_M).astype(f32)
    v_m = split_heads(v_m, NH_M).astype(f32)
    gates = (g_m + b_mgate).astype(f32).reshape(b, s, 4, NH_M).transpose(2, 0, 3, 1)
    h_m = (mlstm_scan(q_m, k_m, v_m, gates[0], gates[1])
           + flip_seq(mlstm_scan(flip_seq(q_m), flip_seq(k_m), flip_seq(v_m),
                                 flip_seq(gates[2]), flip_seq(gates[3]))))
    h_m = jax.nn.sigmoid(split_heads(o_m, NH_M).astype(f32)) * h_m
    u_m = head_rmsnorm(h_m, m_norm_w).astype(x.dtype) * jax.nn.silu(z_m)

    q_r = rope(q_r.reshape(b, s, NH_R, DQK_R).astype(f32), cos, sin).transpose(0, 2, 1, 3) * DQK_R ** -0.5
    k_r = rope(k_r.reshape(b, s, NH_R, DQK_R).astype(f32), cos, sin).transpose(0, 2, 1, 3)
    v_r = split_heads(v_r, NH_R).astype(f32)
    h_r = (retention_scan(q_r, k_r, v_r, log_gamma, True)
           + flip_seq(retention_scan(flip_seq(q_r), flip_seq(k_r), flip_seq(v_r), log_gamma, False)))
    u_r = head_rmsnorm(h_r, r_norm_w).astype(x.dtype) * jax.nn.silu(z_r)

    mix = jax.nn.sigmoid(jnp.concatenate([gate_m, gate_r], axis=-1) + b_mix)
    y = (mix[..., :D_MODEL] * jnp.einsum('bse,ed->bsd', u_m, w_proj_m)
         + mix[..., D_MODEL:] * jnp.einsum('bse,ed->bsd', u_r, w_proj_r))
    return x + jnp.einsum('bsd,de->bse', y, w_out)


def setup_inputs(seed: int = 0) -> dict:
    key = jax.random.key(seed)
    ks = jax.random.split(key, 16)
    nrm = jax.random.normal
    x = nrm(ks[0], (BATCH, SEQ, D_MODEL), jnp.float32)
    offset = jax.random.randint(ks[1], (BATCH, 1), 0, 4096, dtype=jnp.int32)
    positions = offset + jnp.arange(SEQ, dtype=jnp.int32)[None, :]
    norm_w = 1.0 + 0.02 * nrm(ks[2], (DEPTH, D_MODEL), jnp.float32)
    w_in = nrm(ks[3], (DEPTH, D_MODEL, D_IN), jnp.float32) * D_MODEL ** -0.5
    i_bias = 0.1 * nrm(ks[4], (DEPTH, 2, NH_M), jnp.float32)
    f_bias = jnp.linspace(3.0, 6.0, NH_M, dtype=jnp.float32) + 0.1 * nrm(ks[5], (DEPTH, 2, NH_M), jnp.float32)
    b_mgate = jnp.stack([i_bias[:, 0], f_bias[:, 0], i_bias[:, 1], f_bias[:, 1]], axis=1).reshape(DEPTH, N_MGATE)
    conv_w = nrm(ks[6], (DEPTH, CONV_W, 2 * WM_QK), jnp.float32) * CONV_W ** -0.5
    conv_b = 0.02 * nrm(ks[7], (DEPTH, 2 * WM_QK), jnp.float32)
    m_norm_w = 1.0 + 0.02 * nrm(ks[8], (DEPTH, WM_V), jnp.float32)
    r_norm_w = 1.0 + 0.02 * nrm(ks[9], (DEPTH, WR_V), jnp.float32)
    w_proj_m = nrm(ks[10], (DEPTH, WM_V, D_MODEL), jnp.float32) * WM_V ** -0.5
    w_proj_r = nrm(ks[11], (DEPTH, WR_V, D_MODEL), jnp.float32) * WR_V ** -0.5
    b_mix = 0.02 * nrm(ks[12], (DEPTH, 2 * D_MODEL), jnp.float32)
    w_out = nrm(ks[13], (DEPTH, D_MODEL, D_MODEL), jnp.float32) * D_MODEL ** -0.5
    final_norm_w = 1.0 + 0.02 * nrm(ks[14], (D_MODEL,), jnp.float32)
    return {'x': x, 'positions': positions, 'norm_w': norm_w, 'w_in': w_in, 'b_mgate': b_mgate,
            'conv_w': conv_w, 'conv_b': conv_b, 'm_norm_w': m_norm_w, 'r_norm_w': r_norm_w,
            'w_proj_m': w_proj_m, 'w_proj_r': w_proj_r, 'b_mix': b_mix, 'w_out': w_out,
            'final_norm_w': final_norm_w}


def reference(x, positions, norm_w, w_in, b_mgate, conv_w, conv_b, m_norm_w, r_norm_w,
              w_proj_m, w_proj_r, b_mix, w_out, final_norm_w):
    log_gamma = jnp.log1p(-jnp.power(2.0, -5.0 - jnp.arange(NH_R, dtype=jnp.float32)))
    inv_freq = jnp.power(ROPE_BASE, -jnp.arange(DQK_R // 2, dtype=jnp.float32) / (DQK_R // 2))
    angle = positions.astype(jnp.float32)[..., None] * inv_freq
    cos = jnp.cos(angle)[:, :, None, :]
    sin = jnp.sin(angle)[:, :, None, :]
    for layer in range(DEPTH):
        x = hybrid_layer(x, cos, sin, log_gamma, norm_w[layer], w_in[layer], b_mgate[layer],
                         conv_w[layer], conv_b[layer], m_norm_w[layer], r_norm_w[layer],
                         w_proj_m[layer], w_proj_r[layer], b_mix[layer], w_out[layer])
    return rmsnorm(x, final_norm_w)
```

```python
import contextlib
import numpy as np
import concourse.bass as bass
import concourse.mybir as mybir
from concourse.bass_utils import run_bass_kernel_spmd

F32 = mybir.dt.float32
BF16 = mybir.dt.bfloat16
I32 = mybir.dt.int32
AF = mybir.ActivationFunctionType
ALU = mybir.AluOpType
AX = mybir.AxisListType

NCORES = 8
D = 2048
KT = 16
DIN = 18448
P1_COLS = 6160
EPS = 1e-6
O_QM, O_KM, O_VM, O_ZM, O_OM, O_GM = 0, 1024, 2048, 4096, 6144, 8192
O_QR, O_KR, O_VR, O_ZR, O_GTM, O_GTR = 8208, 9232, 10256, 12304, 14352, 16400
GAM = [1.0 - 2.0 ** (-5 - h) for h in range(8)]
PI = float(np.pi)


class Tr:
    __slots__ = ("name", "lw", "rd", "dsem", "dcnt")

    def __init__(self, name=""):
        self.name = name
        self.lw = None
        self.rd = {}
        self.dsem = None
        self.dcnt = 0


class Sched:
    ENG = ("pe", "act", "dve", "pool", "sp")

    def __init__(self, nc, stack):
        self.nc = nc
        self.stack = stack
        self.sems = {}
        self.final = {}
        self.cnt = {}
        self.waited = {e: {} for e in self.ENG}
        self.prog = {e: [] for e in self.ENG}
        self.nsem = 0
        self.dpool = []
        for e in self.ENG:
            self._mksem("E_" + e)
            self.cnt[e] = 0

    def _mksem(self, key):
        self.sems[key] = self.stack.enter_context(self.nc.semaphore(key))
        self.final[key] = 0
        self.nsem += 1
        return key

    def _deps(self, reads, writes):
        deps = {}
        for r in reads:
            if r.lw is not None and deps.get(r.lw[0], 0) < r.lw[1]:
                deps[r.lw[0]] = r.lw[1]
        for w in writes:
            if w.lw is not None and deps.get(w.lw[0], 0) < w.lw[1]:
                deps[w.lw[0]] = w.lw[1]
            for k, v in w.rd.items():
                if deps.get(k, 0) < v:
                    deps[k] = v
        return deps

    def _emit_waits(self, eng, deps):
        wd = self.waited[eng]
        for k, v in deps.items():
            if wd.get(k, 0) >= v:
                continue
            wd[k] = v
            sem = self.sems[k]
            self.prog[eng].append(lambda e, sem=sem, v=v: e.wait_ge(sem, v))

    def _record(self, tok, reads, writes):
        k, v = tok
        self.final[k] = max(self.final[k], v)
        for r in reads:
            if r.rd.get(k, 0) < v:
                r.rd[k] = v
        for w in writes:
            w.lw = tok
            w.rd = {}

    def op(self, eng, fns, reads=(), writes=()):
        if isinstance(fns, tuple):
            fns = [fns]
        fns = [(lambda e, f=f: getattr(e, f[0])(*f[1], **f[2])) for f in fns]
        self._emit_waits(eng, self._deps(reads, writes))
        self.cnt[eng] += 1
        v = self.cnt[eng]
        sem = self.sems["E_" + eng]
        n = len(fns)
        for i, fn in enumerate(fns):
            if i == n - 1:
                self.prog[eng].append(lambda e, fn=fn, sem=sem: fn(e).then_inc(sem, 1))
            else:
                self.prog[eng].append(lambda e, fn=fn: fn(e))
        self._record(("E_" + eng, v), reads, writes)

    def dma(self, q, out_ap, in_ap, owner, reads=(), writes=(), **kw):
        if owner.dsem is None:
            if self.dpool:
                owner.dsem, owner.dcnt = self.dpool.pop()
            else:
                owner.dsem = self._mksem("D%d" % self.nsem)
        deps = self._deps(reads, writes)
        if owner.dcnt > 0 and deps.get(owner.dsem, 0) < owner.dcnt * 16:
            deps[owner.dsem] = owner.dcnt * 16
        self._emit_waits(q, deps)
        owner.dcnt += 1
        v = owner.dcnt * 16
        sem = self.sems[owner.dsem]
        self.prog[q].append(
            lambda e, o=out_ap, i=in_ap, sem=sem, kw=kw: e.dma_start(out=o, in_=i, **kw).then_inc(sem, 16))
        self._record((owner.dsem, v), reads, writes)

    def release(self, trs):
        for t in trs:
            if t.dsem is not None:
                self.dpool.append((t.dsem, t.dcnt))
                t.dsem = None

    def barrier(self):
        for e in self.ENG:
            self._emit_waits(e, {k: v for k, v in self.final.items() if v > 0})

    def emit(self):
        nc = self.nc
        progs = self.prog
        self.prog = {e: [] for e in self.ENG}
        with nc.Block() as block:
            @block.tensor
            def _(e):
                for t in progs["pe"]:
                    t(e)

            @block.scalar
            def _(e):
                for t in progs["act"]:
                    t(e)

            @block.vector
            def _(e):
                for t in progs["dve"]:
                    t(e)

            @block.gpsimd
            def _(e):
                for t in progs["pool"]:
                    t(e)

            @block.sync
            def _(e):
                for t in progs["sp"]:
                    t(e)


def I(name, *args, **kw):
    return (name, args, kw)


class B:
    __slots__ = ("t", "tr")

    def __init__(self, t, name):
        self.t = t
        self.tr = Tr(name)


def host_consts(T):
    c = {}
    a = np.arange(128)
    c["c_ident"] = np.eye(128, dtype=np.float32)
    mu = (a[:, None] <= a[None, :]).astype(np.float32)
    ml = (a[:, None] >= a[None, :]).astype(np.float32)
    c["c_mask"] = np.stack([mu, ml, np.ones((128, 128), np.float32)], axis=1)
    gf = np.zeros((128, 8, 128), np.float32)
    gb = np.zeros((128, 8, 128), np.float32)
    dec = np.zeros((128, 4, 8), np.float32)
    for h in range(8):
        lg = np.log1p(-(2.0 ** (-5.0 - h)))
        diff = (a[None, :] - a[:, None]).astype(np.float64)
        gf[:, h, :] = np.where(diff >= 0, np.exp(np.where(diff >= 0, diff, 0) * lg), 0.0)
        gb[:, h, :] = np.where(diff < 0, np.exp(np.where(diff < 0, -diff, 0) * lg), 0.0)
        dec[:, 0, h] = np.exp((a + 1.0) * lg)
        dec[:, 1, h] = np.exp((128.0 - a) * lg)
        dec[:, 2, h] = np.exp((127.0 - a) * lg)
        dec[:, 3, h] = np.exp((a + 0.0) * lg)
    c["c_gf"] = gf
    c["c_gb"] = gb
    c["c_dec"] = dec
    invf = np.power(np.float32(10000.0), -np.arange(64, dtype=np.float32) / np.float32(64.0)).astype(np.float32)
    c["c_invf"] = np.broadcast_to(invf[None, :], (128, 64)).copy()
    return c


def core_masks(i):
    m = np.zeros((128, 2, NCORES), np.float32)
    for j in range(NCORES):
        m[:, 0, j] = 1.0 if j < i else 0.0
        m[:, 1, j] = 1.0 if j > i else 0.0
    return m


def build(T, mode, last, debug=False, ncores=NCORES):
    NT = T // 128
    TH = T + 4
    nc = bass.Bass("TRN2", target_bir_lowering=False)
    dbg_kind = "ExternalOutput" if debug else "Internal"

    def din(name, shape, dt=F32):
        return nc.dram_tensor(name, list(shape), dt, kind="ExternalInput").ap()

    def dscr(name, shape, dt, kind=None):
        return nc.dram_tensor(name, list(shape), dt, kind=kind or dbg_kind).ap()

    x_in = din("x", [TH, D])
    pos_in = din("pos", [128, NT], I32)
    if mode == "p1":
        O_KM_, O_VM_, O_GM_, O_KR_, O_VR_ = 0, 1024, 3072, 3088, 4112
        w_in = din("w_in", [D, P1_COLS])
    else:
        O_KM_, O_VM_, O_GM_, O_KR_, O_VR_ = O_KM, O_VM, O_GM, O_KR, O_VR
        w_in = din("w_in", [D, DIN])
    nw_in = din("norm_w", [1, D])
    bg_in = din("b_mgate", [1, 16])
    cw_in = din("conv_w", [128, 16, 5])
    cb_in = din("conv_b", [128, 16])
    c_ident = din("c_ident", [128, 128])
    c_mask = din("c_mask", [128, 3, 128])
    c_gf = din("c_gf", [128, 8, 128])
    c_gb = din("c_gb", [128, 8, 128])
    c_dec = din("c_dec", [128, 4, 8])
    c_invf = din("c_invf", [128, 64])
    if mode == "p2":
        mnw_in = din("m_norm_w", [1, D])
        rnw_in = din("r_norm_w", [1, D])
        wpm_in = din("w_proj_m", [D, D])
        wpr_in = din("w_proj_r", [D, D])
        wo_in = din("w_out", [D, D])
        bmix_in = din("b_mix", [1, 2 * D])
        fnw_in = din("final_norm_w", [1, D])
        cm_in = din("cmask", [128, 2, ncores])
        gFc = din("gFc", [ncores, 2, 4, 128, 1024])
        gFn = din("gFn", [ncores, 2, 128, 8])
        gFs = din("gFs", [ncores, 2, 8, 128, 256])
        gG = din("gG", [ncores, 2, 128, 4])
        y_out = dscr("y", [T, D], F32, kind="ExternalOutput")
    else:
        oFc = dscr("oFc", [2, 4, 128, 1024], F32, kind="ExternalOutput")
        oFn = dscr("oFn", [2, 128, 8], F32, kind="ExternalOutput")
        oFs = dscr("oFs", [2, 8, 128, 256], F32, kind="ExternalOutput")
        oG = dscr("oG", [2, 128, 4], F32, kind="ExternalOutput")

    full = mode == "p2"
    d_kTm = dscr("d_kTm", [1024, T], BF16) if full else None
    d_qTm = dscr("d_qTm", [1024, T], BF16) if full else None
    d_km = dscr("d_km", [T, 1024], BF16)
    d_vm = dscr("d_vm", [T, 2048], BF16)
    d_g = dscr("d_g", [T, 16], F32)
    d_kr = dscr("d_kr", [T, 1024], BF16)
    d_vr = dscr("d_vr", [T, 2048], BF16)
    if full:
        d_szm = dscr("d_szm", [T, 2048], BF16)
        d_som = dscr("d_som", [T, 2048], BF16)
        d_qTr = dscr("d_qTr", [1024, T], BF16)
        d_kTr = dscr("d_kTr", [1024, T], BF16)
        d_szr = dscr("d_szr", [T, 2048], BF16)
        d_mix = dscr("d_mix", [T, 4096], BF16)
        d_ofm = dscr("d_ofm", [T, 2048], F32)
        d_ofr = dscr("d_ofr", [T, 2048], F32)
        d_uTm = dscr("d_uTm", [2048, T], BF16)
        d_uTr = dscr("d_uTr", [2048, T], BF16)
        d_xn = dscr("d_xn", [T, D], F32) if last else y_out

    with contextlib.ExitStack() as top:
        S = Sched(nc, top)

        def sb(st, name, shape, dt):
            return B(st.enter_context(nc.sbuf_tensor(name, list(shape), dt)), name)

        def ps(st, name, shape, dt):
            return B(st.enter_context(nc.psum_tensor(name, list(shape), dt)), name)

        identb = sb(top, "identb", [128, 128], BF16)
        mask = sb(top, "mask", [128, 3, 128], F32)
        dec = sb(top, "dec", [128, 4, 8], F32)
        onesb = sb(top, "onesb", [128, 128], BF16)
        S.dma("pool", identb.t[:], c_ident[:, :], identb.tr, writes=[identb.tr])
        S.dma("sp", mask.t[:], c_mask[:, :, :], mask.tr, writes=[mask.tr])
        S.dma("sp", dec.t[:], c_dec[:, :, :], dec.tr, writes=[dec.tr])
        S.op("pool", I("memset", onesb.t[:], 1.0), writes=[onesb.tr])

        with contextlib.ExitStack() as st:
            hT = sb(st, "hT", [128, KT, TH], BF16)
            hT_tr = [Tr("hT%d" % i) for i in range(NT + 1)]
            nwbc = sb(st, "nwbc", [128, D], F32)
            S.dma("sp", nwbc.t[:], nw_in.partition_broadcast(128), nwbc.tr, writes=[nwbc.tr])
            with contextlib.ExitStack() as st1:
                xt = [sb(st1, "xt%d" % i, [128, D], F32) for i in range(2)]
                hb = [sb(st1, "hb%d" % i, [128, D], BF16) for i in range(2)]
                junk = sb(st1, "junk", [128, D], BF16)
                ss = [sb(st1, "ss%d" % i, [128, 1], F32) for i in range(2)]
                pT = [ps(st1, "pT%d" % i, [128, 8, 128], BF16) for i in range(2)]
                for tt in range(NT + 1):
                    b = tt % 2
                    np_ = 128 if tt < NT else 4
                    r0 = tt * 128
                    x_, h_, s_ = xt[b], hb[b], ss[b]
                    S.dma("sp", x_.t[0:np_, :], x_in[r0:r0 + np_, :], x_.tr, writes=[x_.tr])
                    S.op("act", I("activation",
                        out=junk.t[0:np_, :], in_=x_.t[0:np_, :], func=AF.Square, accum_out=s_.t[0:np_, :]),
                        reads=[x_.tr], writes=[junk.tr, s_.tr])
                    S.op("act", I("activation",
                        out=s_.t[0:np_, :], in_=s_.t[0:np_, :], func=AF.Ln, scale=1.0 / D, bias=EPS),
                        reads=[s_.tr], writes=[s_.tr])
                    S.op("act", I("activation",
                        out=s_.t[0:np_, :], in_=s_.t[0:np_, :], func=AF.Exp, scale=-0.5),
                        reads=[s_.tr], writes=[s_.tr])
                    S.op("dve", I("scalar_tensor_tensor",
                        out=h_.t[0:np_, :], in0=x_.t[0:np_, :], scalar=s_.t[0:np_, 0:1], in1=nwbc.t[0:np_, :],
                        op0=ALU.mult, op1=ALU.mult), reads=[x_.tr, s_.tr, nwbc.tr], writes=[h_.tr])
                    for g in range(2):
                        p_ = pT[g]
                        S.op("pe", [I("transpose",
                            out=p_.t[:, k, 0:np_], in_=h_.t[0:np_, (g * 8 + k) * 128:(g * 8 + k + 1) * 128],
                            identity=identb.t[0:np_, 0:np_]) for k in range(8)],
                            reads=[h_.tr, identb.tr], writes=[p_.tr])
                        S.op("act" if g == 0 else "dve", I("copy" if g == 0 else "tensor_copy",
                            out=hT.t[:, g * 8:(g + 1) * 8, r0:r0 + np_], in_=p_.t[:, :, 0:np_]),
                            reads=[p_.tr], writes=[hT_tr[tt]])
                S.barrier()
                S.emit()
                S.release([b_.tr for b_ in xt])

            with contextlib.ExitStack() as st2:
                wb = [sb(st2, "wb%d" % i, [128, KT, 512], BF16) for i in range(2)]
                pm = [ps(st2, "pm%d" % i, [128, 512], F32) for i in range(3)]
                pq = [ps(st2, "pq%d" % i, [128, 4, 128], BF16) for i in range(2)]
                ev = [sb(st2, "ev%d" % i, [128, 512], F32) for i in range(3)]
                ob = [sb(st2, "ob%d" % i, [128, 512], BF16) for i in range(3)]
                ot = [sb(st2, "ot%d" % i, [128, 4, 128], BF16) for i in range(2)]
                bgbc = sb(st2, "bgbc", [128, 16], F32)
                S.dma("sp", bgbc.t[:], bg_in.partition_broadcast(128), bgbc.tr, writes=[bgbc.tr])
                cnt = {"w": 0, "pm": 0, "ev": 0, "ob": 0, "pq": 0, "ot": 0}

                def nxt(key, lst):
                    i = cnt[key] % len(lst)
                    cnt[key] += 1
                    return lst[i]

                def load_w(wdram, c0, ncol):
                    w_ = nxt("w", wb)
                    S.dma("pool", w_.t[:, :, 0:ncol], wdram[:, c0:c0 + ncol].rearrange("(k p) n -> p k n", p=128),
                          w_.tr, writes=[w_.tr])
                    return w_

                def gemm_tok(w_, ncol, tt, extra=None):
                    p_ = nxt("pm", pm)
                    fns = [I("matmul",
                        out=p_.t[:, 0:ncol], lhsT=hT.t[:, k, tt * 128:(tt + 1) * 128], rhs=w_.t[:, k, 0:ncol],
                        start=(k == 0), stop=(k == KT - 1 and extra is None)) for k in range(KT)]
                    rds = [hT_tr[tt], w_.tr]
                    if extra is not None:
                        lhsT_ap, rhs_ap, etr = extra
                        fns.append(I("matmul", out=p_.t[:, 0:ncol], lhsT=lhsT_ap, rhs=rhs_ap,
                                                             start=False, stop=True))
                        rds.append(etr)
                    S.op("pe", fns, reads=rds, writes=[p_.tr])
                    return p_

                def store_tok(o_, dst, tt, c0, ncol):
                    S.dma("sp", dst[tt * 128:(tt + 1) * 128, c0:c0 + ncol], o_.t[:, 0:ncol], o_.tr, reads=[o_.tr])

                def seg_act(wdram, wc0, func, dst, ncols, bias_row=None):
                    for cb in range(ncols // 512):
                        w_ = load_w(wdram, wc0 + cb * 512, 512)
                        for tt in range(NT):
                            extra = None
                            if bias_row is not None:
                                extra = (onesb.t[0:1, 0:128], bias_row.t[0:1, cb * 512:(cb + 1) * 512], bias_row.tr)
                            p_ = gemm_tok(w_, 512, tt, extra)
                            o_ = nxt("ob", ob)
                            if func is None:
                                S.op("dve", I("tensor_copy", out=o_.t[:], in_=p_.t[:]),
                                     reads=[p_.tr], writes=[o_.tr])
                            else:
                                S.op("act", I("activation", out=o_.t[:], in_=p_.t[:], func=func),
                                     reads=[p_.tr], writes=[o_.tr])
                            store_tok(o_, dst, tt, cb * 512, 512)

                with contextlib.ExitStack() as st3:
                    cw = sb(st3, "cw", [128, 16, 5], F32)
                    cbias = sb(st3, "cbias", [128, 16], F32)
                    S.dma("sp", cw.t[:], cw_in[:, :, :], cw.tr, writes=[cw.tr])
                    S.dma("sp", cbias.t[:], cb_in[:, :], cbias.tr, writes=[cbias.tr])
                    crow = [sb(st3, "crow%d" % i, [128, TH], F32) for i in range(2)]
                    cacc = [sb(st3, "cacc%d" % i, [128, T], F32) for i in range(2)]
                    cq = [sb(st3, "cq%d" % i, [128, T], BF16) for i in range(2)]
                    blocks = list(range(16)) if full else list(range(8, 16))
                    wcur = None
                    for bi, fb in enumerate(blocks):
                        if fb % 4 == 0 or wcur is None:
                            wcur = load_w(w_in, (fb // 4) * 512 if full else O_KM_ + ((fb - 8) // 4) * 512, 512)
                        j = fb % 4
                        cr, ca, cq_ = crow[bi % 2], cacc[bi % 2], cq[bi % 2]
                        groups = [(g0, min(512, T - g0)) for g0 in range(0, T, 512)] + [(T, 4)]
                        for (g0, gn) in groups:
                            p_ = nxt("pm", pm)
                            tts = sorted(set(min(t // 128, NT) for t in range(g0, g0 + gn, 128)) | ({NT} if g0 == T else set()))
                            S.op("pe", [I("matmul",
                                out=p_.t[:, 0:gn], lhsT=wcur.t[:, k, j * 128:(j + 1) * 128], rhs=hT.t[:, k, g0:g0 + gn],
                                start=(k == 0), stop=(k == KT - 1)) for k in range(KT)],
                                reads=[wcur.tr] + [hT_tr[t] for t in tts], writes=[p_.tr])
                            if g0 < T:
                                S.op("act", I("copy",
                                    out=cr.t[:, 2 + g0:2 + g0 + gn], in_=p_.t[:, 0:gn]), reads=[p_.tr], writes=[cr.tr])
                            else:
                                S.op("act", I("copy", out=cr.t[:, 0:2], in_=p_.t[:, 0:2]),
                                     reads=[p_.tr], writes=[cr.tr])
                                S.op("act", I("copy", out=cr.t[:, T + 2:T + 4], in_=p_.t[:, 2:4]),
                                     reads=[p_.tr], writes=[cr.tr])
                        S.op("dve", I("tensor_scalar",
                            out=ca.t[:], in0=cr.t[:, 0:T], scalar1=cw.t[:, fb, 0:1], scalar2=cbias.t[:, fb:fb + 1],
                            op0=ALU.mult, op1=ALU.add), reads=[cr.tr, cw.tr, cbias.tr], writes=[ca.tr])
                        for tap in range(1, 5):
                            S.op("dve", I("scalar_tensor_tensor",
                                out=ca.t[:], in0=cr.t[:, tap:tap + T], scalar=cw.t[:, fb, tap:tap + 1], in1=ca.t[:],
                                op0=ALU.mult, op1=ALU.add), reads=[cr.tr, cw.tr, ca.tr], writes=[ca.tr])
                        if fb < 8:
                            S.op("act", I("activation", out=ca.t[:], in_=ca.t[:], func=AF.Silu),
                                 reads=[ca.tr], writes=[ca.tr])
                            S.op("pool", I("tensor_scalar",
                                out=cq_.t[:], in0=ca.t[:], scalar1=1.0 / 16.0, scalar2=None, op0=ALU.mult),
                                reads=[ca.tr], writes=[cq_.tr])
                            S.dma("sp", d_qTm[fb * 128:(fb + 1) * 128, :], cq_.t[:], cq_.tr, reads=[cq_.tr])
                        else:
                            S.op("act", I("activation", out=cq_.t[:], in_=ca.t[:], func=AF.Silu),
                                 reads=[ca.tr], writes=[cq_.tr])
                            fk = fb - 8
                            if full:
                                S.dma("sp", d_kTm[fk * 128:(fk + 1) * 128, :], cq_.t[:], cq_.tr, reads=[cq_.tr])
                            for t4 in range(0, NT, 4):
                                nn = min(4, NT - t4)
                                q_ = nxt("pq", pq)
                                o_ = nxt("ot", ot)
                                S.op("pe", [I("transpose",
                                    out=q_.t[:, i, :], in_=cq_.t[:, (t4 + i) * 128:(t4 + i + 1) * 128], identity=identb.t[:])
                                    for i in range(nn)], reads=[cq_.tr, identb.tr], writes=[q_.tr])
                                S.op("dve", I("tensor_copy", out=o_.t[:, 0:nn, :], in_=q_.t[:, 0:nn, :]),
                                     reads=[q_.tr], writes=[o_.tr])
                                S.dma("sp", d_km[t4 * 128:(t4 + nn) * 128, fk * 128:(fk + 1) * 128].rearrange(
                                    "(i p) c -> p i c", p=128), o_.t[:, 0:nn, :], o_.tr, reads=[o_.tr])
                    S.barrier()
                    S.emit()
                    S.release([b_.tr for b_ in cq] + [cw.tr, cbias.tr])

                seg_act(w_in, O_VM_, None, d_vm, 2048)
                seg_act(w_in, O_VR_, None, d_vr, 2048)
                if full:
                    seg_act(w_in, O_ZM, AF.Silu, d_szm, 2048)
                    seg_act(w_in, O_OM, AF.Sigmoid, d_som, 2048)
                    seg_act(w_in, O_ZR, AF.Silu, d_szr, 2048)
                    bmixb = sb(st2, "bmixb", [1, 2 * D], BF16)
                    S.dma("pool", bmixb.t[:], bmix_in[:, :], bmixb.tr, writes=[bmixb.tr])
                    seg_act(w_in, O_GTM, AF.Sigmoid, d_mix, 4096, bias_row=bmixb)
                w_ = load_w(w_in, O_GM_, 16)
                gt = [sb(st2, "gt%d" % i, [128, 16], F32) for i in range(2)]
                ge = [sb(st2, "ge%d" % i, [128, 16], F32) for i in range(2)]
                for tt in range(NT):
                    p_ = gemm_tok(w_, 16, tt)
                    g_, e_ = gt[tt % 2], ge[tt % 2]
                    S.op("dve", I("tensor_tensor", out=g_.t[:], in0=p_.t[:, 0:16], in1=bgbc.t[:], op=ALU.add),
                         reads=[p_.tr, bgbc.tr], writes=[g_.tr])
                    S.op("act", I("activation", out=e_.t[:], in_=g_.t[:], func=AF.Exp, scale=-1.0),
                         reads=[g_.tr], writes=[e_.tr])
                    S.op("act", I("activation", out=e_.t[:], in_=e_.t[:], func=AF.Ln, bias=1.0),
                         reads=[e_.tr], writes=[e_.tr])
                    for c0 in (4, 12):
                        S.op("dve", I("tensor_scalar",
                            out=g_.t[:, c0:c0 + 4], in0=e_.t[:, c0:c0 + 4], scalar1=-1.0, scalar2=None, op0=ALU.mult),
                            reads=[e_.tr], writes=[g_.tr])
                    S.dma("sp", d_g[tt * 128:(tt + 1) * 128, :], g_.t[:], g_.tr, reads=[g_.tr])

                with contextlib.ExitStack() as st3:
                    posi = sb(st3, "posi", [128, NT], I32)
                    posf = sb(st3, "posf", [128, NT], F32)
                    invf = sb(st3, "invf", [128, 64], F32)
                    ang = sb(st3, "ang", [128, NT, 64], F32)
                    kf = sb(st3, "kf", [128, NT, 64], F32)
                    ki = sb(st3, "ki", [128, NT, 64], I32)
                    rr = sb(st3, "rr", [128, NT, 64], F32)
                    tmp = sb(st3, "tmpang", [128, NT, 64], F32)
                    msk = sb(st3, "mskang", [128, NT, 64], F32)
                    tab = sb(st3, "tab", [128, 4, NT, 64], F32)
                    S.dma("sp", posi.t[:], pos_in[:, :], posi.tr, writes=[posi.tr])
                    S.dma("sp", invf.t[:], c_invf[:, :], invf.tr, writes=[invf.tr])
                    S.op("dve", I("tensor_copy", out=posf.t[:], in_=posi.t[:]), reads=[posi.tr], writes=[posf.tr])
                    for t in range(NT):
                        S.op("dve", I("tensor_scalar", out=ang.t[:, t, :], in0=invf.t[:], scalar1=posf.t[:, t:t + 1],
                                                                   scalar2=None, op0=ALU.mult),
                             reads=[invf.tr, posf.tr], writes=[ang.tr])
                    S.op("dve", I("tensor_scalar", out=kf.t[:], in0=ang.t[:], scalar1=float(1.0 / (2 * np.pi)),
                                                          scalar2=None, op0=ALU.mult), reads=[ang.tr], writes=[kf.tr])
                    S.op("dve", I("tensor_copy", out=ki.t[:], in_=kf.t[:]), reads=[kf.tr], writes=[ki.tr])
                    S.op("dve", I("tensor_copy", out=kf.t[:], in_=ki.t[:]), reads=[ki.tr], writes=[kf.tr])
                    C1 = 6.28125
                    C2 = float(2 * np.pi - 6.28125)
                    S.op("dve", I("scalar_tensor_tensor", out=rr.t[:], in0=kf.t[:], scalar=-C1, in1=ang.t[:],
                                                                 op0=ALU.mult, op1=ALU.add), reads=[kf.tr, ang.tr], writes=[rr.tr])
                    S.op("dve", I("scalar_tensor_tensor", out=rr.t[:], in0=kf.t[:], scalar=-C2, in1=rr.t[:],
                                                                 op0=ALU.mult, op1=ALU.add), reads=[kf.tr, rr.tr], writes=[rr.tr])
                    for which, shift in ((1, 0.0), (0, PI / 2)):
                        S.op("dve", I("tensor_scalar", out=tmp.t[:], in0=rr.t[:], scalar1=shift, scalar2=None,
                                                                           op0=ALU.add), reads=[rr.tr], writes=[tmp.tr])
                        for (cmp_, thr, adj) in ((ALU.is_gt, PI, -2 * PI), (ALU.is_lt, -PI, 2 * PI)):
                            S.op("dve", I("tensor_scalar",
                                out=msk.t[:], in0=tmp.t[:], scalar1=thr, scalar2=None, op0=cmp_), reads=[tmp.tr], writes=[msk.tr])
                            S.op("dve", I("scalar_tensor_tensor",
                                out=tmp.t[:], in0=msk.t[:], scalar=adj, in1=tmp.t[:], op0=ALU.mult, op1=ALU.add),
                                reads=[msk.tr, tmp.tr], writes=[tmp.tr])
                        S.op("dve", I("tensor_scalar", out=tmp.t[:], in0=tmp.t[:], scalar1=PI, scalar2=-PI,
                                                              op0=ALU.min, op1=ALU.max), reads=[tmp.tr], writes=[tmp.tr])
                        S.op("act", I("activation", out=tab.t[:, which, :, :], in_=tmp.t[:], func=AF.Sin),
                             reads=[tmp.tr], writes=[tab.tr])
                    S.op("dve", I("tensor_scalar", out=tab.t[:, 2:4, :, :], in0=tab.t[:, 0:2, :, :],
                                                          scalar1=float(128.0 ** -0.5), scalar2=None, op0=ALU.mult),
                         reads=[tab.tr], writes=[tab.tr])
                    ra = [sb(st3, "ra%d" % i, [128, 4, 2, 64], F32) for i in range(2)]
                    rb = [sb(st3, "rb%d" % i, [128, 4, 2, 64], F32) for i in range(2)]
                    segs = ([("q", O_QR)] if full else []) + [("k", O_KR_)]
                    ri = 0
                    for (which, wc0) in segs:
                        tb = 2 if which == "q" else 0
                        for cb in range(2):
                            w_ = load_w(w_in, wc0 + cb * 512, 512)
                            for tt in range(NT):
                                p_ = gemm_tok(w_, 512, tt)
                                e_ = nxt("ev", ev)
                                o_ = nxt("ob", ob)
                                a_, b_ = ra[ri % 2], rb[ri % 2]
                                ri += 1
                                S.op("act", I("copy", out=e_.t[:], in_=p_.t[:]), reads=[p_.tr], writes=[e_.tr])
                                evv = e_.t[:].rearrange("p (h two j) -> p h two j", h=4, two=2)
                                ov = o_.t[:].rearrange("p (h two j) -> p h two j", h=4, two=2)
                                cosb = tab.t[:, tb, tt, :].unsqueeze(1).to_broadcast([128, 4, 64])
                                sinb = tab.t[:, tb + 1, tt, :].unsqueeze(1).to_broadcast([128, 4, 64])
                                S.op("dve", I("tensor_tensor",
                                    out=a_.t[:, :, 0, :], in0=evv[:, :, 0, :], in1=cosb, op=ALU.mult), reads=[e_.tr, tab.tr], writes=[a_.tr])
                                S.op("dve", I("tensor_tensor",
                                    out=a_.t[:, :, 1, :], in0=evv[:, :, 1, :], in1=cosb, op=ALU.mult), reads=[e_.tr, tab.tr], writes=[a_.tr])
                                S.op("pool", I("tensor_tensor",
                                    out=b_.t[:, :, 0, :], in0=evv[:, :, 1, :], in1=sinb, op=ALU.mult), reads=[e_.tr, tab.tr], writes=[b_.tr])
                                S.op("pool", I("tensor_tensor",
                                    out=b_.t[:, :, 1, :], in0=evv[:, :, 0, :], in1=sinb, op=ALU.mult), reads=[e_.tr, tab.tr], writes=[b_.tr])
                                S.op("dve", I("tensor_tensor",
                                    out=ov[:, :, 0, :], in0=a_.t[:, :, 0, :], in1=b_.t[:, :, 0, :], op=ALU.subtract),
                                    reads=[a_.tr, b_.tr], writes=[o_.tr])
                                S.op("dve", I("tensor_tensor",
                                    out=ov[:, :, 1, :], in0=a_.t[:, :, 1, :], in1=b_.t[:, :, 1, :], op=ALU.add),
                                    reads=[a_.tr, b_.tr], writes=[o_.tr])
                                if which == "k":
                                    store_tok(o_, d_kr, tt, cb * 512, 512)
                                if full:
                                    q_ = nxt("pq", pq)
                                    t_ = nxt("ot", ot)
                                    S.op("pe", [I("transpose",
                                        out=q_.t[:, i, :], in_=o_.t[:, i * 128:(i + 1) * 128], identity=identb.t[:])
                                        for i in range(4)], reads=[o_.tr, identb.tr], writes=[q_.tr])
                                    S.op("act", I("copy", out=t_.t[:], in_=q_.t[:]), reads=[q_.tr], writes=[t_.tr])
                                    dstT = d_qTr if which == "q" else d_kTr
                                    S.dma("sp", dstT[cb * 512:(cb + 1) * 512, tt * 128:(tt + 1) * 128].rearrange(
                                        "(i p) c -> p i c", p=128), t_.t[:], t_.tr, reads=[t_.tr])
                    S.barrier()
                    S.emit()
                    S.release([posi.tr, invf.tr])
                S.release([b_.tr for b_ in ob + ot + gt + wb] + [bgbc.tr])
            S.release([nwbc.tr])

        with contextlib.ExitStack() as st:
            gamT = [float(g ** T) for g in GAM]
            g128 = [float(g ** 128) for g in GAM]
            Cm = [sb(st, "Cm%d" % h, [128, 2, 512], F32) for h in range(4)]
            Nm = sb(st, "Nm", [128, 8], F32)
            Sr = [sb(st, "Sr%d" % h, [128, 256], F32) for h in range(8)]
            Cmb = [sb(st, "Cmb%d" % h, [128, 2, 512], BF16) for h in range(4)]
            Nmb = sb(st, "Nmb", [128, 8], BF16)
            Srb = [sb(st, "Srb%d" % h, [128, 256], BF16) for h in range(8)]
            Gac = sb(st, "Gac", [128, 4], F32)
            gfT = sb(st, "gfT", [128, 8, 128], F32)
            gbT = sb(st, "gbT", [128, 8, 128], F32)
            S.dma("sp", gfT.t[:], c_gf[:, :, :], gfT.tr, writes=[gfT.tr])
            S.dma("sp", gbT.t[:], c_gb[:, :, :], gbT.tr, writes=[gbT.tr])
            NB = 2
            gch = [sb(st, "gch%d" % i, [128, 16], F32) for i in range(NB)]
            kmc = [sb(st, "kmc%d" % i, [128, 1024], BF16) for i in range(NB)]
            vmc = [sb(st, "vmc%d" % i, [128, 2048], BF16) for i in range(NB)]
            krc = [sb(st, "krc%d" % i, [128, 1024], BF16) for i in range(NB)]
            vrc = [sb(st, "vrc%d" % i, [128, 2048], BF16) for i in range(NB)]
            if full:
                qTmc = [sb(st, "qTmc%d" % i, [128, 8, 128], BF16) for i in range(NB)]
                kTmc = [sb(st, "kTmc%d" % i, [128, 8, 128], BF16) for i in range(NB)]
                qTrc = [sb(st, "qTrc%d" % i, [128, 8, 128], BF16) for i in range(NB)]
                kTrc = [sb(st, "kTrc%d" % i, [128, 8, 128], BF16) for i in range(NB)]
                om = [sb(st, "om%d" % i, [128, 2048], F32) for i in range(2)]
                orr = [sb(st, "or%d" % i, [128, 2048], F32) for i in range(2)]
                AT = [sb(st, "AT%d" % i, [128, 128], BF16) for i in range(2)]
                Vh = [sb(st, "Vh%d" % i, [128, 512], BF16) for i in range(2)]
                p1s = [sb(st, "p1s%d" % i, [128, 256], F32) for i in range(2)]
                ps_n = [ps(st, "ps_n%d" % i, [128, 512], F32) for i in range(2)]
            Kh = [sb(st, "Kh%d" % i, [128, 256], BF16) for i in range(2)]
            sc = [sb(st, "sc%d" % i, [128, 8, 4], F32) for i in range(2)]
            scb = [sb(st, "scb%d" % i, [128, 4], BF16) for i in range(2)]
            rsc = [sb(st, "rsc%d" % i, [128, 4], F32) for i in range(2)]
            ps_c = [ps(st, "ps_c%d" % i, [128, 512], F32) for i in range(2)]
            small = ps(st, "ps_small", [128, 512], F32)
            ps_s = [B(small.t[:, i * 128:(i + 1) * 128], "ps_s%d" % i) for i in range(2)]
            ps_g = [B(small.t[:, 256 + i * 8:256 + (i + 1) * 8], "ps_g%d" % i) for i in range(2)]
            ps_d = [B(small.t[:, 272 + i * 2:272 + (i + 1) * 2], "ps_d%d" % i) for i in range(2)]
            ps_x = [B(small.t[:, 276 + i * 2:276 + (i + 1) * 2], "ps_x%d" % i) for i in range(2)]
            rot = {}

            def nx(key, lst):
                i = rot.get(key, 0)
                rot[key] = i + 1
                return lst[i % len(lst)]

            def zero_states():
                for h in range(4):
                    S.op("pool", I("memset", Cm[h].t[:], 0.0), writes=[Cm[h].tr])
                    S.op("pool", I("memset", Cmb[h].t[:], 0.0), writes=[Cmb[h].tr])
                for h in range(8):
                    S.op("pool", I("memset", Sr[h].t[:], 0.0), writes=[Sr[h].tr])
                    S.op("pool", I("memset", Srb[h].t[:], 0.0), writes=[Srb[h].tr])
                S.op("pool", I("memset", Nm.t[:], 0.0), writes=[Nm.tr])
                S.op("pool", I("memset", Nmb.t[:], 0.0), writes=[Nmb.tr])
                S.op("pool", I("memset", Gac.t[:], 0.0), writes=[Gac.tr])

            def refresh_bf():
                for h in range(4):
                    S.op("act", I("copy", out=Cmb[h].t[:], in_=Cm[h].t[:]), reads=[Cm[h].tr], writes=[Cmb[h].tr])
                for h in range(8):
                    S.op("act", I("copy", out=Srb[h].t[:], in_=Sr[h].t[:]), reads=[Sr[h].tr], writes=[Srb[h].tr])
                S.op("act", I("copy", out=Nmb.t[:], in_=Nm.t[:]), reads=[Nm.tr], writes=[Nmb.tr])

            def sweep(d, outputs, post=None):
                order = list(range(NT)) if d == 0 else list(range(NT - 1, -1, -1))
                gT = gfT if d == 0 else gbT
                for ci, c in enumerate(order):
                    r0 = c * 128
                    b = ci % NB
                    g_, km_, vm_, kr_, vr_ = gch[b], kmc[b], vmc[b], krc[b], vrc[b]
                    S.dma("sp", g_.t[:], d_g[r0:r0 + 128, :], g_.tr, writes=[g_.tr])
                    S.dma("sp", km_.t[:], d_km[r0:r0 + 128, :], km_.tr, writes=[km_.tr])
                    S.dma("sp", vm_.t[:], d_vm[r0:r0 + 128, :], vm_.tr, writes=[vm_.tr])
                    S.dma("sp", kr_.t[:], d_kr[r0:r0 + 128, :], kr_.tr, writes=[kr_.tr])
                    S.dma("sp", vr_.t[:], d_vr[r0:r0 + 128, :], vr_.tr, writes=[vr_.tr])
                    if outputs:
                        qTm_, kTm_, qTr_, kTr_ = qTmc[b], kTmc[b], qTrc[b], kTrc[b]
                        for (dst, src) in ((qTm_, d_qTm), (kTm_, d_kTm), (qTr_, d_qTr), (kTr_, d_kTr)):
                            S.dma("sp", dst.t[:], src[:, r0:r0 + 128].rearrange("(f p) t -> p f t", p=128), dst.tr, writes=[dst.tr])
                    s_ = nx("sc", sc)
                    sb_ = nx("scb", scb)
                    pg = nx("psg", ps_g)
                    ic, fc = (0, 4) if d == 0 else (8, 12)
                    S.op("pe", [I("matmul", out=pg.t[:, 0:4], lhsT=mask.t[:, d, :], rhs=g_.t[:, fc:fc + 4],
                                                                        start=True, stop=True),
                                I("matmul", out=pg.t[:, 4:8], lhsT=mask.t[:, 2, :], rhs=g_.t[:, fc:fc + 4],
                                                                        start=True, stop=True)],
                         reads=[mask.tr, g_.tr], writes=[pg.tr])
                    S.op("dve", I("tensor_copy", out=s_.t[:, 0:2, :], in_=pg.t[:].rearrange("p (a b) -> p a b", a=2)),
                         reads=[pg.tr], writes=[s_.tr])
                    S.op("act", I("activation", out=s_.t[:, 2, :], in_=s_.t[:, 0, :], func=AF.Exp), reads=[s_.tr], writes=[s_.tr])
                    S.op("dve", I("tensor_tensor", out=s_.t[:, 6, :], in0=g_.t[:, ic:ic + 4], in1=s_.t[:, 0, :],
                                                                               op=ALU.subtract), reads=[s_.tr, g_.tr], writes=[s_.tr])
                    S.op("act", I("activation", out=s_.t[:, 3, :], in_=s_.t[:, 6, :], func=AF.Exp), reads=[s_.tr], writes=[s_.tr])
                    S.op("dve", I("tensor_tensor", out=s_.t[:, 7, :], in0=s_.t[:, 6, :], in1=s_.t[:, 1, :], op=ALU.add),
                         reads=[s_.tr], writes=[s_.tr])
                    S.op("act", I("activation", out=s_.t[:, 4, :], in_=s_.t[:, 7, :], func=AF.Exp), reads=[s_.tr], writes=[s_.tr])
                    S.op("act", I("activation", out=s_.t[:, 5, :], in_=s_.t[:, 1, :], func=AF.Exp), reads=[s_.tr], writes=[s_.tr])
                    S.op("dve", I("tensor_copy", out=sb_.t[:], in_=s_.t[:, 3, :]), reads=[s_.tr], writes=[sb_.tr])
                    if not outputs:
                        S.op("dve", I("tensor_tensor", out=Gac.t[:], in0=Gac.t[:], in1=s_.t[:, 1, :], op=ALU.add),
                             reads=[s_.tr, Gac.tr], writes=[Gac.tr])
                    if outputs:
                        o_m = nx("om", om)
                        o_r = nx("or", orr)
                    for h in range(4):
                        if outputs:
                            pss = nx("pss", ps_s)
                            a_t = nx("AT", AT)
                            vh = nx("Vh", Vh)
                            pn = nx("psn", ps_n)
                            pd = nx("psd", ps_d)
                            r_ = nx("rsc", rsc)
                            S.op("pe", [I("matmul",
                                out=pss.t[:], lhsT=kTm_.t[:, h * 2 + kt, :], rhs=qTm_.t[:, h * 2 + kt, :], start=(kt == 0), stop=(kt == 1))
                                for kt in range(2)], reads=[kTm_.tr, qTm_.tr], writes=[pss.tr])
                            S.op("dve", I("tensor_tensor", out=a_t.t[:], in0=pss.t[:], in1=mask.t[:, d, :], op=ALU.mult),
                                 reads=[pss.tr, mask.tr], writes=[a_t.tr])
                            S.op("pool", I("tensor_scalar",
                                out=vh.t[:], in0=vm_.t[:, h * 512:(h + 1) * 512], scalar1=s_.t[:, 3, h:h + 1], scalar2=None, op0=ALU.mult),
                                reads=[vm_.tr, s_.tr], writes=[vh.tr])
                            S.op("pe", [I("matmul", out=pn.t[:], lhsT=a_t.t[:], rhs=vh.t[:], start=True, stop=False)]
                                 + [I("matmul", out=pn.t[:], lhsT=qTm_.t[:, h * 2 + kt, :], rhs=Cmb[h].t[:, kt, :],
                                                                          start=False, stop=(kt == 1)) for kt in range(2)]
                                 + [I("matmul", out=pd.t[:, 0:1], lhsT=a_t.t[:], rhs=sb_.t[:, h:h + 1],
                                                                                     start=True, stop=False)]
                                 + [I("matmul", out=pd.t[:, 0:1], lhsT=qTm_.t[:, h * 2 + kt, :],
                                                                          rhs=Nmb.t[:, h * 2 + kt:h * 2 + kt + 1], start=False, stop=(kt == 1))
                                    for kt in range(2)],
                                 reads=[a_t.tr, vh.tr, qTm_.tr, Cmb[h].tr, sb_.tr, Nmb.tr], writes=[pn.tr, pd.tr])
                            S.op("dve", I("tensor_scalar",
                                out=r_.t[:, 0:1], in0=pd.t[:, 0:1], scalar1=s_.t[:, 2, h:h + 1], scalar2=None, op0=ALU.mult),
                                reads=[pd.tr, s_.tr], writes=[r_.tr])
                            S.op("dve", I("tensor_scalar",
                                out=r_.t[:, 3:4], in0=r_.t[:, 0:1], scalar1=-1.0, scalar2=None, op0=ALU.mult),
                                reads=[r_.tr], writes=[r_.tr])
                            S.op("dve", I("scalar_tensor_tensor",
                                out=r_.t[:, 0:1], in0=r_.t[:, 0:1], scalar=1.0, in1=r_.t[:, 3:4], op0=ALU.max, op1=ALU.max),
                                reads=[r_.tr], writes=[r_.tr])
                            S.op("dve", I("reciprocal", out=r_.t[:, 1:2], in_=r_.t[:, 0:1]), reads=[r_.tr], writes=[r_.tr])
                            S.op("dve", I("tensor_tensor", out=r_.t[:, 2:3], in0=r_.t[:, 1:2], in1=s_.t[:, 2, h:h + 1],
                                                                                     op=ALU.mult), reads=[r_.tr, s_.tr], writes=[r_.tr])
                            S.op("act", I("activation",
                                out=o_m.t[:, h * 512:(h + 1) * 512], in_=pn.t[:], func=AF.Copy, scale=r_.t[:, 2:3]),
                                reads=[pn.tr, r_.tr], writes=[o_m.tr])
                        kh = nx("Kh", Kh)
                        S.op("pool", I("tensor_scalar",
                            out=kh.t[:], in0=km_.t[:, h * 256:(h + 1) * 256], scalar1=s_.t[:, 4, h:h + 1], scalar2=None, op0=ALU.mult),
                            reads=[km_.tr, s_.tr], writes=[kh.tr])
                        px = nx("psx", ps_x)
                        pcs = []
                        for kt in range(2):
                            pc = nx("psc", ps_c)
                            pcs.append(pc)
                            S.op("pe", [I("matmul",
                                out=pc.t[:], lhsT=kh.t[:, kt * 128:(kt + 1) * 128], rhs=vm_.t[:, h * 512:(h + 1) * 512], start=True, stop=True),
                                I("matmul",
                                out=px.t[:, kt:kt + 1], lhsT=kh.t[:, kt * 128:(kt + 1) * 128], rhs=onesb.t[:, 0:1], start=True, stop=True)],
                                reads=[kh.tr, vm_.tr, onesb.tr], writes=[pc.tr, px.tr])
                        for kt in range(2):
                            S.op("dve", I("scalar_tensor_tensor",
                                out=Cm[h].t[:, kt, :], in0=Cm[h].t[:, kt, :], scalar=s_.t[:, 5, h:h + 1], in1=pcs[kt].t[:], op0=ALU.mult, op1=ALU.add),
                                reads=[Cm[h].tr, s_.tr, pcs[kt].tr], writes=[Cm[h].tr])
                        S.op("dve", I("scalar_tensor_tensor",
                            out=Nm.t[:, h * 2:h * 2 + 2], in0=Nm.t[:, h * 2:h * 2 + 2], scalar=s_.t[:, 5, h:h + 1], in1=px.t[:, 0:2],
                            op0=ALU.mult, op1=ALU.add), reads=[Nm.tr, s_.tr, px.tr], writes=[Nm.tr])
                        if outputs:
                            S.op("act", I("copy", out=Cmb[h].t[:], in_=Cm[h].t[:]), reads=[Cm[h].tr], writes=[Cmb[h].tr])
                    if outputs:
                        S.op("act", I("copy", out=Nmb.t[:], in_=Nm.t[:]), reads=[Nm.tr], writes=[Nmb.tr])
                    for h in range(8):
                        if outputs:
                            pss = nx("pss", ps_s)
                            a_t = nx("AT", AT)
                            pn = nx("psn", ps_n)
                            p1 = nx("p1s", p1s)
                            S.op("pe", I("matmul", out=pss.t[:], lhsT=kTr_.t[:, h, :], rhs=qTr_.t[:, h, :], start=True, stop=True),
                                 reads=[kTr_.tr, qTr_.tr], writes=[pss.tr])
                            S.op("dve", I("tensor_tensor", out=a_t.t[:], in0=pss.t[:], in1=gT.t[:, h, :], op=ALU.mult),
                                 reads=[pss.tr, gT.tr], writes=[a_t.tr])
                            S.op("pe", [I("matmul", out=pn.t[:, 0:256], lhsT=a_t.t[:], rhs=vr_.t[:, h * 256:(h + 1) * 256],
                                                                                start=True, stop=True),
                                        I("matmul", out=pn.t[:, 256:512], lhsT=qTr_.t[:, h, :], rhs=Srb[h].t[:], start=True, stop=True)],
                                 reads=[a_t.tr, vr_.tr, qTr_.tr, Srb[h].tr], writes=[pn.tr])
                            S.op("act", I("copy", out=p1.t[:], in_=pn.t[:, 0:256]), reads=[pn.tr], writes=[p1.tr])
                            S.op("dve", I("scalar_tensor_tensor",
                                out=o_r.t[:, h * 256:(h + 1) * 256], in0=pn.t[:, 256:512], scalar=dec.t[:, d, h:h + 1], in1=p1.t[:],
                                op0=ALU.mult, op1=ALU.add), reads=[pn.tr, p1.tr, dec.tr], writes=[o_r.tr])
                        kh = nx("Kh", Kh)
                        S.op("pool", I("tensor_scalar",
                            out=kh.t[:, 0:128], in0=kr_.t[:, h * 128:(h + 1) * 128], scalar1=dec.t[:, 2 + d, h:h + 1], scalar2=None, op0=ALU.mult),
                            reads=[kr_.tr, dec.tr], writes=[kh.tr])
                        pc = nx("psc", ps_c)
                        S.op("pe", I("matmul", out=pc.t[:, 0:256], lhsT=kh.t[:, 0:128], rhs=vr_.t[:, h * 256:(h + 1) * 256],
                                                                         start=True, stop=True), reads=[kh.tr, vr_.tr], writes=[pc.tr])
                        S.op("dve", I("scalar_tensor_tensor",
                            out=Sr[h].t[:], in0=Sr[h].t[:], scalar=g128[h], in1=pc.t[:, 0:256], op0=ALU.mult, op1=ALU.add),
                            reads=[Sr[h].tr, pc.tr], writes=[Sr[h].tr])
                        if outputs:
                            S.op("act", I("copy", out=Srb[h].t[:], in_=Sr[h].t[:]), reads=[Sr[h].tr], writes=[Srb[h].tr])
                    if outputs:
                        post(d, c, o_m, o_r)

            if mode == "p1":
                for d in range(2):
                    zero_states()
                    sweep(d, False)
                    for h in range(4):
                        S.dma("sp", oFc[d, h, :, :], Cm[h].t[:].rearrange("p a b -> p (a b)"), Cm[h].tr, reads=[Cm[h].tr])
                    for h in range(8):
                        S.dma("sp", oFs[d, h, :, :], Sr[h].t[:], Sr[h].tr, reads=[Sr[h].tr])
                    S.dma("sp", oFn[d, :, :], Nm.t[:], Nm.tr, reads=[Nm.tr])
                    S.dma("sp", oG[d, :, :], Gac.t[:], Gac.tr, reads=[Gac.tr])
                S.barrier()
                S.emit()
            else:
                cm = sb(st, "cm", [128, 2, ncores], F32)
                S.dma("sp", cm.t[:], cm_in[:, :, :], cm.tr, writes=[cm.tr])
                gGs = sb(st, "gGs", [128, ncores, 4], F32)
                cf = sb(st, "cf", [128, ncores, 4], F32)
                fld = [sb(st, "fld%d" % i, [128, 1024], F32) for i in range(2)]
                fln = [sb(st, "fln%d" % i, [128, 8], F32) for i in range(2)]

                def combine(d):
                    zero_states()
                    S.dma("sp", gGs.t[:], gG[:, d, :, :].rearrange("j p h -> p j h"), gGs.tr, writes=[gGs.tr])
                    S.op("act", I("activation", out=cf.t[:], in_=gGs.t[:], func=AF.Exp), reads=[gGs.tr], writes=[cf.tr])
                    S.op("dve", I("tensor_scalar", out=cf.t[:], in0=cf.t[:], scalar1=-1.0, scalar2=None, op0=ALU.add),
                         reads=[cf.tr], writes=[cf.tr])
                    S.op("dve", I("tensor_tensor", out=cf.t[:], in0=cf.t[:], in1=cm.t[:, d, :].unsqueeze(2).to_broadcast([128, ncores, 4]),
                                                          op=ALU.mult), reads=[cf.tr, cm.tr], writes=[cf.tr])
                    S.op("dve", I("tensor_scalar", out=cf.t[:], in0=cf.t[:], scalar1=1.0, scalar2=None, op0=ALU.add),
                         reads=[cf.tr], writes=[cf.tr])
                    order = list(range(ncores)) if d == 0 else list(range(ncores - 1, -1, -1))
                    fi = 0
                    for j in order:
                        mj = cm.t[:, d, j:j + 1]
                        for h in range(4):
                            f_ = fld[fi % 2]
                            fi += 1
                            S.dma("sp", f_.t[:], gFc[j, d, h, :, :], f_.tr, writes=[f_.tr])
                            S.op("pool", I("tensor_scalar", out=f_.t[:], in0=f_.t[:], scalar1=mj, scalar2=None, op0=ALU.mult),
                                 reads=[f_.tr, cm.tr], writes=[f_.tr])
                            S.op("dve", I("scalar_tensor_tensor",
                                out=Cm[h].t[:].rearrange("p a b -> p (a b)"), in0=Cm[h].t[:].rearrange("p a b -> p (a b)"),
                                scalar=cf.t[:, j, h:h + 1], in1=f_.t[:], op0=ALU.mult, op1=ALU.add),
                                reads=[Cm[h].tr, cf.tr, f_.tr], writes=[Cm[h].tr])
                        n_ = fln[fi % 2]
                        S.dma("sp", n_.t[:], gFn[j, d, :, :], n_.tr, writes=[n_.tr])
                        S.op("pool", I("tensor_scalar", out=n_.t[:], in0=n_.t[:], scalar1=mj, scalar2=None, op0=ALU.mult),
                             reads=[n_.tr, cm.tr], writes=[n_.tr])
                        for h in range(4):
                            S.op("dve", I("scalar_tensor_tensor",
                                out=Nm.t[:, h * 2:h * 2 + 2], in0=Nm.t[:, h * 2:h * 2 + 2], scalar=cf.t[:, j, h:h + 1], in1=n_.t[:, h * 2:h * 2 + 2],
                                op0=ALU.mult, op1=ALU.add), reads=[Nm.tr, cf.tr, n_.tr], writes=[Nm.tr])
                        for h in range(8):
                            f_ = fld[fi % 2]
                            fi += 1
                            S.dma("sp", f_.t[:, 0:256], gFs[j, d, h, :, :], f_.tr, writes=[f_.tr])
                            S.op("pool", I("tensor_scalar", out=f_.t[:, 0:256], in0=f_.t[:, 0:256], scalar1=mj, scalar2=None,
                                                                                 op0=ALU.mult), reads=[f_.tr, cm.tr], writes=[f_.tr])
                            S.op("dve", I("tensor_scalar",
                                out=f_.t[:, 256:257], in0=mj, scalar1=gamT[h] - 1.0, scalar2=1.0, op0=ALU.mult, op1=ALU.add),
                                reads=[cm.tr, f_.tr], writes=[f_.tr])
                            S.op("dve", I("scalar_tensor_tensor",
                                out=Sr[h].t[:], in0=Sr[h].t[:], scalar=f_.t[:, 256:257], in1=f_.t[:, 0:256], op0=ALU.mult, op1=ALU.add),
                                reads=[Sr[h].tr, f_.tr], writes=[Sr[h].tr])
                    refresh_bf()

                mnwbc = sb(st, "mnwbc", [128, D], F32)
                rnwbc = sb(st, "rnwbc", [128, D], F32)
                S.dma("sp", mnwbc.t[:], mnw_in.partition_broadcast(128), mnwbc.tr, writes=[mnwbc.tr])
                S.dma("sp", rnwbc.t[:], rnw_in.partition_broadcast(128), rnwbc.tr, writes=[rnwbc.tr])
                of_l = [sb(st, "ofl%d" % i, [128, 2048], F32) for i in range(2)]
                gl = [sb(st, "gl%d" % i, [128, 2048], BF16) for i in range(2)]
                ub = [sb(st, "ub%d" % i, [128, 2048], BF16) for i in range(2)]
                jk = sb(st, "jk", [128, 512], BF16)
                hs = [sb(st, "hs%d" % i, [128, 16], F32) for i in range(2)]
                pu = [ps(st, "pu%d" % i, [128, 8, 128], BF16) for i in range(1)]
                uT = [sb(st, "uTs%d" % i, [128, 8, 128], BF16) for i in range(2)]

                def post(d, c, o_m, o_r):
                    r0 = c * 128
                    if d == 0:
                        S.dma("sp", d_ofm[r0:r0 + 128, :], o_m.t[:], o_m.tr, reads=[o_m.tr])
                        S.dma("sp", d_ofr[r0:r0 + 128, :], o_r.t[:], o_r.tr, reads=[o_r.tr])
                        return
                    for (o_, d_of, d_gate, d_sz, nwb, dU, nh, dv) in (
                            (o_m, d_ofm, d_som, d_szm, mnwbc, d_uTm, 4, 512), (o_r, d_ofr, None, d_szr, rnwbc, d_uTr, 8, 256)):
                        ofl = nx("ofl", of_l)
                        S.dma("sp", ofl.t[:], d_of[r0:r0 + 128, :], ofl.tr, writes=[ofl.tr])
                        S.op("dve", I("tensor_tensor", out=o_.t[:], in0=o_.t[:], in1=ofl.t[:], op=ALU.add),
                             reads=[o_.tr, ofl.tr], writes=[o_.tr])
                        if d_gate is not None:
                            g2 = nx("gl", gl)
                            S.dma("sp", g2.t[:], d_gate[r0:r0 + 128, :], g2.tr, writes=[g2.tr])
                            S.op("pool", I("tensor_tensor", out=o_.t[:], in0=o_.t[:], in1=g2.t[:], op=ALU.mult),
                                 reads=[o_.tr, g2.tr], writes=[o_.tr])
                        z2 = nx("gl", gl)
                        S.dma("sp", z2.t[:], d_sz[r0:r0 + 128, :], z2.tr, writes=[z2.tr])
                        h_ = nx("hs", hs)
                        for h in range(nh):
                            S.op("act", I("activation",
                                out=jk.t[:, 0:dv], in_=o_.t[:, h * dv:(h + 1) * dv], func=AF.Square, accum_out=h_.t[:, h:h + 1]),
                                reads=[o_.tr], writes=[jk.tr, h_.tr])
                        S.op("act", I("activation", out=h_.t[:, 0:nh], in_=h_.t[:, 0:nh], func=AF.Ln, scale=1.0 / dv, bias=EPS),
                             reads=[h_.tr], writes=[h_.tr])
                        S.op("act", I("activation", out=h_.t[:, 0:nh], in_=h_.t[:, 0:nh], func=AF.Exp, scale=-0.5),
                             reads=[h_.tr], writes=[h_.tr])
                        u_ = nx("ub", ub)
                        for h in range(nh):
                            S.op("dve", I("scalar_tensor_tensor",
                                out=o_.t[:, h * dv:(h + 1) * dv], in0=o_.t[:, h * dv:(h + 1) * dv], scalar=h_.t[:, h:h + 1],
                                in1=nwb.t[:, h * dv:(h + 1) * dv], op0=ALU.mult, op1=ALU.mult), reads=[o_.tr, h_.tr, nwb.tr], writes=[o_.tr])
                        S.op("pool", I("tensor_tensor", out=u_.t[:], in0=o_.t[:], in1=z2.t[:], op=ALU.mult),
                             reads=[o_.tr, z2.tr], writes=[u_.tr])
                        for g in range(2):
                            p_ = pu[0]
                            t_ = nx("uTs", uT)
                            S.op("pe", [I("transpose",
                                out=p_.t[:, k, :], in_=u_.t[:, (g * 8 + k) * 128:(g * 8 + k + 1) * 128], identity=identb.t[:])
                                for k in range(8)], reads=[u_.tr, identb.tr], writes=[p_.tr])
                            S.op("act", I("copy", out=t_.t[:], in_=p_.t[:]), reads=[p_.tr], writes=[t_.tr])
                            S.dma("sp", dU[g * 1024:(g + 1) * 1024, r0:r0 + 128].rearrange("(k p) t -> p k t", p=128), t_.t[:], t_.tr, reads=[t_.tr])

                combine(0)
                sweep(0, True, post)
                S.barrier()
                combine(1)
                sweep(1, True, post)
                S.barrier()
                S.emit()

        if mode == "p2":
            TC = min(T, 1024)
            NTC = TC // 128
            with contextlib.ExitStack() as st:
                uTm = sb(st, "uTm", [128, KT, TC], BF16)
                uTr = sb(st, "uTr", [128, KT, TC], BF16)
                yT = sb(st, "yT", [128, KT, TC], BF16)
                yT_tr = [Tr("yT%d" % i) for i in range(NTC)]
                wm = [sb(st, "wm%d" % i, [128, KT, 512], BF16) for i in range(2)]
                wr = [sb(st, "wr%d" % i, [128, KT, 512], BF16) for i in range(2)]
                pm_ = [ps(st, "cpm%d" % i, [128, 512], F32) for i in range(2)]
                pr_ = [ps(st, "cpr%d" % i, [128, 512], F32) for i in range(2)]
                pq_ = [ps(st, "cpq%d" % i, [128, 4, 128], BF16) for i in range(2)]
                mx = [sb(st, "mx%d" % i, [128, 2, 512], BF16) for i in range(2)]
                t1 = [sb(st, "t1%d" % i, [128, 512], F32) for i in range(2)]
                t2 = [sb(st, "t2%d" % i, [128, 512], F32) for i in range(2)]
                yb = [sb(st, "yb%d" % i, [128, 512], BF16) for i in range(2)]
                xr = [sb(st, "xr%d" % i, [128, 512], F32) for i in range(2)]
                rot2 = {}

                def n2(key, lst):
                    i = rot2.get(key, 0)
                    rot2[key] = i + 1
                    return lst[i % len(lst)]

                for t0 in range(0, T, TC):
                    S.dma("sp", uTm.t[:], d_uTm[:, t0:t0 + TC].rearrange("(k p) t -> p k t", p=128), uTm.tr, writes=[uTm.tr])
                    S.dma("sp", uTr.t[:], d_uTr[:, t0:t0 + TC].rearrange("(k p) t -> p k t", p=128), uTr.tr, writes=[uTr.tr])
                    for cb in range(4):
                        wm_, wr_ = n2("wm", wm), n2("wr", wr)
                        S.dma("pool", wm_.t[:], wpm_in[:, cb * 512:(cb + 1) * 512].rearrange("(k p) n -> p k n", p=128), wm_.tr, writes=[wm_.tr])
                        S.dma("pool", wr_.t[:], wpr_in[:, cb * 512:(cb + 1) * 512].rearrange("(k p) n -> p k n", p=128), wr_.tr, writes=[wr_.tr])
                        for tt in range(NTC):
                            r0 = t0 + tt * 128
                            a_, b_ = n2("pm", pm_), n2("pr", pr_)
                            S.op("pe", [I("matmul", out=a_.t[:], lhsT=uTm.t[:, k, tt * 128:(tt + 1) * 128],
                                                                                      rhs=wm_.t[:, k, :], start=(k == 0), stop=(k == KT - 1))
                                        for k in range(KT)], reads=[uTm.tr, wm_.tr], writes=[a_.tr])
                            S.op("pe", [I("matmul", out=b_.t[:], lhsT=uTr.t[:, k, tt * 128:(tt + 1) * 128],
                                                                                      rhs=wr_.t[:, k, :], start=(k == 0), stop=(k == KT - 1))
                                        for k in range(KT)], reads=[uTr.tr, wr_.tr], writes=[b_.tr])
                            m_ = n2("mx", mx)
                            S.dma("sp", m_.t[:], d_mix[r0:r0 + 128, :].rearrange("p (two c) -> p two c", two=2)[:, :, cb * 512:(cb + 1) * 512],
                                  m_.tr, writes=[m_.tr])
                            u1, u2, y_ = n2("t1", t1), n2("t2", t2), n2("yb", yb)
                            S.op("dve", I("tensor_tensor", out=u1.t[:], in0=a_.t[:], in1=m_.t[:, 0, :], op=ALU.mult),
                                 reads=[a_.tr, m_.tr], writes=[u1.tr])
                            S.op("dve", I("tensor_tensor", out=u2.t[:], in0=b_.t[:], in1=m_.t[:, 1, :], op=ALU.mult),
                                 reads=[b_.tr, m_.tr], writes=[u2.tr])
                            S.op("pool", I("tensor_tensor", out=y_.t[:], in0=u1.t[:], in1=u2.t[:], op=ALU.add),
                                 reads=[u1.tr, u2.tr], writes=[y_.tr])
                            q_ = n2("pq", pq_)
                            S.op("pe", [I("transpose", out=q_.t[:, i, :], in_=y_.t[:, i * 128:(i + 1) * 128], identity=identb.t[:])
                                        for i in range(4)], reads=[y_.tr, identb.tr], writes=[q_.tr])
                            S.op("act", I("copy", out=yT.t[:, cb * 4:(cb + 1) * 4, tt * 128:(tt + 1) * 128], in_=q_.t[:]),
                                 reads=[q_.tr], writes=[yT_tr[tt]])
                    for cb in range(4):
                        wm_ = n2("wm", wm)
                        S.dma("pool", wm_.t[:], wo_in[:, cb * 512:(cb + 1) * 512].rearrange("(k p) n -> p k n", p=128), wm_.tr, writes=[wm_.tr])
                        for tt in range(NTC):
                            r0 = t0 + tt * 128
                            a_ = n2("pm", pm_)
                            S.op("pe", [I("matmul", out=a_.t[:], lhsT=yT.t[:, k, tt * 128:(tt + 1) * 128],
                                                                                      rhs=wm_.t[:, k, :], start=(k == 0), stop=(k == KT - 1))
                                        for k in range(KT)], reads=[yT_tr[tt], wm_.tr], writes=[a_.tr])
                            x_ = n2("xr", xr)
                            S.dma("sp", x_.t[:], x_in[r0:r0 + 128, cb * 512:(cb + 1) * 512], x_.tr, writes=[x_.tr])
                            S.op("dve", I("tensor_tensor", out=x_.t[:], in0=a_.t[:], in1=x_.t[:], op=ALU.add),
                                 reads=[a_.tr, x_.tr], writes=[x_.tr])
                            S.dma("sp", d_xn[r0:r0 + 128, cb * 512:(cb + 1) * 512], x_.t[:], x_.tr, reads=[x_.tr])
                S.barrier()
                S.emit()
            if last:
                with contextlib.ExitStack() as st:
                    fnbc = sb(st, "fnbc", [128, D], F32)
                    S.dma("sp", fnbc.t[:], fnw_in.partition_broadcast(128), fnbc.tr, writes=[fnbc.tr])
                    xf = [sb(st, "xf%d" % i, [128, D], F32) for i in range(2)]
                    jf = sb(st, "jf", [128, D], BF16)
                    sf = [sb(st, "sf%d" % i, [128, 1], F32) for i in range(2)]
                    for tt in range(NT):
                        x_, s_ = xf[tt % 2], sf[tt % 2]
                        S.dma("sp", x_.t[:], d_xn[tt * 128:(tt + 1) * 128, :], x_.tr, writes=[x_.tr])
                        S.op("act", I("activation", out=jf.t[:], in_=x_.t[:], func=AF.Square, accum_out=s_.t[:]),
                             reads=[x_.tr], writes=[jf.tr, s_.tr])
                        S.op("act", I("activation", out=s_.t[:], in_=s_.t[:], func=AF.Ln, scale=1.0 / D, bias=EPS), reads=[s_.tr], writes=[s_.tr])
                        S.op("act", I("activation", out=s_.t[:], in_=s_.t[:], func=AF.Exp, scale=-0.5), reads=[s_.tr], writes=[s_.tr])
                        S.op("dve", I("scalar_tensor_tensor", out=x_.t[:], in0=x_.t[:], scalar=s_.t[:, 0:1], in1=fnbc.t[:],
                                                                                   op0=ALU.mult, op1=ALU.mult), reads=[x_.tr, s_.tr, fnbc.tr], writes=[x_.tr])
                        S.dma("sp", y_out[tt * 128:(tt + 1) * 128, :], x_.t[:], x_.tr, reads=[x_.tr])
                    S.barrier()
                    S.emit()
    return nc


_PROGS = {}


def _prog(T, mode, last, ncores):
    key = (T, mode, last, ncores)
    if key not in _PROGS:
        _PROGS[key] = build(T, mode, last, ncores=ncores)
    return _PROGS[key]


def _layer_inputs(x_full, positions, layer, params, T, ncores, consts):
    maps = []
    S_ = x_full.shape[0]
    for i in range(ncores):
        t0 = i * T
        xh = np.zeros((T + 4, D), np.float32)
        xh[0:T] = x_full[t0:t0 + T]
        if t0 >= 2:
            xh[T:T + 2] = x_full[t0 - 2:t0]
        if t0 + T + 2 <= S_:
            xh[T + 2:T + 4] = x_full[t0 + T:t0 + T + 2]
        m = {"x": xh,
             "pos": np.ascontiguousarray(positions[t0:t0 + T].reshape(T // 128, 128).T.astype(np.int32)),
             "w_in": params["w_in"][layer],
             "norm_w": params["norm_w"][layer].reshape(1, D),
             "b_mgate": params["b_mgate"][layer].reshape(1, 16),
             "conv_w": np.ascontiguousarray(params["conv_w"][layer].reshape(5, 16, 128).transpose(2, 1, 0)),
             "conv_b": np.ascontiguousarray(params["conv_b"][layer].reshape(16, 128).T)}
        m.update(consts)
        maps.append(m)
    return maps


def run_layers(x_full, positions, params, T, ncores, depth):
    consts = host_consts(T)
    x_cur = np.ascontiguousarray(x_full, dtype=np.float32)
    for layer in range(depth):
        last = layer == depth - 1
        maps = _layer_inputs(x_cur, positions, layer, params, T, ncores, consts)
        w_full = params["w_in"][layer]
        w_p1 = np.ascontiguousarray(np.concatenate(
            [w_full[:, O_KM:O_KM + 1024], w_full[:, O_VM:O_VM + 2048], w_full[:, O_GM:O_GM + 16],
             w_full[:, O_KR:O_KR + 1024], w_full[:, O_VR:O_VR + 2048]], axis=1))
        maps1 = [dict(m, w_in=w_p1) for m in maps]
        r1 = run_bass_kernel_spmd(_prog(T, "p1", False, ncores), maps1, core_ids=list(range(ncores))).results
        gFc = np.stack([r["oFc"] for r in r1])
        gFn = np.stack([r["oFn"] for r in r1])
        gFs = np.stack([r["oFs"] for r in r1])
        gG = np.stack([r["oG"] for r in r1])
        for i, m in enumerate(maps):
            m.update({"m_norm_w": params["m_norm_w"][layer].reshape(1, D),
                      "r_norm_w": params["r_norm_w"][layer].reshape(1, D),
                      "w_proj_m": params["w_proj_m"][layer], "w_proj_r": params["w_proj_r"][layer],
                      "w_out": params["w_out"][layer], "b_mix": params["b_mix"][layer].reshape(1, 2 * D),
                      "final_norm_w": params["final_norm_w"].reshape(1, D),
                      "cmask": core_masks(i)[:, :, :ncores].copy(),
                      "gFc": gFc, "gFn": gFn, "gFs": gFs, "gG": gG})
        r2 = run_bass_kernel_spmd(_prog(T, "p2", last, ncores), maps, core_ids=list(range(ncores))).results
        x_cur = np.concatenate([r["y"] for r in r2], axis=0)
    return x_cur


def kernel(x, positions, norm_w, w_in, b_mgate, conv_w, conv_b, m_norm_w, r_norm_w,
           w_proj_m, w_proj_r, b_mix, w_out, final_norm_w):
    params = {k: np.asarray(v) for k, v in dict(
        norm_w=norm_w, w_in=w_in, b_mgate=b_mgate, conv_w=conv_w, conv_b=conv_b, m_norm_w=m_norm_w,
        r_norm_w=r_norm_w, w_proj_m=w_proj_m, w_proj_r=w_proj_r, b_mix=b_mix, w_out=w_out,
        final_norm_w=final_norm_w).items()}
    x = np.asarray(x)
    Sq = x.shape[1]
    T = Sq // NCORES
    out = run_layers(x[0], np.asarray(positions)[0], params, T, NCORES, w_in.shape[0])
    return out.reshape(1, Sq, D).astype(np.float32)
```

```python
import contextlib
import numpy as np
import concourse.bass as bass
import concourse.mybir as mybir
from concourse.bass_utils import run_bass_kernel_spmd

F32 = mybir.dt.float32
BF16 = mybir.dt.bfloat16
I32 = mybir.dt.int32
AF = mybir.ActivationFunctionType
ALU = mybir.AluOpType
AX = mybir.AxisListType

NCORES = 8
IW = 2
D = 2048
KT = 16
DIN = 18448
P1_COLS = 6160
EPS = 1e-6
O_QM, O_KM, O_VM, O_ZM, O_OM, O_GM = 0, 1024, 2048, 4096, 6144, 8192
O_QR, O_KR, O_VR, O_ZR, O_GTM, O_GTR = 8208, 9232, 10256, 12304, 14352, 16400
GAM = [1.0 - 2.0 ** (-5 - h) for h in range(8)]
PI = float(np.pi)


class Tr:
    __slots__ = ("name", "lw", "rd", "dsem", "dcnt")

    def __init__(self, name=""):
        self.name = name
        self.lw = None
        self.rd = {}
        self.dsem = None
        self.dcnt = 0


class Sched:
    ENG = ("pe", "act", "dve", "pool", "sp")

    def __init__(self, nc, stack):
        self.nc = nc
        self.stack = stack
        self.sems = {}
        self.final = {}
        self.cnt = {}
        self.waited = {e: {} for e in self.ENG}
        self.prog = {e: [] for e in self.ENG}
        self.nsem = 0
        self.dpool = []
        for e in self.ENG:
            self._mksem("E_" + e)
            self.cnt[e] = 0

    def _mksem(self, key):
        self.sems[key] = self.stack.enter_context(self.nc.semaphore(key))
        self.final[key] = 0
        self.nsem += 1
        return key

    def _deps(self, reads, writes):
        deps = {}
        for r in reads:
            if r.lw is not None and deps.get(r.lw[0], 0) < r.lw[1]:
                deps[r.lw[0]] = r.lw[1]
        for w in writes:
            if w.lw is not None and deps.get(w.lw[0], 0) < w.lw[1]:
                deps[w.lw[0]] = w.lw[1]
            for k, v in w.rd.items():
                if deps.get(k, 0) < v:
                    deps[k] = v
        return deps

    def _emit_waits(self, eng, deps):
        wd = self.waited[eng]
        for k, v in deps.items():
            if wd.get(k, 0) >= v:
                continue
            wd[k] = v
            sem = self.sems[k]
            self.prog[eng].append(lambda e, sem=sem, v=v: e.wait_ge(sem, v))

    def _record(self, tok, reads, writes):
        k, v = tok
        self.final[k] = max(self.final[k], v)
        for r in reads:
            if r.rd.get(k, 0) < v:
                r.rd[k] = v
        for w in writes:
            w.lw = tok
            w.rd = {}

    def op(self, eng, fns, reads=(), writes=()):
        if isinstance(fns, tuple):
            fns = [fns]
        fns = [(lambda e, f=f: getattr(e, f[0])(*f[1], **f[2])) for f in fns]
        self._emit_waits(eng, self._deps(reads, writes))
        self.cnt[eng] += 1
        v = self.cnt[eng]
        sem = self.sems["E_" + eng]
        n = len(fns)
        for i, fn in enumerate(fns):
            if i == n - 1:
                self.prog[eng].append(lambda e, fn=fn, sem=sem: fn(e).then_inc(sem, 1))
            else:
                self.prog[eng].append(lambda e, fn=fn: fn(e))
        self._record(("E_" + eng, v), reads, writes)

    def dma(self, q, out_ap, in_ap, owner, reads=(), writes=(), **kw):
        if owner.dsem is None:
            if self.dpool:
                owner.dsem, owner.dcnt = self.dpool.pop()
            else:
                owner.dsem = self._mksem("D%d" % self.nsem)
        deps = self._deps(reads, writes)
        if owner.dcnt > 0 and deps.get(owner.dsem, 0) < owner.dcnt * 16:
            deps[owner.dsem] = owner.dcnt * 16
        self._emit_waits(q, deps)
        owner.dcnt += 1
        v = owner.dcnt * 16
        sem = self.sems[owner.dsem]
        self.prog[q].append(
            lambda e, o=out_ap, i=in_ap, sem=sem, kw=kw: e.dma_start(out=o, in_=i, **kw).then_inc(sem, 16))
        self._record((owner.dsem, v), reads, writes)

    def release(self, trs):
        for t in trs:
            if t.dsem is not None:
                self.dpool.append((t.dsem, t.dcnt))
                t.dsem = None

    def barrier(self):
        for e in self.ENG:
            self._emit_waits(e, {k: v for k, v in self.final.items() if v > 0})

    def emit(self):
        nc = self.nc
        progs = self.prog
        self.prog = {e: [] for e in self.ENG}
        with nc.Block() as block:
            @block.tensor
            def _(e):
                for t in progs["pe"]:
                    t(e)

            @block.scalar
            def _(e):
                for t in progs["act"]:
                    t(e)

            @block.vector
            def _(e):
                for t in progs["dve"]:
                    t(e)

            @block.gpsimd
            def _(e):
                for t in progs["pool"]:
                    t(e)

            @block.sync
            def _(e):
                for t in progs["sp"]:
                    t(e)


def I(name, *args, **kw):
    return (name, args, kw)


class B:
    __slots__ = ("t", "tr")

    def __init__(self, t, name):
        self.t = t
        self.tr = Tr(name)


def host_consts(T):
    c = {}
    a = np.arange(128)
    c["c_ident"] = np.eye(128, dtype=np.float32)
    mu = (a[:, None] <= a[None, :]).astype(np.float32)
    ml = (a[:, None] >= a[None, :]).astype(np.float32)
    c["c_mask"] = np.stack([mu, ml, np.ones((128, 128), np.float32)], axis=1)
    gf = np.zeros((128, 8, 128), np.float32)
    gb = np.zeros((128, 8, 128), np.float32)
    dec = np.zeros((128, 4, 8), np.float32)
    for h in range(8):
        lg = np.log1p(-(2.0 ** (-5.0 - h)))
        diff = (a[None, :] - a[:, None]).astype(np.float64)
        gf[:, h, :] = np.where(diff >= 0, np.exp(np.where(diff >= 0, diff, 0) * lg), 0.0)
        gb[:, h, :] = np.where(diff < 0, np.exp(np.where(diff < 0, -diff, 0) * lg), 0.0)
        dec[:, 0, h] = np.exp((a + 1.0) * lg)
        dec[:, 1, h] = np.exp((128.0 - a) * lg)
        dec[:, 2, h] = np.exp((127.0 - a) * lg)
        dec[:, 3, h] = np.exp((a + 0.0) * lg)
    c["c_gf"] = gf
    c["c_gb"] = gb
    c["c_dec"] = dec
    invf = np.power(np.float32(10000.0), -np.arange(64, dtype=np.float32) / np.float32(64.0)).astype(np.float32)
    c["c_invf"] = np.broadcast_to(invf[None, :], (128, 64)).copy()
    return c


def core_masks(i):
    m = np.zeros((128, 2, NCORES), np.float32)
    for j in range(NCORES):
        m[:, 0, j] = 1.0 if j < i else 0.0
        m[:, 1, j] = 1.0 if j > i else 0.0
    return m


def build(T, mode, last, debug=False, ncores=NCORES):
    NT = T // 128
    TH = T + 4
    nc = bass.Bass("TRN2", target_bir_lowering=False)
    dbg_kind = "ExternalOutput" if debug else "Internal"

    def din(name, shape, dt=F32):
        return nc.dram_tensor(name, list(shape), dt, kind="ExternalInput").ap()

    def dscr(name, shape, dt, kind=None):
        return nc.dram_tensor(name, list(shape), dt, kind=kind or dbg_kind).ap()

    x_in = din("x", [TH, D])
    pos_in = din("pos", [128, NT], I32)
    if mode == "p1":
        O_KM_, O_VM_, O_GM_, O_KR_, O_VR_ = 0, 1024, 3072, 3088, 4112
        w_in = din("w_in", [D, P1_COLS])
    else:
        O_KM_, O_VM_, O_GM_, O_KR_, O_VR_ = O_KM, O_VM, O_GM, O_KR, O_VR
        w_in = din("w_in", [D, DIN])
    nw_in = din("norm_w", [1, D])
    bg_in = din("b_mgate", [1, 16])
    cw_in = din("conv_w", [128, 16, 5])
    cb_in = din("conv_b", [128, 16])
    c_ident = din("c_ident", [128, 128])
    c_mask = din("c_mask", [128, 3, 128])
    c_gf = din("c_gf", [128, 8, 128])
    c_gb = din("c_gb", [128, 8, 128])
    c_dec = din("c_dec", [128, 4, 8])
    c_invf = din("c_invf", [128, 64])
    if mode == "p2":
        mnw_in = din("m_norm_w", [1, D])
        rnw_in = din("r_norm_w", [1, D])
        wpm_in = din("w_proj_m", [D, D])
        wpr_in = din("w_proj_r", [D, D])
        wo_in = din("w_out", [D, D])
        bmix_in = din("b_mix", [1, 2 * D])
        fnw_in = din("final_norm_w", [1, D])
        cm_in = din("cmask", [128, 2, ncores])
        gFc = din("gFc", [ncores, 2, 4, 128, 1024])
        gFn = din("gFn", [ncores, 2, 128, 8])
        gFs = din("gFs", [ncores, 2, 8, 128, 256])
        gG = din("gG", [ncores, 2, 128, 4])
        y_out = dscr("y", [T, D], F32, kind="ExternalOutput")
    else:
        oFc = dscr("oFc", [2, 4, 128, 1024], F32, kind="ExternalOutput")
        oFn = dscr("oFn", [2, 128, 8], F32, kind="ExternalOutput")
        oFs = dscr("oFs", [2, 8, 128, 256], F32, kind="ExternalOutput")
        oG = dscr("oG", [2, 128, 4], F32, kind="ExternalOutput")

    full = mode == "p2"
    d_kTm = dscr("d_kTm", [1024, T], BF16) if full else None
    d_qTm = dscr("d_qTm", [1024, T], BF16) if full else None
    d_km = dscr("d_km", [T, 1024], BF16)
    d_vm = dscr("d_vm", [T, 2048], BF16)
    d_g = dscr("d_g", [T, 16], F32)
    d_kr = dscr("d_kr", [T, 1024], BF16)
    d_vr = dscr("d_vr", [T, 2048], BF16)
    if full:
        d_szm = dscr("d_szm", [T, 2048], BF16)
        d_som = dscr("d_som", [T, 2048], BF16)
        d_qTr = dscr("d_qTr", [1024, T], BF16)
        d_kTr = dscr("d_kTr", [1024, T], BF16)
        d_szr = dscr("d_szr", [T, 2048], BF16)
        d_mix = dscr("d_mix", [T, 4096], BF16)
        d_ofm = dscr("d_ofm", [T, 2048], F32)
        d_ofr = dscr("d_ofr", [T, 2048], F32)
        d_uTm = dscr("d_uTm", [2048, T], BF16)
        d_uTr = dscr("d_uTr", [2048, T], BF16)
        d_xn = dscr("d_xn", [T, D], F32) if last else y_out

    with contextlib.ExitStack() as top:
        S = Sched(nc, top)

        def sb(st, name, shape, dt):
            return B(st.enter_context(nc.sbuf_tensor(name, list(shape), dt)), name)

        def ps(st, name, shape, dt):
            return B(st.enter_context(nc.psum_tensor(name, list(shape), dt)), name)

        identb = sb(top, "identb", [128, 128], BF16)
        mask = sb(top, "mask", [128, 3, 128], F32)
        dec = sb(top, "dec", [128, 4, 8], F32)
        onesb = sb(top, "onesb", [128, 128], BF16)
        S.dma("pool", identb.t[:], c_ident[:, :], identb.tr, writes=[identb.tr])
        S.dma("sp", mask.t[:], c_mask[:, :, :], mask.tr, writes=[mask.tr])
        S.dma("sp", dec.t[:], c_dec[:, :, :], dec.tr, writes=[dec.tr])
        S.op("pool", I("memset", onesb.t[:], 1.0), writes=[onesb.tr])

        with contextlib.ExitStack() as st:
            hT = sb(st, "hT", [128, KT, TH], BF16)
            hT_tr = [Tr("hT%d" % i) for i in range(NT + 1)]
            nwbc = sb(st, "nwbc", [128, D], F32)
            S.dma("sp", nwbc.t[:], nw_in.partition_broadcast(128), nwbc.tr, writes=[nwbc.tr])
            with contextlib.ExitStack() as st1:
                xt = [sb(st1, "xt%d" % i, [128, D], F32) for i in range(2)]
                hb = [sb(st1, "hb%d" % i, [128, D], BF16) for i in range(2)]
                junk = sb(st1, "junk", [128, D], BF16)
                ss = [sb(st1, "ss%d" % i, [128, 1], F32) for i in range(2)]
                pT = [ps(st1, "pT%d" % i, [128, 8, 128], BF16) for i in range(2)]
                for tt in range(NT + 1):
                    b = tt % 2
                    np_ = 128 if tt < NT else 4
                    r0 = tt * 128
                    x_, h_, s_ = xt[b], hb[b], ss[b]
                    S.dma("sp", x_.t[0:np_, :], x_in[r0:r0 + np_, :], x_.tr, writes=[x_.tr])
                    S.op("act", I("activation",
                        out=junk.t[0:np_, :], in_=x_.t[0:np_, :], func=AF.Square, accum_out=s_.t[0:np_, :]),
                        reads=[x_.tr], writes=[junk.tr, s_.tr])
                    S.op("act", I("activation",
                        out=s_.t[0:np_, :], in_=s_.t[0:np_, :], func=AF.Ln, scale=1.0 / D, bias=EPS),
                        reads=[s_.tr], writes=[s_.tr])
                    S.op("act", I("activation",
                        out=s_.t[0:np_, :], in_=s_.t[0:np_, :], func=AF.Exp, scale=-0.5),
                        reads=[s_.tr], writes=[s_.tr])
                    S.op("dve", I("scalar_tensor_tensor",
                        out=h_.t[0:np_, :], in0=x_.t[0:np_, :], scalar=s_.t[0:np_, 0:1], in1=nwbc.t[0:np_, :],
                        op0=ALU.mult, op1=ALU.mult), reads=[x_.tr, s_.tr, nwbc.tr], writes=[h_.tr])
                    for g in range(2):
                        p_ = pT[g]
                        S.op("pe", [I("transpose",
                            out=p_.t[:, k, 0:np_], in_=h_.t[0:np_, (g * 8 + k) * 128:(g * 8 + k + 1) * 128],
                            identity=identb.t[0:np_, 0:np_]) for k in range(8)],
                            reads=[h_.tr, identb.tr], writes=[p_.tr])
                        S.op("act" if g == 0 else "dve", I("copy" if g == 0 else "tensor_copy",
                            out=hT.t[:, g * 8:(g + 1) * 8, r0:r0 + np_], in_=p_.t[:, :, 0:np_]),
                            reads=[p_.tr], writes=[hT_tr[tt]])
                S.barrier()
                S.emit()
                S.release([b_.tr for b_ in xt])

            with contextlib.ExitStack() as st2:
                wb = [sb(st2, "wb%d" % i, [128, KT, 512], BF16) for i in range(2)]
                pm = [ps(st2, "pm%d" % i, [128, 512], F32) for i in range(3)]
                pq = [ps(st2, "pq%d" % i, [128, 4, 128], BF16) for i in range(2)]
                ev = [sb(st2, "ev%d" % i, [128, 512], F32) for i in range(3)]
                ob = [sb(st2, "ob%d" % i, [128, 512], BF16) for i in range(3)]
                ot = [sb(st2, "ot%d" % i, [128, 4, 128], BF16) for i in range(2)]
                bgbc = sb(st2, "bgbc", [128, 16], F32)
                S.dma("sp", bgbc.t[:], bg_in.partition_broadcast(128), bgbc.tr, writes=[bgbc.tr])
                cnt = {"w": 0, "pm": 0, "ev": 0, "ob": 0, "pq": 0, "ot": 0}

                def nxt(key, lst):
                    i = cnt[key] % len(lst)
                    cnt[key] += 1
                    return lst[i]

                def load_w(wdram, c0, ncol):
                    w_ = nxt("w", wb)
                    S.dma("pool", w_.t[:, :, 0:ncol], wdram[:, c0:c0 + ncol].rearrange("(k p) n -> p k n", p=128),
                          w_.tr, writes=[w_.tr])
                    return w_

                def gemm_tok(w_, ncol, tt, extra=None):
                    p_ = nxt("pm", pm)
                    fns = [I("matmul",
                        out=p_.t[:, 0:ncol], lhsT=hT.t[:, k, tt * 128:(tt + 1) * 128], rhs=w_.t[:, k, 0:ncol],
                        start=(k == 0), stop=(k == KT - 1 and extra is None)) for k in range(KT)]
                    rds = [hT_tr[tt], w_.tr]
                    if extra is not None:
                        lhsT_ap, rhs_ap, etr = extra
                        fns.append(I("matmul", out=p_.t[:, 0:ncol], lhsT=lhsT_ap, rhs=rhs_ap,
                                                             start=False, stop=True))
                        rds.append(etr)
                    S.op("pe", fns, reads=rds, writes=[p_.tr])
                    return p_

                def store_tok(o_, dst, tt, c0, ncol):
                    S.dma("sp", dst[tt * 128:(tt + 1) * 128, c0:c0 + ncol], o_.t[:, 0:ncol], o_.tr, reads=[o_.tr])

                def seg_act(wdram, wc0, func, dst, ncols, bias_row=None):
                    for cb in range(ncols // 512):
                        w_ = load_w(wdram, wc0 + cb * 512, 512)
                        for tt in range(NT):
                            extra = None
                            if bias_row is not None:
                                extra = (onesb.t[0:1, 0:128], bias_row.t[0:1, cb * 512:(cb + 1) * 512], bias_row.tr)
                            p_ = gemm_tok(w_, 512, tt, extra)
                            o_ = nxt("ob", ob)
                            if func is None:
                                S.op("dve", I("tensor_copy", out=o_.t[:], in_=p_.t[:]),
                                     reads=[p_.tr], writes=[o_.tr])
                            else:
                                S.op("act", I("activation", out=o_.t[:], in_=p_.t[:], func=func),
                                     reads=[p_.tr], writes=[o_.tr])
                            store_tok(o_, dst, tt, cb * 512, 512)

                with contextlib.ExitStack() as st3:
                    cw = sb(st3, "cw", [128, 16, 5], F32)
                    cbias = sb(st3, "cbias", [128, 16], F32)
                    S.dma("sp", cw.t[:], cw_in[:, :, :], cw.tr, writes=[cw.tr])
                    S.dma("sp", cbias.t[:], cb_in[:, :], cbias.tr, writes=[cbias.tr])
                    crow = [sb(st3, "crow%d" % i, [128, TH], F32) for i in range(2)]
                    cacc = [sb(st3, "cacc%d" % i, [128, T], F32) for i in range(2)]
                    cq = [sb(st3, "cq%d" % i, [128, T], BF16) for i in range(2)]
                    blocks = list(range(16)) if full else list(range(8, 16))
                    wcur = None
                    for bi, fb in enumerate(blocks):
                        if fb % 4 == 0 or wcur is None:
                            wcur = load_w(w_in, (fb // 4) * 512 if full else O_KM_ + ((fb - 8) // 4) * 512, 512)
                        j = fb % 4
                        cr, ca, cq_ = crow[bi % 2], cacc[bi % 2], cq[bi % 2]
                        groups = [(g0, min(512, T - g0)) for g0 in range(0, T, 512)] + [(T, 4)]
                        for (g0, gn) in groups:
                            p_ = nxt("pm", pm)
                            tts = sorted(set(min(t // 128, NT) for t in range(g0, g0 + gn, 128)) | ({NT} if g0 == T else set()))
                            S.op("pe", [I("matmul",
                                out=p_.t[:, 0:gn], lhsT=wcur.t[:, k, j * 128:(j + 1) * 128], rhs=hT.t[:, k, g0:g0 + gn],
                                start=(k == 0), stop=(k == KT - 1)) for k in range(KT)],
                                reads=[wcur.tr] + [hT_tr[t] for t in tts], writes=[p_.tr])
                            if g0 < T:
                                S.op("act", I("copy",
                                    out=cr.t[:, 2 + g0:2 + g0 + gn], in_=p_.t[:, 0:gn]), reads=[p_.tr], writes=[cr.tr])
                            else:
                                S.op("act", I("copy", out=cr.t[:, 0:2], in_=p_.t[:, 0:2]),
                                     reads=[p_.tr], writes=[cr.tr])
                                S.op("act", I("copy", out=cr.t[:, T + 2:T + 4], in_=p_.t[:, 2:4]),
                                     reads=[p_.tr], writes=[cr.tr])
                        S.op("dve", I("tensor_scalar",
                            out=ca.t[:], in0=cr.t[:, 0:T], scalar1=cw.t[:, fb, 0:1], scalar2=cbias.t[:, fb:fb + 1],
                            op0=ALU.mult, op1=ALU.add), reads=[cr.tr, cw.tr, cbias.tr], writes=[ca.tr])
                        for tap in range(1, 5):
                            S.op("dve", I("scalar_tensor_tensor",
                                out=ca.t[:], in0=cr.t[:, tap:tap + T], scalar=cw.t[:, fb, tap:tap + 1], in1=ca.t[:],
                                op0=ALU.mult, op1=ALU.add), reads=[cr.tr, cw.tr, ca.tr], writes=[ca.tr])
                        if fb < 8:
                            S.op("act", I("activation", out=ca.t[:], in_=ca.t[:], func=AF.Silu),
                                 reads=[ca.tr], writes=[ca.tr])
                            S.op("act", I("activation", out=cq_.t[:], in_=ca.t[:], func=AF.Copy, scale=1.0 / 16.0),
                                reads=[ca.tr], writes=[cq_.tr])
                            S.dma("sp", d_qTm[fb * 128:(fb + 1) * 128, :], cq_.t[:], cq_.tr, reads=[cq_.tr])
                        else:
                            S.op("act", I("activation", out=cq_.t[:], in_=ca.t[:], func=AF.Silu),
                                 reads=[ca.tr], writes=[cq_.tr])
                            fk = fb - 8
                            if full:
                                S.dma("sp", d_kTm[fk * 128:(fk + 1) * 128, :], cq_.t[:], cq_.tr, reads=[cq_.tr])
                            for t4 in range(0, NT, 4):
                                nn = min(4, NT - t4)
                                q_ = nxt("pq", pq)
                                o_ = nxt("ot", ot)
                                S.op("pe", [I("transpose",
                                    out=q_.t[:, i, :], in_=cq_.t[:, (t4 + i) * 128:(t4 + i + 1) * 128], identity=identb.t[:])
                                    for i in range(nn)], reads=[cq_.tr, identb.tr], writes=[q_.tr])
                                S.op("dve", I("tensor_copy", out=o_.t[:, 0:nn, :], in_=q_.t[:, 0:nn, :]),
                                     reads=[q_.tr], writes=[o_.tr])
                                S.dma("sp", d_km[t4 * 128:(t4 + nn) * 128, fk * 128:(fk + 1) * 128].rearrange(
                                    "(i p) c -> p i c", p=128), o_.t[:, 0:nn, :], o_.tr, reads=[o_.tr])
                    S.barrier()
                    S.emit()
                    S.release([b_.tr for b_ in cq] + [cw.tr, cbias.tr])

                seg_act(w_in, O_VM_, None, d_vm, 2048)
                seg_act(w_in, O_VR_, None, d_vr, 2048)
                if full:
                    seg_act(w_in, O_ZM, AF.Silu, d_szm, 2048)
                    seg_act(w_in, O_OM, AF.Sigmoid, d_som, 2048)
                    seg_act(w_in, O_ZR, AF.Silu, d_szr, 2048)
                    bmixb = sb(st2, "bmixb", [1, 2 * D], BF16)
                    S.dma("pool", bmixb.t[:], bmix_in[:, :], bmixb.tr, writes=[bmixb.tr])
                    seg_act(w_in, O_GTM, AF.Sigmoid, d_mix, 4096, bias_row=bmixb)
                w_ = load_w(w_in, O_GM_, 16)
                gt = [sb(st2, "gt%d" % i, [128, 16], F32) for i in range(2)]
                ge = [sb(st2, "ge%d" % i, [128, 16], F32) for i in range(2)]
                for tt in range(NT):
                    p_ = gemm_tok(w_, 16, tt)
                    g_, e_ = gt[tt % 2], ge[tt % 2]
                    S.op("dve", I("tensor_tensor", out=g_.t[:], in0=p_.t[:, 0:16], in1=bgbc.t[:], op=ALU.add),
                         reads=[p_.tr, bgbc.tr], writes=[g_.tr])
                    S.op("act", I("activation", out=e_.t[:], in_=g_.t[:], func=AF.Exp, scale=-1.0),
                         reads=[g_.tr], writes=[e_.tr])
                    S.op("act", I("activation", out=e_.t[:], in_=e_.t[:], func=AF.Ln, bias=1.0),
                         reads=[e_.tr], writes=[e_.tr])
                    for c0 in (4, 12):
                        S.op("dve", I("tensor_scalar",
                            out=g_.t[:, c0:c0 + 4], in0=e_.t[:, c0:c0 + 4], scalar1=-1.0, scalar2=None, op0=ALU.mult),
                            reads=[e_.tr], writes=[g_.tr])
                    S.dma("sp", d_g[tt * 128:(tt + 1) * 128, :], g_.t[:], g_.tr, reads=[g_.tr])

                with contextlib.ExitStack() as st3:
                    posi = sb(st3, "posi", [128, NT], I32)
                    posf = sb(st3, "posf", [128, NT], F32)
                    invf = sb(st3, "invf", [128, 64], F32)
                    ang = sb(st3, "ang", [128, NT, 64], F32)
                    kf = sb(st3, "kf", [128, NT, 64], F32)
                    ki = sb(st3, "ki", [128, NT, 64], I32)
                    rr = sb(st3, "rr", [128, NT, 64], F32)
                    tmp = sb(st3, "tmpang", [128, NT, 64], F32)
                    msk = sb(st3, "mskang", [128, NT, 64], F32)
                    tab = sb(st3, "tab", [128, 4, NT, 64], F32)
                    S.dma("sp", posi.t[:], pos_in[:, :], posi.tr, writes=[posi.tr])
                    S.dma("sp", invf.t[:], c_invf[:, :], invf.tr, writes=[invf.tr])
                    S.op("dve", I("tensor_copy", out=posf.t[:], in_=posi.t[:]), reads=[posi.tr], writes=[posf.tr])
                    for t in range(NT):
                        S.op("dve", I("tensor_scalar", out=ang.t[:, t, :], in0=invf.t[:], scalar1=posf.t[:, t:t + 1],
                                                                   scalar2=None, op0=ALU.mult),
                             reads=[invf.tr, posf.tr], writes=[ang.tr])
                    S.op("dve", I("tensor_scalar", out=kf.t[:], in0=ang.t[:], scalar1=float(1.0 / (2 * np.pi)),
                                                          scalar2=None, op0=ALU.mult), reads=[ang.tr], writes=[kf.tr])
                    S.op("dve", I("tensor_copy", out=ki.t[:], in_=kf.t[:]), reads=[kf.tr], writes=[ki.tr])
                    S.op("dve", I("tensor_copy", out=kf.t[:], in_=ki.t[:]), reads=[ki.tr], writes=[kf.tr])
                    C1 = 6.28125
                    C2 = float(2 * np.pi - 6.28125)
                    S.op("dve", I("scalar_tensor_tensor", out=rr.t[:], in0=kf.t[:], scalar=-C1, in1=ang.t[:],
                                                                 op0=ALU.mult, op1=ALU.add), reads=[kf.tr, ang.tr], writes=[rr.tr])
                    S.op("dve", I("scalar_tensor_tensor", out=rr.t[:], in0=kf.t[:], scalar=-C2, in1=rr.t[:],
                                                                 op0=ALU.mult, op1=ALU.add), reads=[kf.tr, rr.tr], writes=[rr.tr])
                    for which, shift in ((1, 0.0), (0, PI / 2)):
                        S.op("dve", I("tensor_scalar", out=tmp.t[:], in0=rr.t[:], scalar1=shift, scalar2=None,
                                                                           op0=ALU.add), reads=[rr.tr], writes=[tmp.tr])
                        for (cmp_, thr, adj) in ((ALU.is_gt, PI, -2 * PI), (ALU.is_lt, -PI, 2 * PI)):
                            S.op("dve", I("tensor_scalar",
                                out=msk.t[:], in0=tmp.t[:], scalar1=thr, scalar2=None, op0=cmp_), reads=[tmp.tr], writes=[msk.tr])
                            S.op("dve", I("scalar_tensor_tensor",
                                out=tmp.t[:], in0=msk.t[:], scalar=adj, in1=tmp.t[:], op0=ALU.mult, op1=ALU.add),
                                reads=[msk.tr, tmp.tr], writes=[tmp.tr])
                        S.op("dve", I("tensor_scalar", out=tmp.t[:], in0=tmp.t[:], scalar1=PI, scalar2=-PI,
                                                              op0=ALU.min, op1=ALU.max), reads=[tmp.tr], writes=[tmp.tr])
                        S.op("act", I("activation", out=tab.t[:, which, :, :], in_=tmp.t[:], func=AF.Sin),
                             reads=[tmp.tr], writes=[tab.tr])
                    S.op("dve", I("tensor_scalar", out=tab.t[:, 2:4, :, :], in0=tab.t[:, 0:2, :, :],
                                                          scalar1=float(128.0 ** -0.5), scalar2=None, op0=ALU.mult),
                         reads=[tab.tr], writes=[tab.tr])
                    ra = [sb(st3, "ra%d" % i, [128, 4, 2, 64], F32) for i in range(2)]
                    rb = [sb(st3, "rb%d" % i, [128, 4, 2, 64], F32) for i in range(2)]
                    segs = ([("q", O_QR)] if full else []) + [("k", O_KR_)]
                    ri = 0
                    for (which, wc0) in segs:
                        tb = 2 if which == "q" else 0
                        for cb in range(2):
                            w_ = load_w(w_in, wc0 + cb * 512, 512)
                            for tt in range(NT):
                                p_ = gemm_tok(w_, 512, tt)
                                e_ = nxt("ev", ev)
                                o_ = nxt("ob", ob)
                                a_, b_ = ra[ri % 2], rb[ri % 2]
                                ri += 1
                                S.op("act", I("copy", out=e_.t[:], in_=p_.t[:]), reads=[p_.tr], writes=[e_.tr])
                                evv = e_.t[:].rearrange("p (h two j) -> p h two j", h=4, two=2)
                                ov = o_.t[:].rearrange("p (h two j) -> p h two j", h=4, two=2)
                                cosb = tab.t[:, tb, tt, :].unsqueeze(1).to_broadcast([128, 4, 64])
                                sinb = tab.t[:, tb + 1, tt, :].unsqueeze(1).to_broadcast([128, 4, 64])
                                S.op("dve", I("tensor_tensor",
                                    out=a_.t[:, :, 0, :], in0=evv[:, :, 0, :], in1=cosb, op=ALU.mult), reads=[e_.tr, tab.tr], writes=[a_.tr])
                                S.op("dve", I("tensor_tensor",
                                    out=a_.t[:, :, 1, :], in0=evv[:, :, 1, :], in1=cosb, op=ALU.mult), reads=[e_.tr, tab.tr], writes=[a_.tr])
                                S.op("dve", I("tensor_tensor",
                                    out=b_.t[:, :, 0, :], in0=evv[:, :, 1, :], in1=sinb, op=ALU.mult), reads=[e_.tr, tab.tr], writes=[b_.tr])
                                S.op("dve", I("tensor_tensor",
                                    out=b_.t[:, :, 1, :], in0=evv[:, :, 0, :], in1=sinb, op=ALU.mult), reads=[e_.tr, tab.tr], writes=[b_.tr])
                                S.op("dve", I("tensor_tensor",
                                    out=ov[:, :, 0, :], in0=a_.t[:, :, 0, :], in1=b_.t[:, :, 0, :], op=ALU.subtract),
                                    reads=[a_.tr, b_.tr], writes=[o_.tr])
                                S.op("dve", I("tensor_tensor",
                                    out=ov[:, :, 1, :], in0=a_.t[:, :, 1, :], in1=b_.t[:, :, 1, :], op=ALU.add),
                                    reads=[a_.tr, b_.tr], writes=[o_.tr])
                                if which == "k":
                                    store_tok(o_, d_kr, tt, cb * 512, 512)
                                if full:
                                    q_ = nxt("pq", pq)
                                    t_ = nxt("ot", ot)
                                    S.op("pe", [I("transpose",
                                        out=q_.t[:, i, :], in_=o_.t[:, i * 128:(i + 1) * 128], identity=identb.t[:])
                                        for i in range(4)], reads=[o_.tr, identb.tr], writes=[q_.tr])
                                    S.op("act", I("copy", out=t_.t[:], in_=q_.t[:]), reads=[q_.tr], writes=[t_.tr])
                                    dstT = d_qTr if which == "q" else d_kTr
                                    S.dma("sp", dstT[cb * 512:(cb + 1) * 512, tt * 128:(tt + 1) * 128].rearrange(
                                        "(i p) c -> p i c", p=128), t_.t[:], t_.tr, reads=[t_.tr])
                    S.barrier()
                    S.emit()
                    S.release([posi.tr, invf.tr])
                S.release([b_.tr for b_ in ob + ot + gt + wb] + [bgbc.tr])
            S.release([nwbc.tr])

        with contextlib.ExitStack() as st:
            gamT = [float(g ** T) for g in GAM]
            g128 = [float(g ** 128) for g in GAM]
            Cm = [sb(st, "Cm%d" % h, [128, 2, 512], F32) for h in range(4)]
            Nm = sb(st, "Nm", [128, 8], F32)
            Sr = [sb(st, "Sr%d" % h, [128, 256], F32) for h in range(8)]
            Cmb = [sb(st, "Cmb%d" % h, [128, 2, 512], BF16) for h in range(4)]
            Nmb = sb(st, "Nmb", [128, 8], BF16)
            Srb = [sb(st, "Srb%d" % h, [128, 256], BF16) for h in range(8)]
            Gac = sb(st, "Gac", [128, 4], F32)
            gfT = sb(st, "gfT", [128, 8, 128], F32)
            gbT = sb(st, "gbT", [128, 8, 128], F32)
            S.dma("sp", gfT.t[:], c_gf[:, :, :], gfT.tr, writes=[gfT.tr])
            S.dma("sp", gbT.t[:], c_gb[:, :, :], gbT.tr, writes=[gbT.tr])
            NB = 2
            gch = [sb(st, "gch%d" % i, [128, 16], F32) for i in range(NB)]
            kmc = [sb(st, "kmc%d" % i, [128, 1024], BF16) for i in range(NB)]
            vmc = [sb(st, "vmc%d" % i, [128, 2048], BF16) for i in range(NB)]
            krc = [sb(st, "krc%d" % i, [128, 1024], BF16) for i in range(NB)]
            vrc = [sb(st, "vrc%d" % i, [128, 2048], BF16) for i in range(NB)]
            if full:
                qTmc = [sb(st, "qTmc%d" % i, [128, 8, 128], BF16) for i in range(NB)]
                kTmc = [sb(st, "kTmc%d" % i, [128, 8, 128], BF16) for i in range(NB)]
                qTrc = [sb(st, "qTrc%d" % i, [128, 8, 128], BF16) for i in range(NB)]
                kTrc = [sb(st, "kTrc%d" % i, [128, 8, 128], BF16) for i in range(NB)]
                om = [sb(st, "om%d" % i, [128, 2048], F32) for i in range(2)]
                orr = [sb(st, "or%d" % i, [128, 2048], F32) for i in range(2)]
                AT = [sb(st, "AT%d" % i, [128, 128], BF16) for i in range(2)]
                Vh = [sb(st, "Vh%d" % i, [128, 512], BF16) for i in range(2)]
                p1s = [sb(st, "p1s%d" % i, [128, 256], F32) for i in range(2)]
                ps_n = [ps(st, "ps_n%d" % i, [128, 512], F32) for i in range(2)]
            Kh = [sb(st, "Kh%d" % i, [128, 256], BF16) for i in range(2)]
            sc = [sb(st, "sc%d" % i, [128, 8, 4], F32) for i in range(2)]
            scb = [sb(st, "scb%d" % i, [128, 4], BF16) for i in range(2)]
            rsc = [sb(st, "rsc%d" % i, [128, 4], F32) for i in range(2)]
            ps_c = [ps(st, "ps_c%d" % i, [128, 512], F32) for i in range(4)]
            small = ps(st, "ps_small", [128, 512], F32)
            ps_s = [B(small.t[:, i * 128:(i + 1) * 128], "ps_s%d" % i) for i in range(2)]
            ps_g = [B(small.t[:, 256 + i * 8:256 + (i + 1) * 8], "ps_g%d" % i) for i in range(2)]
            ps_d = [B(small.t[:, 272 + i * 2:272 + (i + 1) * 2], "ps_d%d" % i) for i in range(2)]
            ps_x = [B(small.t[:, 276 + i * 2:276 + (i + 1) * 2], "ps_x%d" % i) for i in range(2)]
            rot = {}

            def nx(key, lst):
                i = rot.get(key, 0)
                rot[key] = i + 1
                return lst[i % len(lst)]

            def interleave(gens, width):
                it = iter(gens)
                active = []
                while True:
                    while len(active) < width:
                        g = next(it, None)
                        if g is None:
                            break
                        active.append(g)
                    if not active:
                        break
                    for g in list(active):
                        try:
                            next(g)
                        except StopIteration:
                            active.remove(g)

            def zero_states():
                for h in range(4):
                    S.op("dve", I("memset", Cm[h].t[:], 0.0), writes=[Cm[h].tr])
                    S.op("dve", I("memset", Cmb[h].t[:], 0.0), writes=[Cmb[h].tr])
                for h in range(8):
                    S.op("dve", I("memset", Sr[h].t[:], 0.0), writes=[Sr[h].tr])
                    S.op("dve", I("memset", Srb[h].t[:], 0.0), writes=[Srb[h].tr])
                S.op("dve", I("memset", Nm.t[:], 0.0), writes=[Nm.tr])
                S.op("dve", I("memset", Nmb.t[:], 0.0), writes=[Nmb.tr])
                S.op("dve", I("memset", Gac.t[:], 0.0), writes=[Gac.tr])

            def refresh_bf():
                for h in range(4):
                    S.op("act", I("copy", out=Cmb[h].t[:], in_=Cm[h].t[:]), reads=[Cm[h].tr], writes=[Cmb[h].tr])
                for h in range(8):
                    S.op("act", I("copy", out=Srb[h].t[:], in_=Sr[h].t[:]), reads=[Sr[h].tr], writes=[Srb[h].tr])
                S.op("act", I("copy", out=Nmb.t[:], in_=Nm.t[:]), reads=[Nm.tr], writes=[Nmb.tr])

            def sweep(d, outputs, post=None):
                order = list(range(NT)) if d == 0 else list(range(NT - 1, -1, -1))
                gT = gfT if d == 0 else gbT
                for ci, c in enumerate(order):
                    r0 = c * 128
                    b = ci % NB
                    g_, km_, vm_, kr_, vr_ = gch[b], kmc[b], vmc[b], krc[b], vrc[b]
                    S.dma("sp", g_.t[:], d_g[r0:r0 + 128, :], g_.tr, writes=[g_.tr])
                    S.dma("sp", km_.t[:], d_km[r0:r0 + 128, :], km_.tr, writes=[km_.tr])
                    S.dma("sp", vm_.t[:], d_vm[r0:r0 + 128, :], vm_.tr, writes=[vm_.tr])
                    S.dma("sp", kr_.t[:], d_kr[r0:r0 + 128, :], kr_.tr, writes=[kr_.tr])
                    S.dma("sp", vr_.t[:], d_vr[r0:r0 + 128, :], vr_.tr, writes=[vr_.tr])
                    if outputs:
                        qTm_, kTm_, qTr_, kTr_ = qTmc[b], kTmc[b], qTrc[b], kTrc[b]
                        for (dst, src) in ((qTm_, d_qTm), (kTm_, d_kTm), (qTr_, d_qTr), (kTr_, d_kTr)):
                            S.dma("sp", dst.t[:], src[:, r0:r0 + 128].rearrange("(f p) t -> p f t", p=128), dst.tr, writes=[dst.tr])
                    s_ = nx("sc", sc)
                    sb_ = nx("scb", scb)
                    pg = nx("psg", ps_g)
                    ic, fc = (0, 4) if d == 0 else (8, 12)
                    S.op("pe", [I("matmul", out=pg.t[:, 0:4], lhsT=mask.t[:, d, :], rhs=g_.t[:, fc:fc + 4],
                                                                        start=True, stop=True),
                                I("matmul", out=pg.t[:, 4:8], lhsT=mask.t[:, 2, :], rhs=g_.t[:, fc:fc + 4],
                                                                        start=True, stop=True)],
                         reads=[mask.tr, g_.tr], writes=[pg.tr])
                    S.op("dve", I("tensor_copy", out=s_.t[:, 0:2, :], in_=pg.t[:].rearrange("p (a b) -> p a b", a=2)),
                         reads=[pg.tr], writes=[s_.tr])
                    S.op("act", I("activation", out=s_.t[:, 2, :], in_=s_.t[:, 0, :], func=AF.Exp), reads=[s_.tr], writes=[s_.tr])
                    S.op("dve", I("tensor_tensor", out=s_.t[:, 6, :], in0=g_.t[:, ic:ic + 4], in1=s_.t[:, 0, :],
                                                                               op=ALU.subtract), reads=[s_.tr, g_.tr], writes=[s_.tr])
                    S.op("act", I("activation", out=s_.t[:, 3, :], in_=s_.t[:, 6, :], func=AF.Exp), reads=[s_.tr], writes=[s_.tr])
                    S.op("dve", I("tensor_tensor", out=s_.t[:, 7, :], in0=s_.t[:, 6, :], in1=s_.t[:, 1, :], op=ALU.add),
                         reads=[s_.tr], writes=[s_.tr])
                    S.op("act", I("activation", out=s_.t[:, 4, :], in_=s_.t[:, 7, :], func=AF.Exp), reads=[s_.tr], writes=[s_.tr])
                    S.op("act", I("activation", out=s_.t[:, 5, :], in_=s_.t[:, 1, :], func=AF.Exp), reads=[s_.tr], writes=[s_.tr])
                    S.op("dve", I("tensor_copy", out=sb_.t[:], in_=s_.t[:, 3, :]), reads=[s_.tr], writes=[sb_.tr])
                    if not outputs:
                        S.op("dve", I("tensor_tensor", out=Gac.t[:], in0=Gac.t[:], in1=s_.t[:, 1, :], op=ALU.add),
                             reads=[s_.tr, Gac.tr], writes=[Gac.tr])
                    if outputs:
                        o_m = nx("om", om)
                        o_r = nx("or", orr)
                    def mh(h):
                        if outputs:
                            pss = nx("pss", ps_s)
                            a_t = nx("AT", AT)
                            vh = nx("Vh", Vh)
                            pn = nx("psn", ps_n)
                            pd = nx("psd", ps_d)
                            r_ = nx("rsc", rsc)
                            S.op("pe", [I("matmul",
                                out=pss.t[:], lhsT=kTm_.t[:, h * 2 + kt, :], rhs=qTm_.t[:, h * 2 + kt, :], start=(kt == 0), stop=(kt == 1))
                                for kt in range(2)], reads=[kTm_.tr, qTm_.tr], writes=[pss.tr])
                            yield
                            S.op("dve", I("tensor_tensor", out=a_t.t[:], in0=pss.t[:], in1=mask.t[:, d, :], op=ALU.mult),
                                 reads=[pss.tr, mask.tr], writes=[a_t.tr])
                            yield
                            S.op("dve", I("tensor_scalar",
                                out=vh.t[:], in0=vm_.t[:, h * 512:(h + 1) * 512], scalar1=s_.t[:, 3, h:h + 1], scalar2=None, op0=ALU.mult),
                                reads=[vm_.tr, s_.tr], writes=[vh.tr])
                            yield
                            S.op("pe", [I("matmul", out=pn.t[:], lhsT=a_t.t[:], rhs=vh.t[:], start=True, stop=False)]
                                 + [I("matmul", out=pn.t[:], lhsT=qTm_.t[:, h * 2 + kt, :], rhs=Cmb[h].t[:, kt, :],
                                                                          start=False, stop=(kt == 1)) for kt in range(2)]
                                 + [I("matmul", out=pd.t[:, 0:1], lhsT=a_t.t[:], rhs=sb_.t[:, h:h + 1],
                                                                                     start=True, stop=False)]
                                 + [I("matmul", out=pd.t[:, 0:1], lhsT=qTm_.t[:, h * 2 + kt, :],
                                                                          rhs=Nmb.t[:, h * 2 + kt:h * 2 + kt + 1], start=False, stop=(kt == 1))
                                    for kt in range(2)],
                                 reads=[a_t.tr, vh.tr, qTm_.tr, Cmb[h].tr, sb_.tr, Nmb.tr], writes=[pn.tr, pd.tr])
                            yield
                            S.op("dve", I("tensor_scalar",
                                out=r_.t[:, 0:1], in0=pd.t[:, 0:1], scalar1=s_.t[:, 2, h:h + 1], scalar2=None, op0=ALU.mult),
                                reads=[pd.tr, s_.tr], writes=[r_.tr])
                            yield
                            S.op("dve", I("tensor_scalar",
                                out=r_.t[:, 3:4], in0=r_.t[:, 0:1], scalar1=-1.0, scalar2=None, op0=ALU.mult),
                                reads=[r_.tr], writes=[r_.tr])
                            yield
                            S.op("dve", I("scalar_tensor_tensor",
                                out=r_.t[:, 0:1], in0=r_.t[:, 0:1], scalar=1.0, in1=r_.t[:, 3:4], op0=ALU.max, op1=ALU.max),
                                reads=[r_.tr], writes=[r_.tr])
                            yield
                            S.op("dve", I("reciprocal", out=r_.t[:, 1:2], in_=r_.t[:, 0:1]), reads=[r_.tr], writes=[r_.tr])
                            yield
                            S.op("dve", I("tensor_tensor", out=r_.t[:, 2:3], in0=r_.t[:, 1:2], in1=s_.t[:, 2, h:h + 1],
                                                                                     op=ALU.mult), reads=[r_.tr, s_.tr], writes=[r_.tr])
                            yield
                            S.op("act", I("activation",
                                out=o_m.t[:, h * 512:(h + 1) * 512], in_=pn.t[:], func=AF.Copy, scale=r_.t[:, 2:3]),
                                reads=[pn.tr, r_.tr], writes=[o_m.tr])
                            yield
                        kh = nx("Kh", Kh)
                        S.op("dve", I("tensor_scalar",
                            out=kh.t[:], in0=km_.t[:, h * 256:(h + 1) * 256], scalar1=s_.t[:, 4, h:h + 1], scalar2=None, op0=ALU.mult),
                            reads=[km_.tr, s_.tr], writes=[kh.tr])
                        yield
                        px = nx("psx", ps_x)
                        pcs = []
                        for kt in range(2):
                            pc = nx("psc", ps_c)
                            pcs.append(pc)
                            S.op("pe", [I("matmul",
                                out=pc.t[:], lhsT=kh.t[:, kt * 128:(kt + 1) * 128], rhs=vm_.t[:, h * 512:(h + 1) * 512], start=True, stop=True),
                                I("matmul",
                                out=px.t[:, kt:kt + 1], lhsT=kh.t[:, kt * 128:(kt + 1) * 128], rhs=onesb.t[:, 0:1], start=True, stop=True)],
                                reads=[kh.tr, vm_.tr, onesb.tr], writes=[pc.tr, px.tr])
                            yield
                        for kt in range(2):
                            S.op("dve", I("scalar_tensor_tensor",
                                out=Cm[h].t[:, kt, :], in0=Cm[h].t[:, kt, :], scalar=s_.t[:, 5, h:h + 1], in1=pcs[kt].t[:], op0=ALU.mult, op1=ALU.add),
                                reads=[Cm[h].tr, s_.tr, pcs[kt].tr], writes=[Cm[h].tr])
                            yield
                        S.op("dve", I("scalar_tensor_tensor",
                            out=Nm.t[:, h * 2:h * 2 + 2], in0=Nm.t[:, h * 2:h * 2 + 2], scalar=s_.t[:, 5, h:h + 1], in1=px.t[:, 0:2],
                            op0=ALU.mult, op1=ALU.add), reads=[Nm.tr, s_.tr, px.tr], writes=[Nm.tr])
                        yield
                        if outputs:
                            S.op("act", I("copy", out=Cmb[h].t[:], in_=Cm[h].t[:]), reads=[Cm[h].tr], writes=[Cmb[h].tr])
                            yield
                    def rh(h):
                        if outputs:
                            pss = nx("pss", ps_s)
                            a_t = nx("AT", AT)
                            pn = nx("psn", ps_n)
                            p1 = nx("p1s", p1s)
                            S.op("pe", I("matmul", out=pss.t[:], lhsT=kTr_.t[:, h, :], rhs=qTr_.t[:, h, :], start=True, stop=True),
                                 reads=[kTr_.tr, qTr_.tr], writes=[pss.tr])
                            yield
                            S.op("dve", I("tensor_tensor", out=a_t.t[:], in0=pss.t[:], in1=gT.t[:, h, :], op=ALU.mult),
                                 reads=[pss.tr, gT.tr], writes=[a_t.tr])
                            yield
                            S.op("pe", [I("matmul", out=pn.t[:, 0:256], lhsT=a_t.t[:], rhs=vr_.t[:, h * 256:(h + 1) * 256],
                                                                                start=True, stop=True),
                                        I("matmul", out=pn.t[:, 256:512], lhsT=qTr_.t[:, h, :], rhs=Srb[h].t[:], start=True, stop=True)],
                                 reads=[a_t.tr, vr_.tr, qTr_.tr, Srb[h].tr], writes=[pn.tr])
                            yield
                            S.op("act", I("copy", out=p1.t[:], in_=pn.t[:, 0:256]), reads=[pn.tr], writes=[p1.tr])
                            yield
                            S.op("dve", I("scalar_tensor_tensor",
                                out=o_r.t[:, h * 256:(h + 1) * 256], in0=pn.t[:, 256:512], scalar=dec.t[:, d, h:h + 1], in1=p1.t[:],
                                op0=ALU.mult, op1=ALU.add), reads=[pn.tr, p1.tr, dec.tr], writes=[o_r.tr])
                            yield
                        kh = nx("Kh", Kh)
                        S.op("dve", I("tensor_scalar",
                            out=kh.t[:, 0:128], in0=kr_.t[:, h * 128:(h + 1) * 128], scalar1=dec.t[:, 2 + d, h:h + 1], scalar2=None, op0=ALU.mult),
                            reads=[kr_.tr, dec.tr], writes=[kh.tr])
                        yield
                        pc = nx("psc", ps_c)
                        S.op("pe", I("matmul", out=pc.t[:, 0:256], lhsT=kh.t[:, 0:128], rhs=vr_.t[:, h * 256:(h + 1) * 256],
                                                                         start=True, stop=True), reads=[kh.tr, vr_.tr], writes=[pc.tr])
                        yield
                        S.op("dve", I("scalar_tensor_tensor",
                            out=Sr[h].t[:], in0=Sr[h].t[:], scalar=g128[h], in1=pc.t[:, 0:256], op0=ALU.mult, op1=ALU.add),
                            reads=[Sr[h].tr, pc.tr], writes=[Sr[h].tr])
                        yield
                        if outputs:
                            S.op("act", I("copy", out=Srb[h].t[:], in_=Sr[h].t[:]), reads=[Sr[h].tr], writes=[Srb[h].tr])
                            yield
                    interleave([mh(h) for h in range(4)] + [rh(h) for h in range(8)], IW)
                    if outputs:
                        S.op("act", I("copy", out=Nmb.t[:], in_=Nm.t[:]), reads=[Nm.tr], writes=[Nmb.tr])
                    if outputs:
                        post(d, c, o_m, o_r)

            if mode == "p1":
                for d in range(2):
                    zero_states()
                    sweep(d, False)
                    for h in range(4):
                        S.dma("sp", oFc[d, h, :, :], Cm[h].t[:].rearrange("p a b -> p (a b)"), Cm[h].tr, reads=[Cm[h].tr])
                    for h in range(8):
                        S.dma("sp", oFs[d, h, :, :], Sr[h].t[:], Sr[h].tr, reads=[Sr[h].tr])
                    S.dma("sp", oFn[d, :, :], Nm.t[:], Nm.tr, reads=[Nm.tr])
                    S.dma("sp", oG[d, :, :], Gac.t[:], Gac.tr, reads=[Gac.tr])
                S.barrier()
                S.emit()
            else:
                cm = sb(st, "cm", [128, 2, ncores], F32)
                S.dma("sp", cm.t[:], cm_in[:, :, :], cm.tr, writes=[cm.tr])
                gGs = sb(st, "gGs", [128, ncores, 4], F32)
                cf = sb(st, "cf", [128, ncores, 4], F32)
                fld = [sb(st, "fld%d" % i, [128, 1024], F32) for i in range(2)]
                fln = [sb(st, "fln%d" % i, [128, 8], F32) for i in range(2)]

                def combine(d):
                    zero_states()
                    S.dma("sp", gGs.t[:], gG[:, d, :, :].rearrange("j p h -> p j h"), gGs.tr, writes=[gGs.tr])
                    S.op("act", I("activation", out=cf.t[:], in_=gGs.t[:], func=AF.Exp), reads=[gGs.tr], writes=[cf.tr])
                    S.op("dve", I("tensor_scalar", out=cf.t[:], in0=cf.t[:], scalar1=-1.0, scalar2=None, op0=ALU.add),
                         reads=[cf.tr], writes=[cf.tr])
                    S.op("dve", I("tensor_tensor", out=cf.t[:], in0=cf.t[:], in1=cm.t[:, d, :].unsqueeze(2).to_broadcast([128, ncores, 4]),
                                                          op=ALU.mult), reads=[cf.tr, cm.tr], writes=[cf.tr])
                    S.op("dve", I("tensor_scalar", out=cf.t[:], in0=cf.t[:], scalar1=1.0, scalar2=None, op0=ALU.add),
                         reads=[cf.tr], writes=[cf.tr])
                    order = list(range(ncores)) if d == 0 else list(range(ncores - 1, -1, -1))
                    fi = 0
                    for j in order:
                        mj = cm.t[:, d, j:j + 1]
                        for h in range(4):
                            f_ = fld[fi % 2]
                            fi += 1
                            S.dma("sp", f_.t[:], gFc[j, d, h, :, :], f_.tr, writes=[f_.tr])
                            S.op("pool", I("tensor_scalar", out=f_.t[:], in0=f_.t[:], scalar1=mj, scalar2=None, op0=ALU.mult),
                                 reads=[f_.tr, cm.tr], writes=[f_.tr])
                            S.op("dve", I("scalar_tensor_tensor",
                                out=Cm[h].t[:].rearrange("p a b -> p (a b)"), in0=Cm[h].t[:].rearrange("p a b -> p (a b)"),
                                scalar=cf.t[:, j, h:h + 1], in1=f_.t[:], op0=ALU.mult, op1=ALU.add),
                                reads=[Cm[h].tr, cf.tr, f_.tr], writes=[Cm[h].tr])
                        n_ = fln[fi % 2]
                        S.dma("sp", n_.t[:], gFn[j, d, :, :], n_.tr, writes=[n_.tr])
                        S.op("pool", I("tensor_scalar", out=n_.t[:], in0=n_.t[:], scalar1=mj, scalar2=None, op0=ALU.mult),
                             reads=[n_.tr, cm.tr], writes=[n_.tr])
                        for h in range(4):
                            S.op("dve", I("scalar_tensor_tensor",
                                out=Nm.t[:, h * 2:h * 2 + 2], in0=Nm.t[:, h * 2:h * 2 + 2], scalar=cf.t[:, j, h:h + 1], in1=n_.t[:, h * 2:h * 2 + 2],
                                op0=ALU.mult, op1=ALU.add), reads=[Nm.tr, cf.tr, n_.tr], writes=[Nm.tr])
                        for h in range(8):
                            f_ = fld[fi % 2]
                            fi += 1
                            S.dma("sp", f_.t[:, 0:256], gFs[j, d, h, :, :], f_.tr, writes=[f_.tr])
                            S.op("pool", I("tensor_scalar", out=f_.t[:, 0:256], in0=f_.t[:, 0:256], scalar1=mj, scalar2=None, op0=ALU.mult),
                                 reads=[f_.tr, cm.tr], writes=[f_.tr])
                            S.op("dve", I("tensor_scalar",
                                out=f_.t[:, 256:257], in0=mj, scalar1=gamT[h] - 1.0, scalar2=1.0, op0=ALU.mult, op1=ALU.add),
                                reads=[cm.tr, f_.tr], writes=[f_.tr])
                            S.op("dve", I("scalar_tensor_tensor",
                                out=Sr[h].t[:], in0=Sr[h].t[:], scalar=f_.t[:, 256:257], in1=f_.t[:, 0:256], op0=ALU.mult, op1=ALU.add),
                                reads=[Sr[h].tr, f_.tr], writes=[Sr[h].tr])
                    refresh_bf()

                mnwbc = sb(st, "mnwbc", [128, D], F32)
                rnwbc = sb(st, "rnwbc", [128, D], F32)
                S.dma("sp", mnwbc.t[:], mnw_in.partition_broadcast(128), mnwbc.tr, writes=[mnwbc.tr])
                S.dma("sp", rnwbc.t[:], rnw_in.partition_broadcast(128), rnwbc.tr, writes=[rnwbc.tr])
                of_l = [sb(st, "ofl%d" % i, [128, 2048], F32) for i in range(2)]
                gl = [sb(st, "gl%d" % i, [128, 2048], BF16) for i in range(2)]
                ub = [sb(st, "ub%d" % i, [128, 2048], BF16) for i in range(2)]
                jk = sb(st, "jk", [128, 512], BF16)
                hs = [sb(st, "hs%d" % i, [128, 16], F32) for i in range(2)]
                pu = [ps(st, "pu%d" % i, [128, 8, 128], BF16) for i in range(1)]
                uT = [sb(st, "uTs%d" % i, [128, 8, 128], BF16) for i in range(2)]

                def post(d, c, o_m, o_r):
                    r0 = c * 128
                    if d == 0:
                        S.dma("sp", d_ofm[r0:r0 + 128, :], o_m.t[:], o_m.tr, reads=[o_m.tr])
                        S.dma("sp", d_ofr[r0:r0 + 128, :], o_r.t[:], o_r.tr, reads=[o_r.tr])
                        return
                    for (o_, d_of, d_gate, d_sz, nwb, dU, nh, dv) in (
                            (o_m, d_ofm, d_som, d_szm, mnwbc, d_uTm, 4, 512), (o_r, d_ofr, None, d_szr, rnwbc, d_uTr, 8, 256)):
                        ofl = nx("ofl", of_l)
                        S.dma("sp", ofl.t[:], d_of[r0:r0 + 128, :], ofl.tr, writes=[ofl.tr])
                        S.op("dve", I("tensor_tensor", out=o_.t[:], in0=o_.t[:], in1=ofl.t[:], op=ALU.add),
                             reads=[o_.tr, ofl.tr], writes=[o_.tr])
                        if d_gate is not None:
                            g2 = nx("gl", gl)
                            S.dma("sp", g2.t[:], d_gate[r0:r0 + 128, :], g2.tr, writes=[g2.tr])
                            S.op("dve", I("tensor_tensor", out=o_.t[:], in0=o_.t[:], in1=g2.t[:], op=ALU.mult),
                                 reads=[o_.tr, g2.tr], writes=[o_.tr])
                        z2 = nx("gl", gl)
                        S.dma("sp", z2.t[:], d_sz[r0:r0 + 128, :], z2.tr, writes=[z2.tr])
                        h_ = nx("hs", hs)
                        for h in range(nh):
                            S.op("act", I("activation",
                                out=jk.t[:, 0:dv], in_=o_.t[:, h * dv:(h + 1) * dv], func=AF.Square, accum_out=h_.t[:, h:h + 1]),
                                reads=[o_.tr], writes=[jk.tr, h_.tr])
                        S.op("act", I("activation", out=h_.t[:, 0:nh], in_=h_.t[:, 0:nh], func=AF.Ln, scale=1.0 / dv, bias=EPS),
                             reads=[h_.tr], writes=[h_.tr])
                        S.op("act", I("activation", out=h_.t[:, 0:nh], in_=h_.t[:, 0:nh], func=AF.Exp, scale=-0.5),
                             reads=[h_.tr], writes=[h_.tr])
                        u_ = nx("ub", ub)
                        for h in range(nh):
                            S.op("dve", I("scalar_tensor_tensor",
                                out=o_.t[:, h * dv:(h + 1) * dv], in0=o_.t[:, h * dv:(h + 1) * dv], scalar=h_.t[:, h:h + 1],
                                in1=nwb.t[:, h * dv:(h + 1) * dv], op0=ALU.mult, op1=ALU.mult), reads=[o_.tr, h_.tr, nwb.tr], writes=[o_.tr])
                        S.op("dve", I("tensor_tensor", out=u_.t[:], in0=o_.t[:], in1=z2.t[:], op=ALU.mult),
                             reads=[o_.tr, z2.tr], writes=[u_.tr])
                        for g in range(2):
                            p_ = pu[0]
                            t_ = nx("uTs", uT)
                            S.op("pe", [I("transpose",
                                out=p_.t[:, k, :], in_=u_.t[:, (g * 8 + k) * 128:(g * 8 + k + 1) * 128], identity=identb.t[:])
                                for k in range(8)], reads=[u_.tr, identb.tr], writes=[p_.tr])
                            S.op("act", I("copy", out=t_.t[:], in_=p_.t[:]), reads=[p_.tr], writes=[t_.tr])
                            S.dma("sp", dU[g * 1024:(g + 1) * 1024, r0:r0 + 128].rearrange("(k p) t -> p k t", p=128), t_.t[:], t_.tr, reads=[t_.tr])

                combine(0)
                sweep(0, True, post)
                S.barrier()
                combine(1)
                sweep(1, True, post)
                S.barrier()
                S.emit()

        if mode == "p2":
            TC = min(T, 1024)
            NTC = TC // 128
            with contextlib.ExitStack() as st:
                uTm = sb(st, "uTm", [128, KT, TC], BF16)
                uTr = sb(st, "uTr", [128, KT, TC], BF16)
                yT = sb(st, "yT", [128, KT, TC], BF16)
                yT_tr = [Tr("yT%d" % i) for i in range(NTC)]
                wm = [sb(st, "wm%d" % i, [128, KT, 512], BF16) for i in range(2)]
                wr = [sb(st, "wr%d" % i, [128, KT, 512], BF16) for i in range(2)]
                pm_ = [ps(st, "cpm%d" % i, [128, 512], F32) for i in range(2)]
                pr_ = [ps(st, "cpr%d" % i, [128, 512], F32) for i in range(2)]
                pq_ = [ps(st, "cpq%d" % i, [128, 4, 128], BF16) for i in range(2)]
                mx = [sb(st, "mx%d" % i, [128, 2, 512], BF16) for i in range(2)]
                t1 = [sb(st, "t1%d" % i, [128, 512], F32) for i in range(2)]
                t2 = [sb(st, "t2%d" % i, [128, 512], F32) for i in range(2)]
                yb = [sb(st, "yb%d" % i, [128, 512], BF16) for i in range(2)]
                xr = [sb(st, "xr%d" % i, [128, 512], F32) for i in range(2)]
                rot2 = {}

                def n2(key, lst):
                    i = rot2.get(key, 0)
                    rot2[key] = i + 1
                    return lst[i % len(lst)]

                for t0 in range(0, T, TC):
                    S.dma("sp", uTm.t[:], d_uTm[:, t0:t0 + TC].rearrange("(k p) t -> p k t", p=128), uTm.tr, writes=[uTm.tr])
                    S.dma("sp", uTr.t[:], d_uTr[:, t0:t0 + TC].rearrange("(k p) t -> p k t", p=128), uTr.tr, writes=[uTr.tr])
                    for cb in range(4):
                        wm_, wr_ = n2("wm", wm), n2("wr", wr)
                        S.dma("pool", wm_.t[:], wpm_in[:, cb * 512:(cb + 1) * 512].rearrange("(k p) n -> p k n", p=128), wm_.tr, writes=[wm_.tr])
                        S.dma("pool", wr_.t[:], wpr_in[:, cb * 512:(cb + 1) * 512].rearrange("(k p) n -> p k n", p=128), wr_.tr, writes=[wr_.tr])
                        for tt in range(NTC):
                            r0 = t0 + tt * 128
                            a_, b_ = n2("pm", pm_), n2("pr", pr_)
                            S.op("pe", [I("matmul", out=a_.t[:], lhsT=uTm.t[:, k, tt * 128:(tt + 1) * 128],
                                                                                      rhs=wm_.t[:, k, :], start=(k == 0), stop=(k == KT - 1))
                                        for k in range(KT)], reads=[uTm.tr, wm_.tr], writes=[a_.tr])
                            S.op("pe", [I("matmul", out=b_.t[:], lhsT=uTr.t[:, k, tt * 128:(tt + 1) * 128],
                                                                                      rhs=wr_.t[:, k, :], start=(k == 0), stop=(k == KT - 1))
                                        for k in range(KT)], reads=[uTr.tr, wr_.tr], writes=[b_.tr])
                            m_ = n2("mx", mx)
                            S.dma("sp", m_.t[:], d_mix[r0:r0 + 128, :].rearrange("p (two c) -> p two c", two=2)[:, :, cb * 512:(cb + 1) * 512],
                                  m_.tr, writes=[m_.tr])
                            u1, u2, y_ = n2("t1", t1), n2("t2", t2), n2("yb", yb)
                            S.op("dve", I("tensor_tensor", out=u1.t[:], in0=a_.t[:], in1=m_.t[:, 0, :], op=ALU.mult),
                                 reads=[a_.tr, m_.tr], writes=[u1.tr])
                            S.op("dve", I("tensor_tensor", out=u2.t[:], in0=b_.t[:], in1=m_.t[:, 1, :], op=ALU.mult),
                                 reads=[b_.tr, m_.tr], writes=[u2.tr])
                            S.op("dve", I("tensor_tensor", out=y_.t[:], in0=u1.t[:], in1=u2.t[:], op=ALU.add),
                                 reads=[u1.tr, u2.tr], writes=[y_.tr])
                            q_ = n2("pq", pq_)
                            S.op("pe", [I("transpose", out=q_.t[:, i, :], in_=y_.t[:, i * 128:(i + 1) * 128], identity=identb.t[:])
                                        for i in range(4)], reads=[y_.tr, identb.tr], writes=[q_.tr])
                            S.op("act", I("copy", out=yT.t[:, cb * 4:(cb + 1) * 4, tt * 128:(tt + 1) * 128], in_=q_.t[:]),
                                 reads=[q_.tr], writes=[yT_tr[tt]])
                    for cb in range(4):
                        wm_ = n2("wm", wm)
                        S.dma("pool", wm_.t[:], wo_in[:, cb * 512:(cb + 1) * 512].rearrange("(k p) n -> p k n", p=128), wm_.tr, writes=[wm_.tr])
                        for tt in range(NTC):
                            r0 = t0 + tt * 128
                            a_ = n2("pm", pm_)
                            S.op("pe", [I("matmul", out=a_.t[:], lhsT=yT.t[:, k, tt * 128:(tt + 1) * 128],
                                                                                      rhs=wm_.t[:, k, :], start=(k == 0), stop=(k == KT - 1))
                                        for k in range(KT)], reads=[yT_tr[tt], wm_.tr], writes=[a_.tr])
                            x_ = n2("xr", xr)
                            S.dma("sp", x_.t[:], x_in[r0:r0 + 128, cb * 512:(cb + 1) * 512], x_.tr, writes=[x_.tr])
                            S.op("dve", I("tensor_tensor", out=x_.t[:], in0=a_.t[:], in1=x_.t[:], op=ALU.add),
                                 reads=[a_.tr, x_.tr], writes=[x_.tr])
                            S.dma("sp", d_xn[r0:r0 + 128, cb * 512:(cb + 1) * 512], x_.t[:], x_.tr, reads=[x_.tr])
                S.barrier()
                S.emit()
            if last:
                with contextlib.ExitStack() as st:
                    fnbc = sb(st, "fnbc", [128, D], F32)
                    S.dma("sp", fnbc.t[:], fnw_in.partition_broadcast(128), fnbc.tr, writes=[fnbc.tr])
                    xf = [sb(st, "xf%d" % i, [128, D], F32) for i in range(2)]
                    jf = sb(st, "jf", [128, D], BF16)
                    sf = [sb(st, "sf%d" % i, [128, 1], F32) for i in range(2)]
                    for tt in range(NT):
                        x_, s_ = xf[tt % 2], sf[tt % 2]
                        S.dma("sp", x_.t[:], d_xn[tt * 128:(tt + 1) * 128, :], x_.tr, writes=[x_.tr])
                        S.op("act", I("activation", out=jf.t[:], in_=x_.t[:], func=AF.Square, accum_out=s_.t[:]),
                             reads=[x_.tr], writes=[jf.tr, s_.tr])
                        S.op("act", I("activation", out=s_.t[:], in_=s_.t[:], func=AF.Ln, scale=1.0 / D, bias=EPS), reads=[s_.tr], writes=[s_.tr])
                        S.op("act", I("activation", out=s_.t[:], in_=s_.t[:], func=AF.Exp, scale=-0.5), reads=[s_.tr], writes=[s_.tr])
                        S.op("dve", I("scalar_tensor_tensor", out=x_.t[:], in0=x_.t[:], scalar=s_.t[:, 0:1], in1=fnbc.t[:],
                                                                                   op0=ALU.mult, op1=ALU.mult), reads=[x_.tr, s_.tr, fnbc.tr], writes=[x_.tr])
                        S.dma("sp", y_out[tt * 128:(tt + 1) * 128, :], x_.t[:], x_.tr, reads=[x_.tr])
                    S.barrier()
                    S.emit()
    return nc


_PROGS = {}


def _prog(T, mode, last, ncores):
    key = (T, mode, last, ncores)
    if key not in _PROGS:
        _PROGS[key] = build(T, mode, last, ncores=ncores)
    return _PROGS[key]


def _layer_inputs(x_full, positions, layer, params, T, ncores, consts):
    maps = []
    S_ = x_full.shape[0]
    for i in range(ncores):
        t0 = i * T
        xh = np.zeros((T + 4, D), np.float32)
        xh[0:T] = x_full[t0:t0 + T]
        if t0 >= 2:
            xh[T:T + 2] = x_full[t0 - 2:t0]
        if t0 + T + 2 <= S_:
            xh[T + 2:T + 4] = x_full[t0 + T:t0 + T + 2]
        m = {"x": xh,
             "pos": np.ascontiguousarray(positions[t0:t0 + T].reshape(T // 128, 128).T.astype(np.int32)),
             "w_in": params["w_in"][layer],
             "norm_w": params["norm_w"][layer].reshape(1, D),
             "b_mgate": params["b_mgate"][layer].reshape(1, 16),
             "conv_w": np.ascontiguousarray(params["conv_w"][layer].reshape(5, 16, 128).transpose(2, 1, 0)),
             "conv_b": np.ascontiguousarray(params["conv_b"][layer].reshape(16, 128).T)}
        m.update(consts)
        maps.append(m)
    return maps


def run_layers(x_full, positions, params, T, ncores, depth):
    consts = host_consts(T)
    x_cur = np.ascontiguousarray(x_full, dtype=np.float32)
    for layer in range(depth):
        last = layer == depth - 1
        maps = _layer_inputs(x_cur, positions, layer, params, T, ncores, consts)
        w_full = params["w_in"][layer]
        w_p1 = np.ascontiguousarray(np.concatenate(
            [w_full[:, O_KM:O_KM + 1024], w_full[:, O_VM:O_VM + 2048], w_full[:, O_GM:O_GM + 16],
             w_full[:, O_KR:O_KR + 1024], w_full[:, O_VR:O_VR + 2048]], axis=1))
        maps1 = [dict(m, w_in=w_p1) for m in maps]
        r1 = run_bass_kernel_spmd(_prog(T, "p1", False, ncores), maps1, core_ids=list(range(ncores))).results
        gFc = np.stack([r["oFc"] for r in r1])
        gFn = np.stack([r["oFn"] for r in r1])
        gFs = np.stack([r["oFs"] for r in r1])
        gG = np.stack([r["oG"] for r in r1])
        for i, m in enumerate(maps):
            m.update({"m_norm_w": params["m_norm_w"][layer].reshape(1, D),
                      "r_norm_w": params["r_norm_w"][layer].reshape(1, D),
                      "w_proj_m": params["w_proj_m"][layer], "w_proj_r": params["w_proj_r"][layer],
                      "w_out": params["w_out"][layer], "b_mix": params["b_mix"][layer].reshape(1, 2 * D),
                      "final_norm_w": params["final_norm_w"].reshape(1, D),
                      "cmask": core_masks(i)[:, :, :ncores].copy(),
                      "gFc": gFc, "gFn": gFn, "gFs": gFs, "gG": gG})
        r2 = run_bass_kernel_spmd(_prog(T, "p2", last, ncores), maps, core_ids=list(range(ncores))).results
        x_cur = np.concatenate([r["y"] for r in r2], axis=0)
    return x_cur


def kernel(x, positions, norm_w, w_in, b_mgate, conv_w, conv_b, m_norm_w, r_norm_w,
           w_proj_m, w_proj_r, b_mix, w_out, final_norm_w):
    params = {k: np.asarray(v) for k, v in dict(
        norm_w=norm_w, w_in=w_in, b_mgate=b_mgate, conv_w=conv_w, conv_b=conv_b, m_norm_w=m_norm_w,
        r_norm_w=r_norm_w, w_proj_m=w_proj_m, w_proj_r=w_proj_r, b_mix=b_mix, w_out=w_out,
        final_norm_w=final_norm_w).items()}
    x = np.asarray(x)
    Sq = x.shape[1]
    T = Sq // NCORES
    out = run_layers(x[0], np.asarray(positions)[0], params, T, NCORES, w_in.shape[0])
    return out.reshape(1, Sq, D).astype(np.float32)
```

```python
import contextlib
import numpy as np
import concourse.bass as bass
import concourse.mybir as mybir
from concourse.bass_utils import run_bass_kernel_spmd

F32 = mybir.dt.float32
BF16 = mybir.dt.bfloat16
I32 = mybir.dt.int32
AF = mybir.ActivationFunctionType
ALU = mybir.AluOpType
AX = mybir.AxisListType

NCORES = 8
IW = 2
D = 2048
KT = 16
DIN = 18448
P1_COLS = 6160
EPS = 1e-6
O_QM, O_KM, O_VM, O_ZM, O_OM, O_GM = 0, 1024, 2048, 4096, 6144, 8192
O_QR, O_KR, O_VR, O_ZR, O_GTM, O_GTR = 8208, 9232, 10256, 12304, 14352, 16400
GAM = [1.0 - 2.0 ** (-5 - h) for h in range(8)]
PI = float(np.pi)


class Tr:
    __slots__ = ("name", "lw", "rd", "dsem", "dcnt", "dq")

    def __init__(self, name=""):
        self.name = name
        self.lw = None
        self.rd = {}
        self.dsem = None
        self.dcnt = 0
        self.dq = None


class Sched:
    ENG = ("pe", "act", "dve", "pool", "sp")

    def __init__(self, nc, stack):
        self.nc = nc
        self.stack = stack
        self.sems = {}
        self.final = {}
        self.cnt = {}
        self.waited = {e: {} for e in self.ENG}
        self.prog = {e: [] for e in self.ENG}
        self.nsem = 0
        self.dpool = {"pool": [], "hw": []}
        for e in self.ENG:
            self._mksem("E_" + e)
            self.cnt[e] = 0

    def _mksem(self, key):
        self.sems[key] = self.stack.enter_context(self.nc.semaphore(key))
        self.final[key] = 0
        self.nsem += 1
        return key

    def _deps(self, reads, writes):
        deps = {}
        for r in reads:
            if r.lw is not None and deps.get(r.lw[0], 0) < r.lw[1]:
                deps[r.lw[0]] = r.lw[1]
        for w in writes:
            if w.lw is not None and deps.get(w.lw[0], 0) < w.lw[1]:
                deps[w.lw[0]] = w.lw[1]
            for k, v in w.rd.items():
                if deps.get(k, 0) < v:
                    deps[k] = v
        return deps

    def _emit_waits(self, eng, deps):
        wd = self.waited[eng]
        for k, v in deps.items():
            if wd.get(k, 0) >= v:
                continue
            wd[k] = v
            sem = self.sems[k]
            self.prog[eng].append(lambda e, sem=sem, v=v: e.wait_ge(sem, v))

    def _record(self, tok, reads, writes):
        k, v = tok
        self.final[k] = max(self.final[k], v)
        for r in reads:
            if r.rd.get(k, 0) < v:
                r.rd[k] = v
        for w in writes:
            w.lw = tok
            w.rd = {}

    def op(self, eng, fns, reads=(), writes=()):
        if isinstance(fns, tuple):
            fns = [fns]
        fns = [(lambda e, f=f: getattr(e, f[0])(*f[1], **f[2])) for f in fns]
        self._emit_waits(eng, self._deps(reads, writes))
        self.cnt[eng] += 1
        v = self.cnt[eng]
        sem = self.sems["E_" + eng]
        n = len(fns)
        for i, fn in enumerate(fns):
            if i == n - 1:
                self.prog[eng].append(lambda e, fn=fn, sem=sem: fn(e).then_inc(sem, 1))
            else:
                self.prog[eng].append(lambda e, fn=fn: fn(e))
        self._record(("E_" + eng, v), reads, writes)

    def dma(self, q, out_ap, in_ap, owner, reads=(), writes=(), **kw):
        qc = "pool" if q == "pool" else "hw"
        if owner.dsem is not None and owner.dq != qc:
            pre = {owner.dsem: owner.dcnt * 16}
            self._emit_waits(q, pre)
            self.dpool[owner.dq].append((owner.dsem, owner.dcnt))
            owner.dsem = None
        if owner.dsem is None:
            if self.dpool[qc]:
                owner.dsem, owner.dcnt = self.dpool[qc].pop()
            else:
                owner.dsem = self._mksem("D%d" % self.nsem)
                owner.dcnt = 0
            owner.dq = qc
        deps = self._deps(reads, writes)
        if owner.dcnt > 0 and deps.get(owner.dsem, 0) < owner.dcnt * 16:
            deps[owner.dsem] = owner.dcnt * 16
        self._emit_waits(q, deps)
        owner.dcnt += 1
        v = owner.dcnt * 16
        sem = self.sems[owner.dsem]
        self.prog[q].append(
            lambda e, o=out_ap, i=in_ap, sem=sem, kw=kw: e.dma_start(out=o, in_=i, **kw).then_inc(sem, 16))
        self._record((owner.dsem, v), reads, writes)

    def release(self, trs):
        for t in trs:
            if t.dsem is not None:
                self.dpool[t.dq].append((t.dsem, t.dcnt))
                t.dsem = None

    def barrier(self):
        for e in self.ENG:
            self._emit_waits(e, {k: v for k, v in self.final.items() if v > 0})

    def emit(self):
        nc = self.nc
        progs = self.prog
        self.prog = {e: [] for e in self.ENG}
        with nc.Block() as block:
            @block.tensor
            def _(e):
                for t in progs["pe"]:
                    t(e)

            @block.scalar
            def _(e):
                for t in progs["act"]:
                    t(e)

            @block.vector
            def _(e):
                for t in progs["dve"]:
                    t(e)

            @block.gpsimd
            def _(e):
                for t in progs["pool"]:
                    t(e)

            @block.sync
            def _(e):
                for t in progs["sp"]:
                    t(e)


def I(name, *args, **kw):
    return (name, args, kw)


class B:
    __slots__ = ("t", "tr")

    def __init__(self, t, name):
        self.t = t
        self.tr = Tr(name)


def host_consts(T):
    c = {}
    a = np.arange(128)
    c["c_ident"] = np.eye(128, dtype=np.float32)
    mu = (a[:, None] <= a[None, :]).astype(np.float32)
    ml = (a[:, None] >= a[None, :]).astype(np.float32)
    c["c_mask"] = np.stack([mu, ml, np.ones((128, 128), np.float32)], axis=1)
    gf = np.zeros((128, 8, 128), np.float32)
    gb = np.zeros((128, 8, 128), np.float32)
    dec = np.zeros((128, 4, 8), np.float32)
    for h in range(8):
        lg = np.log1p(-(2.0 ** (-5.0 - h)))
        diff = (a[None, :] - a[:, None]).astype(np.float64)
        gf[:, h, :] = np.where(diff >= 0, np.exp(np.where(diff >= 0, diff, 0) * lg), 0.0)
        gb[:, h, :] = np.where(diff < 0, np.exp(np.where(diff < 0, -diff, 0) * lg), 0.0)
        dec[:, 0, h] = np.exp((a + 1.0) * lg)
        dec[:, 1, h] = np.exp((128.0 - a) * lg)
        dec[:, 2, h] = np.exp((127.0 - a) * lg)
        dec[:, 3, h] = np.exp((a + 0.0) * lg)
    c["c_gf"] = gf
    c["c_gb"] = gb
    c["c_dec"] = dec
    invf = np.power(np.float32(10000.0), -np.arange(64, dtype=np.float32) / np.float32(64.0)).astype(np.float32)
    c["c_invf"] = np.broadcast_to(invf[None, :], (128, 64)).copy()
    return c


def core_masks(i):
    m = np.zeros((128, 2, NCORES), np.float32)
    for j in range(NCORES):
        m[:, 0, j] = 1.0 if j < i else 0.0
        m[:, 1, j] = 1.0 if j > i else 0.0
    return m


def build(T, mode, last, debug=False, ncores=NCORES):
    NT = T // 128
    TH = T + 4
    nc = bass.Bass("TRN2", target_bir_lowering=False)
    dbg_kind = "ExternalOutput" if debug else "Internal"

    def din(name, shape, dt=F32):
        return nc.dram_tensor(name, list(shape), dt, kind="ExternalInput").ap()

    def dscr(name, shape, dt, kind=None):
        return nc.dram_tensor(name, list(shape), dt, kind=kind or dbg_kind).ap()

    x_in = din("x", [TH, D])
    pos_in = din("pos", [128, NT], I32)
    if mode == "p1":
        O_KM_, O_VM_, O_GM_, O_KR_, O_VR_ = 0, 1024, 3072, 3088, 4112
        w_in = din("w_in", [D, P1_COLS])
    else:
        O_KM_, O_VM_, O_GM_, O_KR_, O_VR_ = O_KM, O_VM, O_GM, O_KR, O_VR
        w_in = din("w_in", [D, DIN])
    nw_in = din("norm_w", [1, D])
    bg_in = din("b_mgate", [1, 16])
    cw_in = din("conv_w", [128, 16, 5])
    cb_in = din("conv_b", [128, 16])
    c_ident = din("c_ident", [128, 128])
    c_mask = din("c_mask", [128, 3, 128])
    c_gf = din("c_gf", [128, 8, 128])
    c_gb = din("c_gb", [128, 8, 128])
    c_dec = din("c_dec", [128, 4, 8])
    c_invf = din("c_invf", [128, 64])
    if mode == "p2":
        mnw_in = din("m_norm_w", [1, D])
        rnw_in = din("r_norm_w", [1, D])
        wpm_in = din("w_proj_m", [D, D])
        wpr_in = din("w_proj_r", [D, D])
        wo_in = din("w_out", [D, D])
        bmix_in = din("b_mix", [1, 2 * D])
        fnw_in = din("final_norm_w", [1, D])
        cm_in = din("cmask", [128, 2, ncores])
        gFc = din("gFc", [ncores, 2, 4, 128, 1024])
        gFn = din("gFn", [ncores, 2, 128, 8])
        gFs = din("gFs", [ncores, 2, 8, 128, 256])
        gG = din("gG", [ncores, 2, 128, 4])
        y_out = dscr("y", [T, D], F32, kind="ExternalOutput")
    else:
        oFc = dscr("oFc", [2, 4, 128, 1024], F32, kind="ExternalOutput")
        oFn = dscr("oFn", [2, 128, 8], F32, kind="ExternalOutput")
        oFs = dscr("oFs", [2, 8, 128, 256], F32, kind="ExternalOutput")
        oG = dscr("oG", [2, 128, 4], F32, kind="ExternalOutput")

    full = mode == "p2"
    d_kTm = dscr("d_kTm", [1024, T], BF16) if full else None
    d_qTm = dscr("d_qTm", [1024, T], BF16) if full else None
    d_km = dscr("d_km", [T, 1024], BF16)
    d_vm = dscr("d_vm", [T, 2048], BF16)
    d_g = dscr("d_g", [T, 16], F32)
    d_kr = dscr("d_kr", [T, 1024], BF16)
    d_vr = dscr("d_vr", [T, 2048], BF16)
    if full:
        d_szm = dscr("d_szm", [T, 2048], BF16)
        d_som = dscr("d_som", [T, 2048], BF16)
        d_qTr = dscr("d_qTr", [1024, T], BF16)
        d_kTr = dscr("d_kTr", [1024, T], BF16)
        d_szr = dscr("d_szr", [T, 2048], BF16)
        d_mix = dscr("d_mix", [T, 4096], BF16)
        d_ofm = dscr("d_ofm", [T, 2048], F32)
        d_ofr = dscr("d_ofr", [T, 2048], F32)
        d_uTm = dscr("d_uTm", [2048, T], BF16)
        d_uTr = dscr("d_uTr", [2048, T], BF16)
        d_xn = dscr("d_xn", [T, D], F32) if last else y_out

    with contextlib.ExitStack() as top:
        S = Sched(nc, top)

        def sb(st, name, shape, dt):
            return B(st.enter_context(nc.sbuf_tensor(name, list(shape), dt)), name)

        def ps(st, name, shape, dt):
            return B(st.enter_context(nc.psum_tensor(name, list(shape), dt)), name)

        identb = sb(top, "identb", [128, 128], BF16)
        mask = sb(top, "mask", [128, 3, 128], F32)
        dec = sb(top, "dec", [128, 4, 8], F32)
        onesb = sb(top, "onesb", [128, 128], BF16)
        S.dma("pool", identb.t[:], c_ident[:, :], identb.tr, writes=[identb.tr])
        S.dma("sp", mask.t[:], c_mask[:, :, :], mask.tr, writes=[mask.tr])
        S.dma("sp", dec.t[:], c_dec[:, :, :], dec.tr, writes=[dec.tr])
        S.op("pool", I("memset", onesb.t[:], 1.0), writes=[onesb.tr])

        with contextlib.ExitStack() as st:
            hT = sb(st, "hT", [128, KT, TH], BF16)
            hT_tr = [Tr("hT%d" % i) for i in range(NT + 1)]
            nwbc = sb(st, "nwbc", [128, D], F32)
            S.dma("sp", nwbc.t[:], nw_in.partition_broadcast(128), nwbc.tr, writes=[nwbc.tr])
            with contextlib.ExitStack() as st1:
                xt = [sb(st1, "xt%d" % i, [128, D], F32) for i in range(2)]
                hb = [sb(st1, "hb%d" % i, [128, D], BF16) for i in range(2)]
                junk = sb(st1, "junk", [128, D], BF16)
                ss = [sb(st1, "ss%d" % i, [128, 1], F32) for i in range(2)]
                pT = [ps(st1, "pT%d" % i, [128, 8, 128], BF16) for i in range(2)]
                for tt in range(NT + 1):
                    b = tt % 2
                    np_ = 128 if tt < NT else 4
                    r0 = tt * 128
                    x_, h_, s_ = xt[b], hb[b], ss[b]
                    S.dma("sp", x_.t[0:np_, :], x_in[r0:r0 + np_, :], x_.tr, writes=[x_.tr])
                    S.op("act", I("activation",
                        out=junk.t[0:np_, :], in_=x_.t[0:np_, :], func=AF.Square, accum_out=s_.t[0:np_, :]),
                        reads=[x_.tr], writes=[junk.tr, s_.tr])
                    S.op("act", I("activation",
                        out=s_.t[0:np_, :], in_=s_.t[0:np_, :], func=AF.Ln, scale=1.0 / D, bias=EPS),
                        reads=[s_.tr], writes=[s_.tr])
                    S.op("act", I("activation",
                        out=s_.t[0:np_, :], in_=s_.t[0:np_, :], func=AF.Exp, scale=-0.5),
                        reads=[s_.tr], writes=[s_.tr])
                    S.op("dve", I("scalar_tensor_tensor",
                        out=h_.t[0:np_, :], in0=x_.t[0:np_, :], scalar=s_.t[0:np_, 0:1], in1=nwbc.t[0:np_, :],
                        op0=ALU.mult, op1=ALU.mult), reads=[x_.tr, s_.tr, nwbc.tr], writes=[h_.tr])
                    for g in range(2):
                        p_ = pT[g]
                        S.op("pe", [I("transpose",
                            out=p_.t[:, k, 0:np_], in_=h_.t[0:np_, (g * 8 + k) * 128:(g * 8 + k + 1) * 128],
                            identity=identb.t[0:np_, 0:np_]) for k in range(8)],
                            reads=[h_.tr, identb.tr], writes=[p_.tr])
                        S.op("act" if g == 0 else "dve", I("copy" if g == 0 else "tensor_copy",
                            out=hT.t[:, g * 8:(g + 1) * 8, r0:r0 + np_], in_=p_.t[:, :, 0:np_]),
                            reads=[p_.tr], writes=[hT_tr[tt]])
                S.barrier()
                S.emit()
                S.release([b_.tr for b_ in xt])

            with contextlib.ExitStack() as st2:
                wb = [sb(st2, "wb%d" % i, [128, KT, 512], BF16) for i in range(2)]
                pm = [ps(st2, "pm%d" % i, [128, 512], F32) for i in range(3)]
                pq = [ps(st2, "pq%d" % i, [128, 4, 128], BF16) for i in range(2)]
                ev = [sb(st2, "ev%d" % i, [128, 512], F32) for i in range(3)]
                ob = [sb(st2, "ob%d" % i, [128, 512], BF16) for i in range(3)]
                ot = [sb(st2, "ot%d" % i, [128, 4, 128], BF16) for i in range(2)]
                bgbc = sb(st2, "bgbc", [128, 16], F32)
                S.dma("sp", bgbc.t[:], bg_in.partition_broadcast(128), bgbc.tr, writes=[bgbc.tr])
                cnt = {"w": 0, "pm": 0, "ev": 0, "ob": 0, "pq": 0, "ot": 0}

                def nxt(key, lst):
                    i = cnt[key] % len(lst)
                    cnt[key] += 1
                    return lst[i]

                def load_w(wdram, c0, ncol):
                    w_ = nxt("w", wb)
                    S.dma("pool", w_.t[:, :, 0:ncol], wdram[:, c0:c0 + ncol].rearrange("(k p) n -> p k n", p=128),
                          w_.tr, writes=[w_.tr])
                    return w_

                def gemm_tok(w_, ncol, tt, extra=None):
                    p_ = nxt("pm", pm)
                    fns = [I("matmul",
                        out=p_.t[:, 0:ncol], lhsT=hT.t[:, k, tt * 128:(tt + 1) * 128], rhs=w_.t[:, k, 0:ncol],
                        start=(k == 0), stop=(k == KT - 1 and extra is None)) for k in range(KT)]
                    rds = [hT_tr[tt], w_.tr]
                    if extra is not None:
                        lhsT_ap, rhs_ap, etr = extra
                        fns.append(I("matmul", out=p_.t[:, 0:ncol], lhsT=lhsT_ap, rhs=rhs_ap,
                                                             start=False, stop=True))
                        rds.append(etr)
                    S.op("pe", fns, reads=rds, writes=[p_.tr])
                    return p_

                def store_tok(o_, dst, tt, c0, ncol):
                    S.dma("sp", dst[tt * 128:(tt + 1) * 128, c0:c0 + ncol], o_.t[:, 0:ncol], o_.tr, reads=[o_.tr])

                def seg_act(wdram, wc0, func, dst, ncols, bias_row=None):
                    for cb in range(ncols // 512):
                        w_ = load_w(wdram, wc0 + cb * 512, 512)
                        for tt in range(NT):
                            extra = None
                            if bias_row is not None:
                                extra = (onesb.t[0:1, 0:128], bias_row.t[0:1, cb * 512:(cb + 1) * 512], bias_row.tr)
                            p_ = gemm_tok(w_, 512, tt, extra)
                            o_ = nxt("ob", ob)
                            if func is None:
                                S.op("dve", I("tensor_copy", out=o_.t[:], in_=p_.t[:]),
                                     reads=[p_.tr], writes=[o_.tr])
                            else:
                                S.op("act", I("activation", out=o_.t[:], in_=p_.t[:], func=func),
                                     reads=[p_.tr], writes=[o_.tr])
                            store_tok(o_, dst, tt, cb * 512, 512)

                with contextlib.ExitStack() as st3:
                    cw = sb(st3, "cw", [128, 16, 5], F32)
                    cbias = sb(st3, "cbias", [128, 16], F32)
                    S.dma("sp", cw.t[:], cw_in[:, :, :], cw.tr, writes=[cw.tr])
                    S.dma("sp", cbias.t[:], cb_in[:, :], cbias.tr, writes=[cbias.tr])
                    crow = [sb(st3, "crow%d" % i, [128, TH], F32) for i in range(2)]
                    cacc = [sb(st3, "cacc%d" % i, [128, T], F32) for i in range(2)]
                    cq = [sb(st3, "cq%d" % i, [128, T], BF16) for i in range(2)]
                    blocks = list(range(16)) if full else list(range(8, 16))
                    wcur = None
                    for bi, fb in enumerate(blocks):
                        if fb % 4 == 0 or wcur is None:
                            wcur = load_w(w_in, (fb // 4) * 512 if full else O_KM_ + ((fb - 8) // 4) * 512, 512)
                        j = fb % 4
                        cr, ca, cq_ = crow[bi % 2], cacc[bi % 2], cq[bi % 2]
                        groups = [(g0, min(512, T - g0)) for g0 in range(0, T, 512)] + [(T, 4)]
                        for (g0, gn) in groups:
                            p_ = nxt("pm", pm)
                            tts = sorted(set(min(t // 128, NT) for t in range(g0, g0 + gn, 128)) | ({NT} if g0 == T else set()))
                            S.op("pe", [I("matmul",
                                out=p_.t[:, 0:gn], lhsT=wcur.t[:, k, j * 128:(j + 1) * 128], rhs=hT.t[:, k, g0:g0 + gn],
                                start=(k == 0), stop=(k == KT - 1)) for k in range(KT)],
                                reads=[wcur.tr] + [hT_tr[t] for t in tts], writes=[p_.tr])
                            if g0 < T:
                                S.op("act", I("copy",
                                    out=cr.t[:, 2 + g0:2 + g0 + gn], in_=p_.t[:, 0:gn]), reads=[p_.tr], writes=[cr.tr])
                            else:
                                S.op("act", I("copy", out=cr.t[:, 0:2], in_=p_.t[:, 0:2]),
                                     reads=[p_.tr], writes=[cr.tr])
                                S.op("act", I("copy", out=cr.t[:, T + 2:T + 4], in_=p_.t[:, 2:4]),
                                     reads=[p_.tr], writes=[cr.tr])
                        S.op("dve", I("tensor_scalar",
                            out=ca.t[:], in0=cr.t[:, 0:T], scalar1=cw.t[:, fb, 0:1], scalar2=cbias.t[:, fb:fb + 1],
                            op0=ALU.mult, op1=ALU.add), reads=[cr.tr, cw.tr, cbias.tr], writes=[ca.tr])
                        for tap in range(1, 5):
                            S.op("dve", I("scalar_tensor_tensor",
                                out=ca.t[:], in0=cr.t[:, tap:tap + T], scalar=cw.t[:, fb, tap:tap + 1], in1=ca.t[:],
                                op0=ALU.mult, op1=ALU.add), reads=[cr.tr, cw.tr, ca.tr], writes=[ca.tr])
                        if fb < 8:
                            S.op("act", I("activation", out=ca.t[:], in_=ca.t[:], func=AF.Silu),
                                 reads=[ca.tr], writes=[ca.tr])
                            S.op("act", I("activation", out=cq_.t[:], in_=ca.t[:], func=AF.Copy, scale=1.0 / 16.0),
                                reads=[ca.tr], writes=[cq_.tr])
                            S.dma("sp", d_qTm[fb * 128:(fb + 1) * 128, :], cq_.t[:], cq_.tr, reads=[cq_.tr])
                        else:
                            S.op("act", I("activation", out=cq_.t[:], in_=ca.t[:], func=AF.Silu),
                                 reads=[ca.tr], writes=[cq_.tr])
                            fk = fb - 8
                            if full:
                                S.dma("sp", d_kTm[fk * 128:(fk + 1) * 128, :], cq_.t[:], cq_.tr, reads=[cq_.tr])
                            for t4 in range(0, NT, 4):
                                nn = min(4, NT - t4)
                                q_ = nxt("pq", pq)
                                o_ = nxt("ot", ot)
                                S.op("pe", [I("transpose",
                                    out=q_.t[:, i, :], in_=cq_.t[:, (t4 + i) * 128:(t4 + i + 1) * 128], identity=identb.t[:])
                                    for i in range(nn)], reads=[cq_.tr, identb.tr], writes=[q_.tr])
                                S.op("dve", I("tensor_copy", out=o_.t[:, 0:nn, :], in_=q_.t[:, 0:nn, :]),
                                     reads=[q_.tr], writes=[o_.tr])
                                S.dma("sp", d_km[t4 * 128:(t4 + nn) * 128, fk * 128:(fk + 1) * 128].rearrange(
                                    "(i p) c -> p i c", p=128), o_.t[:, 0:nn, :], o_.tr, reads=[o_.tr])
                    S.barrier()
                    S.emit()
                    S.release([b_.tr for b_ in cq] + [cw.tr, cbias.tr])

                seg_act(w_in, O_VM_, None, d_vm, 2048)
                seg_act(w_in, O_VR_, None, d_vr, 2048)
                if full:
                    seg_act(w_in, O_ZM, AF.Silu, d_szm, 2048)
                    seg_act(w_in, O_OM, AF.Sigmoid, d_som, 2048)
                    seg_act(w_in, O_ZR, AF.Silu, d_szr, 2048)
                    bmixb = sb(st2, "bmixb", [1, 2 * D], BF16)
                    S.dma("pool", bmixb.t[:], bmix_in[:, :], bmixb.tr, writes=[bmixb.tr])
                    seg_act(w_in, O_GTM, AF.Sigmoid, d_mix, 4096, bias_row=bmixb)
                w_ = load_w(w_in, O_GM_, 16)
                gt = [sb(st2, "gt%d" % i, [128, 16], F32) for i in range(2)]
                ge = [sb(st2, "ge%d" % i, [128, 16], F32) for i in range(2)]
                for tt in range(NT):
                    p_ = gemm_tok(w_, 16, tt)
                    g_, e_ = gt[tt % 2], ge[tt % 2]
                    S.op("dve", I("tensor_tensor", out=g_.t[:], in0=p_.t[:, 0:16], in1=bgbc.t[:], op=ALU.add),
                         reads=[p_.tr, bgbc.tr], writes=[g_.tr])
                    S.op("act", I("activation", out=e_.t[:], in_=g_.t[:], func=AF.Exp, scale=-1.0),
                         reads=[g_.tr], writes=[e_.tr])
                    S.op("act", I("activation", out=e_.t[:], in_=e_.t[:], func=AF.Ln, bias=1.0),
                         reads=[e_.tr], writes=[e_.tr])
                    for c0 in (4, 12):
                        S.op("dve", I("tensor_scalar",
                            out=g_.t[:, c0:c0 + 4], in0=e_.t[:, c0:c0 + 4], scalar1=-1.0, scalar2=None, op0=ALU.mult),
                            reads=[e_.tr], writes=[g_.tr])
                    S.dma("sp", d_g[tt * 128:(tt + 1) * 128, :], g_.t[:], g_.tr, reads=[g_.tr])

                with contextlib.ExitStack() as st3:
                    posi = sb(st3, "posi", [128, NT], I32)
                    posf = sb(st3, "posf", [128, NT], F32)
                    invf = sb(st3, "invf", [128, 64], F32)
                    ang = sb(st3, "ang", [128, NT, 64], F32)
                    kf = sb(st3, "kf", [128, NT, 64], F32)
                    ki = sb(st3, "ki", [128, NT, 64], I32)
                    rr = sb(st3, "rr", [128, NT, 64], F32)
                    tmp = sb(st3, "tmpang", [128, NT, 64], F32)
                    msk = sb(st3, "mskang", [128, NT, 64], F32)
                    tab = sb(st3, "tab", [128, 4, NT, 64], F32)
                    S.dma("sp", posi.t[:], pos_in[:, :], posi.tr, writes=[posi.tr])
                    S.dma("sp", invf.t[:], c_invf[:, :], invf.tr, writes=[invf.tr])
                    S.op("dve", I("tensor_copy", out=posf.t[:], in_=posi.t[:]), reads=[posi.tr], writes=[posf.tr])
                    for t in range(NT):
                        S.op("dve", I("tensor_scalar", out=ang.t[:, t, :], in0=invf.t[:], scalar1=posf.t[:, t:t + 1],
                                                                   scalar2=None, op0=ALU.mult),
                             reads=[invf.tr, posf.tr], writes=[ang.tr])
                    S.op("dve", I("tensor_scalar", out=kf.t[:], in0=ang.t[:], scalar1=float(1.0 / (2 * np.pi)),
                                                          scalar2=None, op0=ALU.mult), reads=[ang.tr], writes=[kf.tr])
                    S.op("dve", I("tensor_copy", out=ki.t[:], in_=kf.t[:]), reads=[kf.tr], writes=[ki.tr])
                    S.op("dve", I("tensor_copy", out=kf.t[:], in_=ki.t[:]), reads=[ki.tr], writes=[kf.tr])
                    C1 = 6.28125
                    C2 = float(2 * np.pi - 6.28125)
                    S.op("dve", I("scalar_tensor_tensor", out=rr.t[:], in0=kf.t[:], scalar=-C1, in1=ang.t[:],
                                                                 op0=ALU.mult, op1=ALU.add), reads=[kf.tr, ang.tr], writes=[rr.tr])
                    S.op("dve", I("scalar_tensor_tensor", out=rr.t[:], in0=kf.t[:], scalar=-C2, in1=rr.t[:],
                                                                 op0=ALU.mult, op1=ALU.add), reads=[kf.tr, rr.tr], writes=[rr.tr])
                    for which, shift in ((1, 0.0), (0, PI / 2)):
                        S.op("dve", I("tensor_scalar", out=tmp.t[:], in0=rr.t[:], scalar1=shift, scalar2=None,
                                                                           op0=ALU.add), reads=[rr.tr], writes=[tmp.tr])
                        for (cmp_, thr, adj) in ((ALU.is_gt, PI, -2 * PI), (ALU.is_lt, -PI, 2 * PI)):
                            S.op("dve", I("tensor_scalar",
                                out=msk.t[:], in0=tmp.t[:], scalar1=thr, scalar2=None, op0=cmp_), reads=[tmp.tr], writes=[msk.tr])
                            S.op("dve", I("scalar_tensor_tensor",
                                out=tmp.t[:], in0=msk.t[:], scalar=adj, in1=tmp.t[:], op0=ALU.mult, op1=ALU.add),
                                reads=[msk.tr, tmp.tr], writes=[tmp.tr])
                        S.op("dve", I("tensor_scalar", out=tmp.t[:], in0=tmp.t[:], scalar1=PI, scalar2=-PI,
                                                              op0=ALU.min, op1=ALU.max), reads=[tmp.tr], writes=[tmp.tr])
                        S.op("act", I("activation", out=tab.t[:, which, :, :], in_=tmp.t[:], func=AF.Sin),
                             reads=[tmp.tr], writes=[tab.tr])
                    S.op("dve", I("tensor_scalar", out=tab.t[:, 2:4, :, :], in0=tab.t[:, 0:2, :, :],
                                                          scalar1=float(128.0 ** -0.5), scalar2=None, op0=ALU.mult),
                         reads=[tab.tr], writes=[tab.tr])
                    ra = [sb(st3, "ra%d" % i, [128, 4, 2, 64], F32) for i in range(2)]
                    rb = [sb(st3, "rb%d" % i, [128, 4, 2, 64], F32) for i in range(2)]
                    segs = ([("q", O_QR)] if full else []) + [("k", O_KR_)]
                    ri = 0
                    for (which, wc0) in segs:
                        tb = 2 if which == "q" else 0
                        for cb in range(2):
                            w_ = load_w(w_in, wc0 + cb * 512, 512)
                            for tt in range(NT):
                                p_ = gemm_tok(w_, 512, tt)
                                e_ = nxt("ev", ev)
                                o_ = nxt("ob", ob)
                                a_, b_ = ra[ri % 2], rb[ri % 2]
                                ri += 1
                                S.op("act", I("copy", out=e_.t[:], in_=p_.t[:]), reads=[p_.tr], writes=[e_.tr])
                                evv = e_.t[:].rearrange("p (h two j) -> p h two j", h=4, two=2)
                                ov = o_.t[:].rearrange("p (h two j) -> p h two j", h=4, two=2)
                                cosb = tab.t[:, tb, tt, :].unsqueeze(1).to_broadcast([128, 4, 64])
                                sinb = tab.t[:, tb + 1, tt, :].unsqueeze(1).to_broadcast([128, 4, 64])
                                S.op("dve", I("tensor_tensor",
                                    out=a_.t[:, :, 0, :], in0=evv[:, :, 0, :], in1=cosb, op=ALU.mult), reads=[e_.tr, tab.tr], writes=[a_.tr])
                                S.op("dve", I("tensor_tensor",
                                    out=a_.t[:, :, 1, :], in0=evv[:, :, 1, :], in1=cosb, op=ALU.mult), reads=[e_.tr, tab.tr], writes=[a_.tr])
                                S.op("dve", I("tensor_tensor",
                                    out=b_.t[:, :, 0, :], in0=evv[:, :, 1, :], in1=sinb, op=ALU.mult), reads=[e_.tr, tab.tr], writes=[b_.tr])
                                S.op("dve", I("tensor_tensor",
                                    out=b_.t[:, :, 1, :], in0=evv[:, :, 0, :], in1=sinb, op=ALU.mult), reads=[e_.tr, tab.tr], writes=[b_.tr])
                                S.op("dve", I("tensor_tensor",
                                    out=ov[:, :, 0, :], in0=a_.t[:, :, 0, :], in1=b_.t[:, :, 0, :], op=ALU.subtract),
                                    reads=[a_.tr, b_.tr], writes=[o_.tr])
                                S.op("dve", I("tensor_tensor",
                                    out=ov[:, :, 1, :], in0=a_.t[:, :, 1, :], in1=b_.t[:, :, 1, :], op=ALU.add),
                                    reads=[a_.tr, b_.tr], writes=[o_.tr])
                                if which == "k":
                                    store_tok(o_, d_kr, tt, cb * 512, 512)
                                if full:
                                    q_ = nxt("pq", pq)
                                    t_ = nxt("ot", ot)
                                    S.op("pe", [I("transpose",
                                        out=q_.t[:, i, :], in_=o_.t[:, i * 128:(i + 1) * 128], identity=identb.t[:])
                                        for i in range(4)], reads=[o_.tr, identb.tr], writes=[q_.tr])
                                    S.op("act", I("copy", out=t_.t[:], in_=q_.t[:]), reads=[q_.tr], writes=[t_.tr])
                                    dstT = d_qTr if which == "q" else d_kTr
                                    S.dma("sp", dstT[cb * 512:(cb + 1) * 512, tt * 128:(tt + 1) * 128].rearrange(
                                        "(i p) c -> p i c", p=128), t_.t[:], t_.tr, reads=[t_.tr])
                    S.barrier()
                    S.emit()
                    S.release([posi.tr, invf.tr])
                S.release([b_.tr for b_ in ob + ot + gt + wb] + [bgbc.tr])
            S.release([nwbc.tr])

        with contextlib.ExitStack() as st:
            gamT = [float(g ** T) for g in GAM]
            g128 = [float(g ** 128) for g in GAM]
            Cm = [sb(st, "Cm%d" % h, [128, 2, 512], F32) for h in range(4)]
            Nm = sb(st, "Nm", [128, 8], F32)
            Sr = [sb(st, "Sr%d" % h, [128, 256], F32) for h in range(8)]
            Cmb = [sb(st, "Cmb%d" % h, [128, 2, 512], BF16) for h in range(4)]
            Nmb = sb(st, "Nmb", [128, 8], BF16)
            Srb = [sb(st, "Srb%d" % h, [128, 256], BF16) for h in range(8)]
            Gac = sb(st, "Gac", [128, 4], F32)
            gfT = sb(st, "gfT", [128, 8, 128], F32)
            gbT = sb(st, "gbT", [128, 8, 128], F32)
            S.dma("sp", gfT.t[:], c_gf[:, :, :], gfT.tr, writes=[gfT.tr])
            S.dma("sp", gbT.t[:], c_gb[:, :, :], gbT.tr, writes=[gbT.tr])
            NB = 2
            gch = [sb(st, "gch%d" % i, [128, 16], F32) for i in range(NB)]
            kmc = [sb(st, "kmc%d" % i, [128, 1024], BF16) for i in range(NB)]
            vmc = [sb(st, "vmc%d" % i, [128, 2048], BF16) for i in range(NB)]
            krc = [sb(st, "krc%d" % i, [128, 1024], BF16) for i in range(NB)]
            vrc = [sb(st, "vrc%d" % i, [128, 2048], BF16) for i in range(NB)]
            if full:
                qTmc = [sb(st, "qTmc%d" % i, [128, 8, 128], BF16) for i in range(NB)]
                kTmc = [sb(st, "kTmc%d" % i, [128, 8, 128], BF16) for i in range(NB)]
                qTrc = [sb(st, "qTrc%d" % i, [128, 8, 128], BF16) for i in range(NB)]
                kTrc = [sb(st, "kTrc%d" % i, [128, 8, 128], BF16) for i in range(NB)]
                om = [sb(st, "om%d" % i, [128, 2048], F32) for i in range(2)]
                orr = [sb(st, "or%d" % i, [128, 2048], F32) for i in range(2)]
                AT = [sb(st, "AT%d" % i, [128, 128], BF16) for i in range(2)]
                Vh = [sb(st, "Vh%d" % i, [128, 512], BF16) for i in range(2)]
                p1s = [sb(st, "p1s%d" % i, [128, 256], F32) for i in range(2)]
                ps_n = [ps(st, "ps_n%d" % i, [128, 512], F32) for i in range(2)]
            Kh = [sb(st, "Kh%d" % i, [128, 256], BF16) for i in range(2)]
            sc = [sb(st, "sc%d" % i, [128, 8, 4], F32) for i in range(2)]
            scb = [sb(st, "scb%d" % i, [128, 4], BF16) for i in range(2)]
            rsc = [sb(st, "rsc%d" % i, [128, 4], F32) for i in range(2)]
            ps_c = [ps(st, "ps_c%d" % i, [128, 512], F32) for i in range(4)]
            small = ps(st, "ps_small", [128, 512], F32)
            ps_s = [B(small.t[:, i * 128:(i + 1) * 128], "ps_s%d" % i) for i in range(2)]
            ps_g = [B(small.t[:, 256 + i * 8:256 + (i + 1) * 8], "ps_g%d" % i) for i in range(2)]
            ps_d = [B(small.t[:, 272 + i * 2:272 + (i + 1) * 2], "ps_d%d" % i) for i in range(2)]
            ps_x = [B(small.t[:, 276 + i * 2:276 + (i + 1) * 2], "ps_x%d" % i) for i in range(2)]
            rot = {}

            def nx(key, lst):
                i = rot.get(key, 0)
                rot[key] = i + 1
                return lst[i % len(lst)]

            def interleave(gens, width):
                it = iter(gens)
                active = []
                while True:
                    while len(active) < width:
                        g = next(it, None)
                        if g is None:
                            break
                        active.append(g)
                    if not active:
                        break
                    for g in list(active):
                        try:
                            next(g)
                        except StopIteration:
                            active.remove(g)

            def zero_states():
                for h in range(4):
                    S.op("dve", I("memset", Cm[h].t[:], 0.0), writes=[Cm[h].tr])
                    S.op("dve", I("memset", Cmb[h].t[:], 0.0), writes=[Cmb[h].tr])
                for h in range(8):
                    S.op("dve", I("memset", Sr[h].t[:], 0.0), writes=[Sr[h].tr])
                    S.op("dve", I("memset", Srb[h].t[:], 0.0), writes=[Srb[h].tr])
                S.op("dve", I("memset", Nm.t[:], 0.0), writes=[Nm.tr])
                S.op("dve", I("memset", Nmb.t[:], 0.0), writes=[Nmb.tr])
                S.op("dve", I("memset", Gac.t[:], 0.0), writes=[Gac.tr])

            def refresh_bf():
                for h in range(4):
                    S.op("act", I("copy", out=Cmb[h].t[:], in_=Cm[h].t[:]), reads=[Cm[h].tr], writes=[Cmb[h].tr])
                for h in range(8):
                    S.op("act", I("copy", out=Srb[h].t[:], in_=Sr[h].t[:]), reads=[Sr[h].tr], writes=[Srb[h].tr])
                S.op("act", I("copy", out=Nmb.t[:], in_=Nm.t[:]), reads=[Nm.tr], writes=[Nmb.tr])

            def sweep(d, outputs, post=None):
                order = list(range(NT)) if d == 0 else list(range(NT - 1, -1, -1))
                gT = gfT if d == 0 else gbT
                for ci, c in enumerate(order):
                    r0 = c * 128
                    b = ci % NB
                    g_, km_, vm_, kr_, vr_ = gch[b], kmc[b], vmc[b], krc[b], vrc[b]
                    S.dma("sp", g_.t[:], d_g[r0:r0 + 128, :], g_.tr, writes=[g_.tr])
                    S.dma("sp", km_.t[:], d_km[r0:r0 + 128, :], km_.tr, writes=[km_.tr])
                    S.dma("sp", vm_.t[:], d_vm[r0:r0 + 128, :], vm_.tr, writes=[vm_.tr])
                    S.dma("sp", kr_.t[:], d_kr[r0:r0 + 128, :], kr_.tr, writes=[kr_.tr])
                    S.dma("sp", vr_.t[:], d_vr[r0:r0 + 128, :], vr_.tr, writes=[vr_.tr])
                    if outputs:
                        qTm_, kTm_, qTr_, kTr_ = qTmc[b], kTmc[b], qTrc[b], kTrc[b]
                        for (dst, src) in ((qTm_, d_qTm), (kTm_, d_kTm), (qTr_, d_qTr), (kTr_, d_kTr)):
                            S.dma("sp", dst.t[:], src[:, r0:r0 + 128].rearrange("(f p) t -> p f t", p=128), dst.tr, writes=[dst.tr])
                    s_ = nx("sc", sc)
                    sb_ = nx("scb", scb)
                    pg = nx("psg", ps_g)
                    ic, fc = (0, 4) if d == 0 else (8, 12)
                    S.op("pe", [I("matmul", out=pg.t[:, 0:4], lhsT=mask.t[:, d, :], rhs=g_.t[:, fc:fc + 4],
                                                                        start=True, stop=True),
                                I("matmul", out=pg.t[:, 4:8], lhsT=mask.t[:, 2, :], rhs=g_.t[:, fc:fc + 4],
                                                                        start=True, stop=True)],
                         reads=[mask.tr, g_.tr], writes=[pg.tr])
                    S.op("dve", I("tensor_copy", out=s_.t[:, 0:2, :], in_=pg.t[:].rearrange("p (a b) -> p a b", a=2)),
                         reads=[pg.tr], writes=[s_.tr])
                    S.op("act", I("activation", out=s_.t[:, 2, :], in_=s_.t[:, 0, :], func=AF.Exp), reads=[s_.tr], writes=[s_.tr])
                    S.op("dve", I("tensor_tensor", out=s_.t[:, 6, :], in0=g_.t[:, ic:ic + 4], in1=s_.t[:, 0, :],
                                                                               op=ALU.subtract), reads=[s_.tr, g_.tr], writes=[s_.tr])
                    S.op("act", I("activation", out=s_.t[:, 3, :], in_=s_.t[:, 6, :], func=AF.Exp), reads=[s_.tr], writes=[s_.tr])
                    S.op("dve", I("tensor_tensor", out=s_.t[:, 7, :], in0=s_.t[:, 6, :], in1=s_.t[:, 1, :], op=ALU.add),
                         reads=[s_.tr], writes=[s_.tr])
                    S.op("act", I("activation", out=s_.t[:, 4, :], in_=s_.t[:, 7, :], func=AF.Exp), reads=[s_.tr], writes=[s_.tr])
                    S.op("act", I("activation", out=s_.t[:, 5, :], in_=s_.t[:, 1, :], func=AF.Exp), reads=[s_.tr], writes=[s_.tr])
                    S.op("dve", I("tensor_copy", out=sb_.t[:], in_=s_.t[:, 3, :]), reads=[s_.tr], writes=[sb_.tr])
                    if not outputs:
                        S.op("dve", I("tensor_tensor", out=Gac.t[:], in0=Gac.t[:], in1=s_.t[:, 1, :], op=ALU.add),
                             reads=[s_.tr, Gac.tr], writes=[Gac.tr])
                    if outputs:
                        o_m = nx("om", om)
                        o_r = nx("or", orr)
                    def mh(h):
                        if outputs:
                            pss = nx("pss", ps_s)
                            a_t = nx("AT", AT)
                            vh = nx("Vh", Vh)
                            pn = nx("psn", ps_n)
                            pd = nx("psd", ps_d)
                            r_ = nx("rsc", rsc)
                            S.op("pe", [I("matmul",
                                out=pss.t[:], lhsT=kTm_.t[:, h * 2 + kt, :], rhs=qTm_.t[:, h * 2 + kt, :], start=(kt == 0), stop=(kt == 1))
                                for kt in range(2)], reads=[kTm_.tr, qTm_.tr], writes=[pss.tr])
                            yield
                            S.op("dve", I("tensor_tensor", out=a_t.t[:], in0=pss.t[:], in1=mask.t[:, d, :], op=ALU.mult),
                                 reads=[pss.tr, mask.tr], writes=[a_t.tr])
                            yield
                            S.op("dve", I("tensor_scalar",
                                out=vh.t[:], in0=vm_.t[:, h * 512:(h + 1) * 512], scalar1=s_.t[:, 3, h:h + 1], scalar2=None, op0=ALU.mult),
                                reads=[vm_.tr, s_.tr], writes=[vh.tr])
                            yield
                            S.op("pe", [I("matmul", out=pn.t[:], lhsT=a_t.t[:], rhs=vh.t[:], start=True, stop=False)]
                                 + [I("matmul", out=pn.t[:], lhsT=qTm_.t[:, h * 2 + kt, :], rhs=Cmb[h].t[:, kt, :],
                                                                          start=False, stop=(kt == 1)) for kt in range(2)]
                                 + [I("matmul", out=pd.t[:, 0:1], lhsT=a_t.t[:], rhs=sb_.t[:, h:h + 1],
                                                                                     start=True, stop=False)]
                                 + [I("matmul", out=pd.t[:, 0:1], lhsT=qTm_.t[:, h * 2 + kt, :],
                                                                          rhs=Nmb.t[:, h * 2 + kt:h * 2 + kt + 1], start=False, stop=(kt == 1))
                                    for kt in range(2)],
                                 reads=[a_t.tr, vh.tr, qTm_.tr, Cmb[h].tr, sb_.tr, Nmb.tr], writes=[pn.tr, pd.tr])
                            yield
                            S.op("dve", I("tensor_scalar",
                                out=r_.t[:, 0:1], in0=pd.t[:, 0:1], scalar1=s_.t[:, 2, h:h + 1], scalar2=None, op0=ALU.mult),
                                reads=[pd.tr, s_.tr], writes=[r_.tr])
                            yield
                            S.op("dve", I("tensor_scalar",
                                out=r_.t[:, 3:4], in0=r_.t[:, 0:1], scalar1=-1.0, scalar2=None, op0=ALU.mult),
                                reads=[r_.tr], writes=[r_.tr])
                            yield
                            S.op("dve", I("scalar_tensor_tensor",
                                out=r_.t[:, 0:1], in0=r_.t[:, 0:1], scalar=1.0, in1=r_.t[:, 3:4], op0=ALU.max, op1=ALU.max),
                                reads=[r_.tr], writes=[r_.tr])
                            yield
                            S.op("dve", I("reciprocal", out=r_.t[:, 1:2], in_=r_.t[:, 0:1]), reads=[r_.tr], writes=[r_.tr])
                            yield
                            S.op("dve", I("tensor_tensor", out=r_.t[:, 2:3], in0=r_.t[:, 1:2], in1=s_.t[:, 2, h:h + 1],
                                                                                     op=ALU.mult), reads=[r_.tr, s_.tr], writes=[r_.tr])
                            yield
                            S.op("act", I("activation",
                                out=o_m.t[:, h * 512:(h + 1) * 512], in_=pn.t[:], func=AF.Copy, scale=r_.t[:, 2:3]),
                                reads=[pn.tr, r_.tr], writes=[o_m.tr])
                            yield
                        kh = nx("Kh", Kh)
                        S.op("dve", I("tensor_scalar",
                            out=kh.t[:], in0=km_.t[:, h * 256:(h + 1) * 256], scalar1=s_.t[:, 4, h:h + 1], scalar2=None, op0=ALU.mult),
                            reads=[km_.tr, s_.tr], writes=[kh.tr])
                        yield
                        px = nx("psx", ps_x)
                        pcs = []
                        for kt in range(2):
                            pc = nx("psc", ps_c)
                            pcs.append(pc)
                            S.op("pe", [I("matmul",
                                out=pc.t[:], lhsT=kh.t[:, kt * 128:(kt + 1) * 128], rhs=vm_.t[:, h * 512:(h + 1) * 512], start=True, stop=True),
                                I("matmul",
                                out=px.t[:, kt:kt + 1], lhsT=kh.t[:, kt * 128:(kt + 1) * 128], rhs=onesb.t[:, 0:1], start=True, stop=True)],
                                reads=[kh.tr, vm_.tr, onesb.tr], writes=[pc.tr, px.tr])
                            yield
                        for kt in range(2):
                            S.op("dve", I("scalar_tensor_tensor",
                                out=Cm[h].t[:, kt, :], in0=Cm[h].t[:, kt, :], scalar=s_.t[:, 5, h:h + 1], in1=pcs[kt].t[:], op0=ALU.mult, op1=ALU.add),
                                reads=[Cm[h].tr, s_.tr, pcs[kt].tr], writes=[Cm[h].tr])
                            yield
                        S.op("dve", I("scalar_tensor_tensor",
                            out=Nm.t[:, h * 2:h * 2 + 2], in0=Nm.t[:, h * 2:h * 2 + 2], scalar=s_.t[:, 5, h:h + 1], in1=px.t[:, 0:2],
                            op0=ALU.mult, op1=ALU.add), reads=[Nm.tr, s_.tr, px.tr], writes=[Nm.tr])
                        yield
                        if outputs:
                            S.op("act", I("copy", out=Cmb[h].t[:], in_=Cm[h].t[:]), reads=[Cm[h].tr], writes=[Cmb[h].tr])
                            yield
                    def rh(h):
                        if outputs:
                            pss = nx("pss", ps_s)
                            a_t = nx("AT", AT)
                            pn = nx("psn", ps_n)
                            p1 = nx("p1s", p1s)
                            S.op("pe", I("matmul", out=pss.t[:], lhsT=kTr_.t[:, h, :], rhs=qTr_.t[:, h, :], start=True, stop=True),
                                 reads=[kTr_.tr, qTr_.tr], writes=[pss.tr])
                            yield
                            S.op("dve", I("tensor_tensor", out=a_t.t[:], in0=pss.t[:], in1=gT.t[:, h, :], op=ALU.mult),
                                 reads=[pss.tr, gT.tr], writes=[a_t.tr])
                            yield
                            S.op("pe", [I("matmul", out=pn.t[:, 0:256], lhsT=a_t.t[:], rhs=vr_.t[:, h * 256:(h + 1) * 256],
                                                                                start=True, stop=True),
                                        I("matmul", out=pn.t[:, 256:512], lhsT=qTr_.t[:, h, :], rhs=Srb[h].t[:], start=True, stop=True)],
                                 reads=[a_t.tr, vr_.tr, qTr_.tr, Srb[h].tr], writes=[pn.tr])
                            yield
                            S.op("act", I("copy", out=p1.t[:], in_=pn.t[:, 0:256]), reads=[pn.tr], writes=[p1.tr])
                            yield
                            S.op("dve", I("scalar_tensor_tensor",
                                out=o_r.t[:, h * 256:(h + 1) * 256], in0=pn.t[:, 256:512], scalar=dec.t[:, d, h:h + 1], in1=p1.t[:],
                                op0=ALU.mult, op1=ALU.add), reads=[pn.tr, p1.tr, dec.tr], writes=[o_r.tr])
                            yield
                        kh = nx("Kh", Kh)
                        S.op("dve", I("tensor_scalar",
                            out=kh.t[:, 0:128], in0=kr_.t[:, h * 128:(h + 1) * 128], scalar1=dec.t[:, 2 + d, h:h + 1], scalar2=None, op0=ALU.mult),
                            reads=[kr_.tr, dec.tr], writes=[kh.tr])
                        yield
                        pc = nx("psc", ps_c)
                        S.op("pe", I("matmul", out=pc.t[:, 0:256], lhsT=kh.t[:, 0:128], rhs=vr_.t[:, h * 256:(h + 1) * 256],
                                                                         start=True, stop=True), reads=[kh.tr, vr_.tr], writes=[pc.tr])
                        yield
                        S.op("dve", I("scalar_tensor_tensor",
                            out=Sr[h].t[:], in0=Sr[h].t[:], scalar=g128[h], in1=pc.t[:, 0:256], op0=ALU.mult, op1=ALU.add),
                            reads=[Sr[h].tr, pc.tr], writes=[Sr[h].tr])
                        yield
                        if outputs:
                            S.op("act", I("copy", out=Srb[h].t[:], in_=Sr[h].t[:]), reads=[Sr[h].tr], writes=[Srb[h].tr])
                            yield
                    interleave([mh(h) for h in range(4)] + [rh(h) for h in range(8)], IW)
                    if outputs:
                        S.op("act", I("copy", out=Nmb.t[:], in_=Nm.t[:]), reads=[Nm.tr], writes=[Nmb.tr])
                    if outputs:
                        post(d, c, o_m, o_r)

            if mode == "p1":
                for d in range(2):
                    zero_states()
                    sweep(d, False)
                    for h in range(4):
                        S.dma("sp", oFc[d, h, :, :], Cm[h].t[:].rearrange("p a b -> p (a b)"), Cm[h].tr, reads=[Cm[h].tr])
                    for h in range(8):
                        S.dma("sp", oFs[d, h, :, :], Sr[h].t[:], Sr[h].tr, reads=[Sr[h].tr])
                    S.dma("sp", oFn[d, :, :], Nm.t[:], Nm.tr, reads=[Nm.tr])
                    S.dma("sp", oG[d, :, :], Gac.t[:], Gac.tr, reads=[Gac.tr])
                S.barrier()
                S.emit()
            else:
                cm = sb(st, "cm", [128, 2, ncores], F32)
                S.dma("sp", cm.t[:], cm_in[:, :, :], cm.tr, writes=[cm.tr])
                gGs = sb(st, "gGs", [128, ncores, 4], F32)
                cf = sb(st, "cf", [128, ncores, 4], F32)
                fld = [sb(st, "fld%d" % i, [128, 1024], F32) for i in range(2)]
                fln = [sb(st, "fln%d" % i, [128, 8], F32) for i in range(2)]

                def combine(d):
                    zero_states()
                    S.dma("sp", gGs.t[:], gG[:, d, :, :].rearrange("j p h -> p j h"), gGs.tr, writes=[gGs.tr])
                    S.op("act", I("activation", out=cf.t[:], in_=gGs.t[:], func=AF.Exp), reads=[gGs.tr], writes=[cf.tr])
                    S.op("dve", I("tensor_scalar", out=cf.t[:], in0=cf.t[:], scalar1=-1.0, scalar2=None, op0=ALU.add),
                         reads=[cf.tr], writes=[cf.tr])
                    S.op("dve", I("tensor_tensor", out=cf.t[:], in0=cf.t[:], in1=cm.t[:, d, :].unsqueeze(2).to_broadcast([128, ncores, 4]),
                                                          op=ALU.mult), reads=[cf.tr, cm.tr], writes=[cf.tr])
                    S.op("dve", I("tensor_scalar", out=cf.t[:], in0=cf.t[:], scalar1=1.0, scalar2=None, op0=ALU.add),
                         reads=[cf.tr], writes=[cf.tr])
                    order = list(range(ncores)) if d == 0 else list(range(ncores - 1, -1, -1))
                    fi = 0
                    for j in order:
                        mj = cm.t[:, d, j:j + 1]
                        for h in range(4):
                            f_ = fld[fi % 2]
                            fi += 1
                            S.dma("sp", f_.t[:], gFc[j, d, h, :, :], f_.tr, writes=[f_.tr])
                            S.op("pool", I("tensor_scalar", out=f_.t[:], in0=f_.t[:], scalar1=mj, scalar2=None, op0=ALU.mult),
                                 reads=[f_.tr, cm.tr], writes=[f_.tr])
                            S.op("dve", I("scalar_tensor_tensor",
                                out=Cm[h].t[:].rearrange("p a b -> p (a b)"), in0=Cm[h].t[:].rearrange("p a b -> p (a b)"),
                                scalar=cf.t[:, j, h:h + 1], in1=f_.t[:], op0=ALU.mult, op1=ALU.add),
                                reads=[Cm[h].tr, cf.tr, f_.tr], writes=[Cm[h].tr])
                        n_ = fln[fi % 2]
                        S.dma("sp", n_.t[:], gFn[j, d, :, :], n_.tr, writes=[n_.tr])
                        S.op("pool", I("tensor_scalar", out=n_.t[:], in0=n_.t[:], scalar1=mj, scalar2=None, op0=ALU.mult),
                             reads=[n_.tr, cm.tr], writes=[n_.tr])
                        for h in range(4):
                            S.op("dve", I("scalar_tensor_tensor",
                                out=Nm.t[:, h * 2:h * 2 + 2], in0=Nm.t[:, h * 2:h * 2 + 2], scalar=cf.t[:, j, h:h + 1], in1=n_.t[:, h * 2:h * 2 + 2],
                                op0=ALU.mult, op1=ALU.add), reads=[Nm.tr, cf.tr, n_.tr], writes=[Nm.tr])
                        for h in range(8):
                            f_ = fld[fi % 2]
                            fi += 1
                            S.dma("sp", f_.t[:, 0:256], gFs[j, d, h, :, :], f_.tr, writes=[f_.tr])
                            S.op("pool", I("tensor_scalar", out=f_.t[:, 0:256], in0=f_.t[:, 0:256], scalar1=mj, scalar2=None, op0=ALU.mult),
                                 reads=[f_.tr, cm.tr], writes=[f_.tr])
                            S.op("dve", I("tensor_scalar",
                                out=f_.t[:, 256:257], in0=mj, scalar1=gamT[h] - 1.0, scalar2=1.0, op0=ALU.mult, op1=ALU.add),
                                reads=[cm.tr, f_.tr], writes=[f_.tr])
                            S.op("dve", I("scalar_tensor_tensor",
                                out=Sr[h].t[:], in0=Sr[h].t[:], scalar=f_.t[:, 256:257], in1=f_.t[:, 0:256], op0=ALU.mult, op1=ALU.add),
                                reads=[Sr[h].tr, f_.tr], writes=[Sr[h].tr])
                    refresh_bf()

                mnwbc = sb(st, "mnwbc", [128, D], F32)
                rnwbc = sb(st, "rnwbc", [128, D], F32)
                S.dma("sp", mnwbc.t[:], mnw_in.partition_broadcast(128), mnwbc.tr, writes=[mnwbc.tr])
                S.dma("sp", rnwbc.t[:], rnw_in.partition_broadcast(128), rnwbc.tr, writes=[rnwbc.tr])
                of_l = [sb(st, "ofl%d" % i, [128, 2048], F32) for i in range(2)]
                gl = [sb(st, "gl%d" % i, [128, 2048], BF16) for i in range(2)]
                ub = [sb(st, "ub%d" % i, [128, 2048], BF16) for i in range(2)]
                jk = sb(st, "jk", [128, 512], BF16)
                hs = [sb(st, "hs%d" % i, [128, 16], F32) for i in range(2)]
                pu = [ps(st, "pu%d" % i, [128, 8, 128], BF16) for i in range(1)]
                uT = [sb(st, "uTs%d" % i, [128, 8, 128], BF16) for i in range(2)]

                def post(d, c, o_m, o_r):
                    r0 = c * 128
                    if d == 0:
                        S.dma("sp", d_ofm[r0:r0 + 128, :], o_m.t[:], o_m.tr, reads=[o_m.tr])
                        S.dma("sp", d_ofr[r0:r0 + 128, :], o_r.t[:], o_r.tr, reads=[o_r.tr])
                        return
                    for (o_, d_of, d_gate, d_sz, nwb, dU, nh, dv) in (
                            (o_m, d_ofm, d_som, d_szm, mnwbc, d_uTm, 4, 512), (o_r, d_ofr, None, d_szr, rnwbc, d_uTr, 8, 256)):
                        ofl = nx("ofl", of_l)
                        S.dma("sp", ofl.t[:], d_of[r0:r0 + 128, :], ofl.tr, writes=[ofl.tr])
                        S.op("dve", I("tensor_tensor", out=o_.t[:], in0=o_.t[:], in1=ofl.t[:], op=ALU.add),
                             reads=[o_.tr, ofl.tr], writes=[o_.tr])
                        if d_gate is not None:
                            g2 = nx("gl", gl)
                            S.dma("sp", g2.t[:], d_gate[r0:r0 + 128, :], g2.tr, writes=[g2.tr])
                            S.op("dve", I("tensor_tensor", out=o_.t[:], in0=o_.t[:], in1=g2.t[:], op=ALU.mult),
                                 reads=[o_.tr, g2.tr], writes=[o_.tr])
                        z2 = nx("gl", gl)
                        S.dma("sp", z2.t[:], d_sz[r0:r0 + 128, :], z2.tr, writes=[z2.tr])
                        h_ = nx("hs", hs)
                        for h in range(nh):
                            S.op("act", I("activation",
                                out=jk.t[:, 0:dv], in_=o_.t[:, h * dv:(h + 1) * dv], func=AF.Square, accum_out=h_.t[:, h:h + 1]),
                                reads=[o_.tr], writes=[jk.tr, h_.tr])
                        S.op("act", I("activation", out=h_.t[:, 0:nh], in_=h_.t[:, 0:nh], func=AF.Ln, scale=1.0 / dv, bias=EPS),
                             reads=[h_.tr], writes=[h_.tr])
                        S.op("act", I("activation", out=h_.t[:, 0:nh], in_=h_.t[:, 0:nh], func=AF.Exp, scale=-0.5),
                             reads=[h_.tr], writes=[h_.tr])
                        u_ = nx("ub", ub)
                        for h in range(nh):
                            S.op("dve", I("scalar_tensor_tensor",
                                out=o_.t[:, h * dv:(h + 1) * dv], in0=o_.t[:, h * dv:(h + 1) * dv], scalar=h_.t[:, h:h + 1],
                                in1=nwb.t[:, h * dv:(h + 1) * dv], op0=ALU.mult, op1=ALU.mult), reads=[o_.tr, h_.tr, nwb.tr], writes=[o_.tr])
                        S.op("dve", I("tensor_tensor", out=u_.t[:], in0=o_.t[:], in1=z2.t[:], op=ALU.mult),
                             reads=[o_.tr, z2.tr], writes=[u_.tr])
                        for g in range(2):
                            p_ = pu[0]
                            t_ = nx("uTs", uT)
                            S.op("pe", [I("transpose",
                                out=p_.t[:, k, :], in_=u_.t[:, (g * 8 + k) * 128:(g * 8 + k + 1) * 128], identity=identb.t[:])
                                for k in range(8)], reads=[u_.tr, identb.tr], writes=[p_.tr])
                            S.op("act", I("copy", out=t_.t[:], in_=p_.t[:]), reads=[p_.tr], writes=[t_.tr])
                            S.dma("sp", dU[g * 1024:(g + 1) * 1024, r0:r0 + 128].rearrange("(k p) t -> p k t", p=128), t_.t[:], t_.tr, reads=[t_.tr])

                combine(0)
                sweep(0, True, post)
                S.barrier()
                combine(1)
                sweep(1, True, post)
                S.barrier()
                S.emit()

        if mode == "p2":
            TC = min(T, 1024)
            NTC = TC // 128
            with contextlib.ExitStack() as st:
                uTm = sb(st, "uTm", [128, KT, TC], BF16)
                uTr = sb(st, "uTr", [128, KT, TC], BF16)
                yT = sb(st, "yT", [128, KT, TC], BF16)
                yT_tr = [Tr("yT%d" % i) for i in range(NTC)]
                wm = [sb(st, "wm%d" % i, [128, KT, 512], BF16) for i in range(2)]
                wr = [sb(st, "wr%d" % i, [128, KT, 512], BF16) for i in range(2)]
                pm_ = [ps(st, "cpm%d" % i, [128, 512], F32) for i in range(2)]
                pr_ = [ps(st, "cpr%d" % i, [128, 512], F32) for i in range(2)]
                pq_ = [ps(st, "cpq%d" % i, [128, 4, 128], BF16) for i in range(2)]
                mx = [sb(st, "mx%d" % i, [128, 2, 512], BF16) for i in range(2)]
                t1 = [sb(st, "t1%d" % i, [128, 512], F32) for i in range(2)]
                t2 = [sb(st, "t2%d" % i, [128, 512], F32) for i in range(2)]
                yb = [sb(st, "yb%d" % i, [128, 512], BF16) for i in range(2)]
                xr = [sb(st, "xr%d" % i, [128, 512], F32) for i in range(2)]
                rot2 = {}

                def n2(key, lst):
                    i = rot2.get(key, 0)
                    rot2[key] = i + 1
                    return lst[i % len(lst)]

                for t0 in range(0, T, TC):
                    S.dma("sp", uTm.t[:], d_uTm[:, t0:t0 + TC].rearrange("(k p) t -> p k t", p=128), uTm.tr, writes=[uTm.tr])
                    S.dma("sp", uTr.t[:], d_uTr[:, t0:t0 + TC].rearrange("(k p) t -> p k t", p=128), uTr.tr, writes=[uTr.tr])
                    for cb in range(4):
                        wm_, wr_ = n2("wm", wm), n2("wr", wr)
                        S.dma("pool", wm_.t[:], wpm_in[:, cb * 512:(cb + 1) * 512].rearrange("(k p) n -> p k n", p=128), wm_.tr, writes=[wm_.tr])
                        S.dma("pool", wr_.t[:], wpr_in[:, cb * 512:(cb + 1) * 512].rearrange("(k p) n -> p k n", p=128), wr_.tr, writes=[wr_.tr])
                        for tt in range(NTC):
                            r0 = t0 + tt * 128
                            a_, b_ = n2("pm", pm_), n2("pr", pr_)
                            S.op("pe", [I("matmul", out=a_.t[:], lhsT=uTm.t[:, k, tt * 128:(tt + 1) * 128],
                                                                                      rhs=wm_.t[:, k, :], start=(k == 0), stop=(k == KT - 1))
                                        for k in range(KT)], reads=[uTm.tr, wm_.tr], writes=[a_.tr])
                            S.op("pe", [I("matmul", out=b_.t[:], lhsT=uTr.t[:, k, tt * 128:(tt + 1) * 128],
                                                                                      rhs=wr_.t[:, k, :], start=(k == 0), stop=(k == KT - 1))
                                        for k in range(KT)], reads=[uTr.tr, wr_.tr], writes=[b_.tr])
                            m_ = n2("mx", mx)
                            S.dma("sp", m_.t[:], d_mix[r0:r0 + 128, :].rearrange("p (two c) -> p two c", two=2)[:, :, cb * 512:(cb + 1) * 512],
                                  m_.tr, writes=[m_.tr])
                            u1, u2, y_ = n2("t1", t1), n2("t2", t2), n2("yb", yb)
                            S.op("dve", I("tensor_tensor", out=u1.t[:], in0=a_.t[:], in1=m_.t[:, 0, :], op=ALU.mult),
                                 reads=[a_.tr, m_.tr], writes=[u1.tr])
                            S.op("dve", I("tensor_tensor", out=u2.t[:], in0=b_.t[:], in1=m_.t[:, 1, :], op=ALU.mult),
                                 reads=[b_.tr, m_.tr], writes=[u2.tr])
                            S.op("dve", I("tensor_tensor", out=y_.t[:], in0=u1.t[:], in1=u2.t[:], op=ALU.add),
                                 reads=[u1.tr, u2.tr], writes=[y_.tr])
                            q_ = n2("pq", pq_)
                            S.op("pe", [I("transpose", out=q_.t[:, i, :], in_=y_.t[:, i * 128:(i + 1) * 128], identity=identb.t[:])
                                        for i in range(4)], reads=[y_.tr, identb.tr], writes=[q_.tr])
                            S.op("act", I("copy", out=yT.t[:, cb * 4:(cb + 1) * 4, tt * 128:(tt + 1) * 128], in_=q_.t[:]),
                                 reads=[q_.tr], writes=[yT_tr[tt]])
                    for cb in range(4):
                        wm_ = n2("wm", wm)
                        S.dma("pool", wm_.t[:], wo_in[:, cb * 512:(cb + 1) * 512].rearrange("(k p) n -> p k n", p=128), wm_.tr, writes=[wm_.tr])
                        for tt in range(NTC):
                            r0 = t0 + tt * 128
                            a_ = n2("pm", pm_)
                            S.op("pe", [I("matmul", out=a_.t[:], lhsT=yT.t[:, k, tt * 128:(tt + 1) * 128],
                                                                                      rhs=wm_.t[:, k, :], start=(k == 0), stop=(k == KT - 1))
                                        for k in range(KT)], reads=[yT_tr[tt], wm_.tr], writes=[a_.tr])
                            x_ = n2("xr", xr)
                            S.dma("sp", x_.t[:], x_in[r0:r0 + 128, cb * 512:(cb + 1) * 512], x_.tr, writes=[x_.tr])
                            S.op("dve", I("tensor_tensor", out=x_.t[:], in0=a_.t[:], in1=x_.t[:], op=ALU.add),
                                 reads=[a_.tr, x_.tr], writes=[x_.tr])
                            S.dma("sp", d_xn[r0:r0 + 128, cb * 512:(cb + 1) * 512], x_.t[:], x_.tr, reads=[x_.tr])
                S.barrier()
                S.emit()
            if last:
                with contextlib.ExitStack() as st:
                    fnbc = sb(st, "fnbc", [128, D], F32)
                    S.dma("sp", fnbc.t[:], fnw_in.partition_broadcast(128), fnbc.tr, writes=[fnbc.tr])
                    xf = [sb(st, "xf%d" % i, [128, D], F32) for i in range(2)]
                    jf = sb(st, "jf", [128, D], BF16)
                    sf = [sb(st, "sf%d" % i, [128, 1], F32) for i in range(2)]
                    for tt in range(NT):
                        x_, s_ = xf[tt % 2], sf[tt % 2]
                        S.dma("sp", x_.t[:], d_xn[tt * 128:(tt + 1) * 128, :], x_.tr, writes=[x_.tr])
                        S.op("act", I("activation", out=jf.t[:], in_=x_.t[:], func=AF.Square, accum_out=s_.t[:]),
                             reads=[x_.tr], writes=[jf.tr, s_.tr])
                        S.op("act", I("activation", out=s_.t[:], in_=s_.t[:], func=AF.Ln, scale=1.0 / D, bias=EPS), reads=[s_.tr], writes=[s_.tr])
                        S.op("act", I("activation", out=s_.t[:], in_=s_.t[:], func=AF.Exp, scale=-0.5), reads=[s_.tr], writes=[s_.tr])
                        S.op("dve", I("scalar_tensor_tensor", out=x_.t[:], in0=x_.t[:], scalar=s_.t[:, 0:1], in1=fnbc.t[:],
                                                                                   op0=ALU.mult, op1=ALU.mult), reads=[x_.tr, s_.tr, fnbc.tr], writes=[x_.tr])
                        S.dma("sp", y_out[tt * 128:(tt + 1) * 128, :], x_.t[:], x_.tr, reads=[x_.tr])
                    S.barrier()
                    S.emit()
    return nc


_PROGS = {}


def _prog(T, mode, last, ncores):
    key = (T, mode, last, ncores)
    if key not in _PROGS:
        _PROGS[key] = build(T, mode, last, ncores=ncores)
    return _PROGS[key]


def _layer_inputs(x_full, positions, layer, params, T, ncores, consts):
    maps = []
    S_ = x_full.shape[0]
    for i in range(ncores):
        t0 = i * T
        xh = np.zeros((T + 4, D), np.float32)
        xh[0:T] = x_full[t0:t0 + T]
        if t0 >= 2:
            xh[T:T + 2] = x_full[t0 - 2:t0]
        if t0 + T + 2 <= S_:
            xh[T + 2:T + 4] = x_full[t0 + T:t0 + T + 2]
        m = {"x": xh,
             "pos": np.ascontiguousarray(positions[t0:t0 + T].reshape(T // 128, 128).T.astype(np.int32)),
             "w_in": params["w_in"][layer],
             "norm_w": params["norm_w"][layer].reshape(1, D),
             "b_mgate": params["b_mgate"][layer].reshape(1, 16),
             "conv_w": np.ascontiguousarray(params["conv_w"][layer].reshape(5, 16, 128).transpose(2, 1, 0)),
             "conv_b": np.ascontiguousarray(params["conv_b"][layer].reshape(16, 128).T)}
        m.update(consts)
        maps.append(m)
    return maps


def run_layers(x_full, positions, params, T, ncores, depth):
    consts = host_consts(T)
    x_cur = np.ascontiguousarray(x_full, dtype=np.float32)
    for layer in range(depth):
        last = layer == depth - 1
        maps = _layer_inputs(x_cur, positions, layer, params, T, ncores, consts)
        w_full = params["w_in"][layer]
        w_p1 = np.ascontiguousarray(np.concatenate(
            [w_full[:, O_KM:O_KM + 1024], w_full[:, O_VM:O_VM + 2048], w_full[:, O_GM:O_GM + 16],
             w_full[:, O_KR:O_KR + 1024], w_full[:, O_VR:O_VR + 2048]], axis=1))
        maps1 = [dict(m, w_in=w_p1) for m in maps]
        r1 = run_bass_kernel_spmd(_prog(T, "p1", False, ncores), maps1, core_ids=list(range(ncores))).results
        gFc = np.stack([r["oFc"] for r in r1])
        gFn = np.stack([r["oFn"] for r in r1])
        gFs = np.stack([r["oFs"] for r in r1])
        gG = np.stack([r["oG"] for r in r1])
        for i, m in enumerate(maps):
            m.update({"m_norm_w": params["m_norm_w"][layer].reshape(1, D),
                      "r_norm_w": params["r_norm_w"][layer].reshape(1, D),
                      "w_proj_m": params["w_proj_m"][layer], "w_proj_r": params["w_proj_r"][layer],
                      "w_out": params["w_out"][layer], "b_mix": params["b_mix"][layer].reshape(1, 2 * D),
                      "final_norm_w": params["final_norm_w"].reshape(1, D),
                      "cmask": core_masks(i)[:, :, :ncores].copy(),
                      "gFc": gFc, "gFn": gFn, "gFs": gFs, "gG": gG})
        r2 = run_bass_kernel_spmd(_prog(T, "p2", last, ncores), maps, core_ids=list(range(ncores))).results
        x_cur = np.concatenate([r["y"] for r in r2], axis=0)
    return x_cur


def kernel(x, positions, norm_w, w_in, b_mgate, conv_w, conv_b, m_norm_w, r_norm_w,
           w_proj_m, w_proj_r, b_mix, w_out, final_norm_w):
    params = {k: np.asarray(v) for k, v in dict(
        norm_w=norm_w, w_in=w_in, b_mgate=b_mgate, conv_w=conv_w, conv_b=conv_b, m_norm_w=m_norm_w,
        r_norm_w=r_norm_w, w_proj_m=w_proj_m, w_proj_r=w_proj_r, b_mix=b_mix, w_out=w_out,
        final_norm_w=final_norm_w).items()}
    x = np.asarray(x)
    Sq = x.shape[1]
    T = Sq // NCORES
    out = run_layers(x[0], np.asarray(positions)[0], params, T, NCORES, w_in.shape[0])
    return out.reshape(1, Sq, D).astype(np.float32)
```
